# Optimizing a Trainium2 kernel written in Bass

```python
import jax, jax.numpy as jnp
from jax import lax
import numpy as np

D_MODEL = 1024
BATCH = 2
SEQ = 8192
DEPTH = 1
DEC_BATCH = 128
DEC_SEQ = 8
PAST_LEN = 16384
PAGE_SIZE = 128

D_LRU = D_MODEL // 2
LRU_BLOCKS = 8
LRU_BLOCK = D_LRU // LRU_BLOCKS
CONV_W = 4
LRU_C = 8.0
MLA_HEADS = 8
QK_NOPE = 64
QK_ROPE = 32
V_DIM = 64
Q_LORA = D_MODEL // 4
KV_LORA = D_MODEL // 8
D_MLA = MLA_HEADS * V_DIM
D_MIX = D_LRU + D_MLA
D_IN = 2 * D_LRU + Q_LORA + KV_LORA + QK_ROPE
D_FF = 2816
ROPE_BASE = 10000.0
Q_BLOCK = 128
EPS = 1e-6

kernel_name = 'hymba_rglru_mla_macaron_step'


def rmsnorm(x, g):
    xf = x.astype(jnp.float32)
    y = xf * lax.rsqrt(jnp.mean(xf * xf, axis=-1, keepdims=True) + EPS)
    return (y * g.astype(jnp.float32)).astype(x.dtype)


def swiglu(x, w_gate, w_up, w_down):
    return (jax.nn.silu(x @ w_gate) * (x @ w_up)) @ w_down


def rope(x, pos):
    half = x.shape[-1] // 2
    freqs = ROPE_BASE ** (-jnp.arange(half, dtype=jnp.float32) / half)
    ang = pos.astype(jnp.float32)[:, None] * freqs[None, :]
    cos = jnp.cos(ang)[None, :, None, :]
    sin = jnp.sin(ang)[None, :, None, :]
    xf = x.astype(jnp.float32)
    x1, x2 = xf[..., :half], xf[..., half:]
    return jnp.concatenate([x1 * cos - x2 * sin, x2 * cos + x1 * sin], axis=-1).astype(x.dtype)


def causal_conv(x, buf, w, b):
    S = x.shape[1]
    xp = jnp.concatenate([buf.astype(x.dtype), x], axis=1)
    y = b + w[0] * xp[:, 0:S]
    for k in range(1, CONV_W):
        y = y + w[k] * xp[:, k:k + S]
    return y, xp[:, xp.shape[1] - (CONV_W - 1):]


def rg_lru(x, h0, w_a, b_a, w_i, b_i, lam):
    B, S, _ = x.shape
    xb = x.reshape(B, S, LRU_BLOCKS, LRU_BLOCK)
    r = jax.nn.sigmoid(jnp.einsum('bsnd,nde->bsne', xb, w_a).reshape(B, S, D_LRU) + b_a)
    i = jax.nn.sigmoid(jnp.einsum('bsnd,nde->bsne', xb, w_i).reshape(B, S, D_LRU) + b_i)
    log_a = -LRU_C * r.astype(jnp.float32) * jax.nn.softplus(-lam.astype(jnp.float32))
    a = jnp.exp(log_a)
    u = jnp.sqrt(-jnp.expm1(2.0 * log_a)) * (i * x).astype(jnp.float32)

    def step(h, au):
        a_t, u_t = au
        h = a_t * h + u_t
        return h, h

    h_last, hs = lax.scan(step, h0.astype(jnp.float32), (a.swapaxes(0, 1), u.swapaxes(0, 1)))
    return hs.swapaxes(0, 1).astype(x.dtype), h_last.astype(h0.dtype)


def mla_attend(q_lat, q_rope, q_pos, c_kv, k_rope, k_pos):
    B, S, H, C = q_lat.shape
    R = q_rope.shape[-1]
    qb = min(Q_BLOCK, S)
    nqb = S // qb
    scale = (QK_NOPE + QK_ROPE) ** -0.5

    def block(args):
        ql, qr, qp = args
        s = (jnp.einsum('bqhc,btc->bhqt', ql, c_kv)
             + jnp.einsum('bqhr,btr->bhqt', qr, k_rope)).astype(jnp.float32) * scale
        mask = k_pos[None, :] <= qp[:, None]
        s = jnp.where(mask[None, None], s, -jnp.inf)
        p = jax.nn.softmax(s, axis=-1).astype(c_kv.dtype)
        return jnp.einsum('bhqt,btc->bqhc', p, c_kv)

    xs = (q_lat.reshape(B, nqb, qb, H, C).swapaxes(0, 1),
          q_rope.reshape(B, nqb, qb, H, R).swapaxes(0, 1),
          q_pos.reshape(nqb, qb))
    o = lax.map(block, xs)
    return o.swapaxes(0, 1).reshape(B, S, H, C)


def mixer(hn, pos, past_c, past_kr, h0, conv0, p):
    B, S, _ = hn.shape
    z = hn @ p['w_in']
    o1 = D_LRU
    o2 = 2 * D_LRU
    o3 = o2 + Q_LORA
    o4 = o3 + KV_LORA
    x_l, g_l, c_q, c_kv, k_r = z[..., :o1], z[..., o1:o2], z[..., o2:o3], z[..., o3:o4], z[..., o4:]
    xc, conv_new = causal_conv(x_l, conv0, p['conv_w'], p['conv_b'])
    hs, h_new = rg_lru(xc, h0, p['lru_w_a'], p['lru_b_a'], p['lru_w_i'], p['lru_b_i'], p['lru_lambda'])
    y_lru = jax.nn.gelu(g_l) * hs
    q = (rmsnorm(c_q, p['q_norm']) @ p['w_q_up']).reshape(B, S, MLA_HEADS, QK_NOPE + QK_ROPE)
    q_nope = q[..., :QK_NOPE]
    q_rope = rope(q[..., QK_NOPE:], pos)
    c_kv = rmsnorm(c_kv, p['kv_norm'])
    k_r = rope(k_r[:, :, None, :], pos)[:, :, 0, :]
    q_lat = jnp.einsum('bshn,chn->bshc', q_nope, p['w_uk'])
    c_all = jnp.concatenate([past_c.astype(c_kv.dtype), c_kv], axis=1)
    kr_all = jnp.concatenate([past_kr.astype(k_r.dtype), k_r], axis=1)
    k_pos = jnp.arange(c_all.shape[1], dtype=jnp.int32)
    o_lat = mla_attend(q_lat, q_rope, pos, c_all, kr_all, k_pos)
    y_mla = jnp.einsum('bshc,chv->bshv', o_lat, p['w_uv']).reshape(B, S, D_MLA)
    y = jnp.concatenate([rmsnorm(y_lru, p['out_norm_lru']), rmsnorm(y_mla, p['out_norm_mla'])], axis=-1)
    return y @ p['w_out'], c_kv, k_r, h_new, conv_new


def layer(x, pos, past_c, past_kr, h0, conv0, p):
    x = x + 0.5 * swiglu(rmsnorm(x, p['ffn1_norm']), p['ffn1_w_gate'], p['ffn1_w_up'], p['ffn1_w_down'])
    y, c_new, kr_new, h_new, conv_new = mixer(rmsnorm(x, p['mix_norm']), pos, past_c, past_kr, h0, conv0, p)
    x = x + y
    x = x + 0.5 * swiglu(rmsnorm(x, p['ffn2_norm']), p['ffn2_w_gate'], p['ffn2_w_up'], p['ffn2_w_down'])
    return x, c_new, kr_new, h_new, conv_new


def setup_inputs(seed: int = 0) -> dict:
    key = jax.random.key(seed)
    ks = iter(jax.random.split(key, 40))
    n_pages = PAST_LEN // PAGE_SIZE
    n_used = DEC_BATCH * n_pages
    n_phys = (n_used * 5) // 4

    def nrm(shape, scale=1.0):
        return jax.random.normal(next(ks), shape, jnp.float32) * scale

    def gain(shape):
        return 1.0 + 0.05 * jax.random.normal(next(ks), shape, jnp.float32)

    L = DEPTH
    u = jax.random.uniform(next(ks), (L, D_LRU), jnp.float32, 0.9, 0.999)
    s = u ** (1.0 / LRU_C)
    lam = jnp.log(s) - jnp.log1p(-s)
    page_table = jax.random.permutation(next(ks), n_phys)[:n_used].reshape(DEC_BATCH, n_pages).astype(jnp.int32)
    return {
        'x_prompt': nrm((BATCH, SEQ, D_MODEL)),
        'x_sample': nrm((DEC_BATCH, DEC_SEQ, D_MODEL)),
        'cache_kv_latent': nrm((L, n_phys, PAGE_SIZE, KV_LORA)),
        'cache_k_rope': nrm((L, n_phys, PAGE_SIZE, QK_ROPE)),
        'state_lru_h': nrm((L, DEC_BATCH, D_LRU), 0.5),
        'state_conv': nrm((L, DEC_BATCH, CONV_W - 1, D_LRU)),
        'page_table': page_table,
        'ffn1_norm': gain((L, D_MODEL)),
        'ffn1_w_gate': nrm((L, D_MODEL, D_FF), D_MODEL ** -0.5),
        'ffn1_w_up': nrm((L, D_MODEL, D_FF), D_MODEL ** -0.5),
        'ffn1_w_down': nrm((L, D_FF, D_MODEL), D_FF ** -0.5),
        'mix_norm': gain((L, D_MODEL)),
        'w_in': nrm((L, D_MODEL, D_IN), D_MODEL ** -0.5),
        'conv_w': nrm((L, CONV_W, D_LRU), CONV_W ** -0.5),
        'conv_b': nrm((L, D_LRU), 0.02),
        'lru_w_a': nrm((L, LRU_BLOCKS, LRU_BLOCK, LRU_BLOCK), LRU_BLOCK ** -0.5),
        'lru_b_a': nrm((L, D_LRU), 0.02),
        'lru_w_i': nrm((L, LRU_BLOCKS, LRU_BLOCK, LRU_BLOCK), LRU_BLOCK ** -0.5),
        'lru_b_i': nrm((L, D_LRU), 0.02),
        'lru_lambda': lam,
        'q_norm': gain((L, Q_LORA)),
        'w_q_up': nrm((L, Q_LORA, MLA_HEADS * (QK_NOPE + QK_ROPE)), Q_LORA ** -0.5),
        'kv_norm': gain((L, KV_LORA)),
        'w_uk': nrm((L, KV_LORA, MLA_HEADS, QK_NOPE), KV_LORA ** -0.5),
        'w_uv': nrm((L, KV_LORA, MLA_HEADS, V_DIM), KV_LORA ** -0.5),
        'out_norm_lru': gain((L, D_LRU)),
        'out_norm_mla': gain((L, D_MLA)),
        'w_out': nrm((L, D_MIX, D_MODEL), D_MIX ** -0.5),
        'ffn2_norm': gain((L, D_MODEL)),
        'ffn2_w_gate': nrm((L, D_MODEL, D_FF), D_MODEL ** -0.5),
        'ffn2_w_up': nrm((L, D_MODEL, D_FF), D_MODEL ** -0.5),
        'ffn2_w_down': nrm((L, D_FF, D_MODEL), D_FF ** -0.5),
        'final_norm': gain((D_MODEL,)),
    }


def reference(x_prompt, x_sample, cache_kv_latent, cache_k_rope, state_lru_h, state_conv, page_table,
              ffn1_norm, ffn1_w_gate, ffn1_w_up, ffn1_w_down, mix_norm, w_in, conv_w, conv_b,
              lru_w_a, lru_b_a, lru_w_i, lru_b_i, lru_lambda, q_norm, w_q_up, kv_norm, w_uk, w_uv,
              out_norm_lru, out_norm_mla, w_out, ffn2_norm, ffn2_w_gate, ffn2_w_up, ffn2_w_down, final_norm):
    B, S = x_prompt.shape[0], x_prompt.shape[1]
    DB, DS = x_sample.shape[0], x_sample.shape[1]
    past_len = page_table.shape[1] * PAGE_SIZE
    pos_p = jnp.arange(S, dtype=jnp.int32)
    pos_s = past_len + jnp.arange(DS, dtype=jnp.int32)
    yp, ys = x_prompt, x_sample
    kvp, krp, hp, cvp, kvs, krs, hs_, cvs = [], [], [], [], [], [], [], []
    for l in range(DEPTH):
        p = {
            'ffn1_norm': ffn1_norm[l], 'ffn1_w_gate': ffn1_w_gate[l], 'ffn1_w_up': ffn1_w_up[l],
            'ffn1_w_down': ffn1_w_down[l], 'mix_norm': mix_norm[l], 'w_in': w_in[l],
            'conv_w': conv_w[l], 'conv_b': conv_b[l], 'lru_w_a': lru_w_a[l], 'lru_b_a': lru_b_a[l],
            'lru_w_i': lru_w_i[l], 'lru_b_i': lru_b_i[l], 'lru_lambda': lru_lambda[l],
            'q_norm': q_norm[l], 'w_q_up': w_q_up[l], 'kv_norm': kv_norm[l], 'w_uk': w_uk[l],
            'w_uv': w_uv[l], 'out_norm_lru': out_norm_lru[l], 'out_norm_mla': out_norm_mla[l],
            'w_out': w_out[l], 'ffn2_norm': ffn2_norm[l], 'ffn2_w_gate': ffn2_w_gate[l],
            'ffn2_w_up': ffn2_w_up[l], 'ffn2_w_down': ffn2_w_down[l],
        }
        zc = jnp.zeros((B, 0, KV_LORA), x_prompt.dtype)
        zr = jnp.zeros((B, 0, QK_ROPE), x_prompt.dtype)
        zh = jnp.zeros((B, D_LRU), jnp.float32)
        zv = jnp.zeros((B, CONV_W - 1, D_LRU), x_prompt.dtype)
        yp, c1, r1, h1, v1 = layer(yp, pos_p, zc, zr, zh, zv, p)
        past_c = cache_kv_latent[l, page_table].reshape(DB, past_len, KV_LORA)
        past_kr = cache_k_rope[l, page_table].reshape(DB, past_len, QK_ROPE)
        ys, c2, r2, h2, v2 = layer(ys, pos_s, past_c, past_kr, state_lru_h[l], state_conv[l], p)
        kvp.append(c1); krp.append(r1); hp.append(h1); cvp.append(v1)
        kvs.append(c2); krs.append(r2); hs_.append(h2); cvs.append(v2)
    yp = rmsnorm(yp, final_norm)
    ys = rmsnorm(ys, final_norm)
    return (yp, ys, jnp.stack(kvp), jnp.stack(krp), jnp.stack(hp), jnp.stack(cvp),
            jnp.stack(kvs), jnp.stack(krs), jnp.stack(hs_), jnp.stack(cvs))
```

```python
import numpy as np
from contextlib import ExitStack
import concourse.bass as bass
import concourse.mybir as mybir
from concourse.bass_utils import run_bass_kernel_spmd

F32 = mybir.dt.float32
BF16 = mybir.dt.bfloat16
I32 = mybir.dt.int32
AF = mybir.ActivationFunctionType
ALU = mybir.AluOpType

D = 1024
DFF = 2816
NF = DFF // 128
DL = 512
QL = 256
KVL = 128
RP = 32
NH = 8
DIN = 1440
EPS = 1e-6
EPOCH = 2000
DGEN = 120
KDMA = 8


class Prog:
    def __init__(self, nc, es):
        self.nc = nc
        self.es = es
        self.engs = {'pe': nc.tensor, 'act': nc.scalar, 'dve': nc.vector, 'pool': nc.gpsimd, 'sp': nc.sync}
        self.cnt = {e: 0 for e in self.engs}
        self.sems = {e: [] for e in self.engs}
        self.dma_cnt = {'sp': 0, 'pool': 0}
        self.dma_sems = {q: [] for q in ('sp', 'pool')}
        self.waited = {}
        self.last_w = {}
        self.readers = {}
        self.semown = {}

    def _sem(self, e, ep):
        while len(self.sems[e]) <= ep:
            s = self.es.enter_context(self.nc.semaphore(f"s_{e}_{len(self.sems[e])}"))
            self.semown[id(s)] = e
            self.sems[e].append(s)
        return self.sems[e][ep]

    def _wait(self, e, tok):
        sem, val = tok
        k = (e, id(sem))
        if self.waited.get(k, 0) >= val:
            return
        self.waited[k] = val
        self.engs[e].wait_ge(sem, val)

    def _deps(self, e, r, w):
        toks = []
        for k in r:
            t = self.last_w.get(k)
            if t is not None:
                toks.append(t)
        for k in w:
            t = self.last_w.get(k)
            if t is not None:
                toks.append(t)
            for t in self.readers.get(k, ()):
                toks.append(t)
        out = []
        for t in toks:
            if e == 'pe' and self.semown.get(id(t[0])) == 'pe':
                continue
            out.append(t)
        return out

    def _record(self, tok, r, w):
        for k in w:
            self.last_w[k] = tok
            self.readers[k] = []
        for k in r:
            self.readers.setdefault(k, []).append(tok)

    def op(self, e, fn, r=(), w=(), inc=True):
        for t in self._deps(e, r, w):
            self._wait(e, t)
        i = self.cnt[e]
        sem = self._sem(e, i // EPOCH)
        inst = fn()
        tok = (sem, i % EPOCH + 1)
        if inc:
            inst.then_inc(sem, 1)
            self.cnt[e] += 1
        self._record(tok, r, w)
        return tok

    def _dsem(self, q, n):
        st = n // (KDMA * DGEN)
        while len(self.dma_sems[q]) <= st:
            k = len(self.dma_sems[q])
            self.dma_sems[q].append([self.es.enter_context(self.nc.semaphore(f"dq_{q}_{k}_{i}")) for i in range(KDMA)])
        m = n % (KDMA * DGEN)
        return self.dma_sems[q][st][m % KDMA], m // KDMA, st

    def dma(self, q, fn, r=(), w=()):
        for t in self._deps(q, r, w):
            self._wait(q, t)
        n = self.dma_cnt[q]
        sem, gen, st = self._dsem(q, n)
        if gen > 0:
            self._wait(q, (sem, 16 * gen))
        elif st > 0:
            ps, pg, _ = self._dsem(q, n - KDMA)
            self._wait(q, (ps, 16 * (pg + 1)))
        inst = fn()
        inst.then_inc(sem, 16)
        tok = (sem, 16 * (gen + 1))
        self.dma_cnt[q] += 1
        self._record(tok, r, w)
        return tok

    def barrier(self):
        toks = []
        for e in self.engs:
            i = self.cnt[e]
            if i > 0:
                toks.append((self.sems[e][(i - 1) // EPOCH], (i - 1) % EPOCH + 1))
        toks += self._dma_tail()
        for e in self.engs:
            for t in toks:
                if self.semown.get(id(t[0])) == e:
                    continue
                self._wait(e, t)

    def _dma_tail(self):
        toks = []
        for q in ('sp', 'pool'):
            n = self.dma_cnt[q]
            for m in range(max(0, n - KDMA), n):
                sem, gen, _ = self._dsem(q, m)
                toks.append((sem, 16 * (gen + 1)))
        return toks

    def finish(self):
        for t in self._dma_tail():
            for q in ('sp', 'pool'):
                self._wait(q, t)


def _tile(nc, es, name, shape, dt):
    return es.enter_context(nc.sbuf_tensor(name, list(shape), dt))


def _psum(nc, es, name, shape, dt):
    return es.enter_context(nc.psum_tensor(name, list(shape), dt))


def emit_ffn_weight_prep(P, nc, es0, wg, wu, wd, gain_cols, WGU_s, WD_s, tag):
    with ExitStack() as es:
        stg = [_tile(nc, es, f"wp_stg{tag}{i}", [128, 2, 8, 128], F32) for i in range(2)]
        stgd = [_tile(nc, es, f"wp_stgd{tag}{i}", [128, 1024], F32) for i in range(2)]
        ob = [_tile(nc, es, f"wp_ob{tag}{i}", [128, 2, 8, 128], BF16) for i in range(2)]
        obd = [_tile(nc, es, f"wp_obd{tag}{i}", [128, 1024], BF16) for i in range(2)]
        gb = _tile(nc, es, f"wp_gb{tag}", [128, 8, 128], F32)
        gc = _tile(nc, es, f"wp_gc{tag}", [128, 8], F32)
        ones = _tile(nc, es, f"wp_ones{tag}", [128, 128], F32)
        P.dma('sp', lambda: nc.sync.dma_start(out=gc[:, :], in_=gain_cols), w=[('wp_gc', tag)])
        P.op('dve', lambda: nc.vector.memset(ones[:, :], 1.0), w=[('wp_ones', tag)])
        for k in range(8):
            P.op('dve', lambda k=k: nc.vector.tensor_scalar(
                out=gb[:, k, :], in0=ones[:, :], scalar1=gc[:, k:k + 1], scalar2=None, op0=ALU.mult),
                r=[('wp_gc', tag), ('wp_ones', tag)], w=[('wp_gb', tag)])
        for f in range(NF):
            b = f % 2
            P.dma('sp', lambda: nc.sync.dma_start(
                out=stg[b][:, 0, :, :], in_=wg[:, f * 128:(f + 1) * 128].rearrange("(k p) c -> p k c", p=128)),
                w=[('wp_stg', tag, b, 0)])
            P.dma('sp', lambda: nc.sync.dma_start(
                out=stg[b][:, 1, :, :], in_=wu[:, f * 128:(f + 1) * 128].rearrange("(k p) c -> p k c", p=128)),
                w=[('wp_stg', tag, b, 1)])
            P.dma('sp', lambda: nc.sync.dma_start(out=stgd[b][:, :], in_=wd[f * 128:(f + 1) * 128, :]),
                  w=[('wp_stgd', tag, b)])
            for h in range(2):
                eng = 'dve' if h == 0 else 'pool'
                E = nc.vector if h == 0 else nc.gpsimd
                P.op(eng, lambda E=E, h=h: E.tensor_tensor(
                    out=ob[b][:, h, :, :], in0=stg[b][:, h, :, :], in1=gb[:, :, :], op=ALU.mult),
                    r=[('wp_stg', tag, b, h), ('wp_gb', tag)], w=[('wp_ob', tag, b, h)])
            P.op('act', lambda: nc.scalar.copy(out=obd[b][:, :], in_=stgd[b][:, :]),
                 r=[('wp_stgd', tag, b)], w=[('wp_obd', tag, b)])
            P.dma('pool', lambda: nc.gpsimd.dma_start(out=WGU_s[f], in_=ob[b][:, :, :, :]),
                  r=[('wp_ob', tag, b, 0), ('wp_ob', tag, b, 1)], w=[('WGU', tag, f)])
            P.dma('pool', lambda: nc.gpsimd.dma_start(out=WD_s[f], in_=obd[b][:, :]),
                  r=[('wp_obd', tag, b)], w=[('WD', tag, f)])
    P.barrier()


def emit_ffn_phase(P, nc, identb, tiles, WGU_s, WD_s, tag, final_gain=None):
    with ExitStack() as es:
        NW = 3
        xt = [_tile(nc, es, f"f_xt{tag}{i}", [128, 4, D], F32) for i in range(2)]
        xn = _tile(nc, es, f"f_xn{tag}", [128, D], BF16)
        xnT = _tile(nc, es, f"f_xnT{tag}", [128, 8, 512], BF16)
        hT = _tile(nc, es, f"f_hT{tag}", [128, NF, 512], BF16)
        sg = [_tile(nc, es, f"f_sg{tag}{i}", [128, 512], F32) for i in range(2)]
        wgu = [_tile(nc, es, f"f_wgu{tag}{i}", [128, 2, 8, 128], BF16) for i in range(NW)]
        wd_all = _tile(nc, es, f"f_wdall{tag}", [128, NF, 1024], BF16)
        ss = _tile(nc, es, f"f_ss{tag}", [128, 8], F32)
        rstd = _tile(nc, es, f"f_rstd{tag}", [128, 8], F32)
        junk = _tile(nc, es, f"f_junk{tag}", [128, D], F32)
        pg = [_psum(nc, es, f"f_pg{tag}{i}", [128, 512], F32) for i in range(2)]
        pu = [_psum(nc, es, f"f_pu{tag}{i}", [128, 512], F32) for i in range(2)]
        py = [_psum(nc, es, f"f_py{tag}{i}", [128, 512], F32) for i in range(2)]
        pt = _psum(nc, es, f"f_pt{tag}", [128, 8, 128], BF16)
        if final_gain is not None:
            fg = _tile(nc, es, f"f_fg{tag}", [128, D], F32)
            P.dma('sp', lambda: nc.sync.dma_start(out=fg[:, :], in_=final_gain.partition_broadcast(128)),
                  w=[('f_fg', tag)])
        wcount = 0
        ycount = 0
        P.dma('sp', lambda: nc.sync.dma_start(out=wd_all[:, :, :], in_=WD_s.rearrange("f p c -> p f c")),
              r=[('WD', tag, f) for f in range(NF)], w=[('f_wdall', tag)])
        for ti, (src, dst, T, skeys, dkeys) in enumerate(tiles):
            ns = T // 128
            xb = xt[ti % 2]
            kx = ('f_xt', tag, ti % 2)
            P.dma('sp', lambda: nc.sync.dma_start(
                out=xb[:, 0:ns, :], in_=src.rearrange("(s p) d -> p s d", p=128)), r=skeys, w=[kx])
            for s in range(ns):
                P.op('act', lambda s=s: nc.scalar.activation(
                    out=junk[:, :], in_=xb[:, s, :], func=AF.Square, accum_out=ss[:, s:s + 1]),
                    r=[kx], w=[('f_junk', tag), ('f_ss', tag, s)])
            P.op('act', lambda: nc.scalar.activation(
                out=rstd[:, 0:ns], in_=ss[:, 0:ns], func=AF.Sqrt, bias=EPS, scale=1.0 / D),
                r=[('f_ss', tag, s) for s in range(ns)], w=[('f_rstd', tag)])
            P.op('dve', lambda: nc.vector.reciprocal(out=rstd[:, 0:ns], in_=rstd[:, 0:ns]),
                 r=[('f_rstd', tag)], w=[('f_rstd', tag)])
            for s in range(ns):
                P.op('dve', lambda s=s: nc.vector.tensor_scalar(
                    out=xn[:, :], in0=xb[:, s, :], scalar1=rstd[:, s:s + 1], scalar2=None, op0=ALU.mult),
                    r=[kx, ('f_rstd', tag)], w=[('f_xn', tag)])
                for k in range(8):
                    P.op('pe', lambda k=k: nc.tensor.transpose(
                        out=pt[:, k, :], in_=xn[:, k * 128:(k + 1) * 128], identity=identb[:, :]),
                        r=[('f_xn', tag), 'identb'], w=[('f_pt', tag)], inc=(k == 7))
                P.op('act', lambda s=s: nc.scalar.copy(out=xnT[:, :, s * 128:(s + 1) * 128], in_=pt[:, :, :]),
                     r=[('f_pt', tag)], w=[('f_xnT', tag, s)])
            kxnT = [('f_xnT', tag, s) for s in range(ns)]
            for f in range(NF):
                wb = wcount % NW
                wcount += 1
                P.dma('sp', lambda: nc.sync.dma_start(out=wgu[wb][:, :, :, :], in_=WGU_s[f]),
                      r=[('WGU', tag, f)], w=[('f_wgu', tag, wb)])
                pb = f % 2
                for k in range(8):
                    P.op('pe', lambda k=k: nc.tensor.matmul(
                        pg[pb][:, 0:T], lhsT=wgu[wb][:, 0, k, :], rhs=xnT[:, k, 0:T], start=(k == 0), stop=(k == 7)),
                        r=[('f_wgu', tag, wb)] + kxnT, w=[('f_pg', tag, pb)], inc=(k == 7))
                for k in range(8):
                    P.op('pe', lambda k=k: nc.tensor.matmul(
                        pu[pb][:, 0:T], lhsT=wgu[wb][:, 1, k, :], rhs=xnT[:, k, 0:T], start=(k == 0), stop=(k == 7)),
                        r=[('f_wgu', tag, wb)] + kxnT, w=[('f_pu', tag, pb)], inc=(k == 7))
                P.op('act', lambda: nc.scalar.activation(out=sg[pb][:, 0:T], in_=pg[pb][:, 0:T], func=AF.Silu),
                     r=[('f_pg', tag, pb)], w=[('f_sg', tag, pb)])
                P.op('dve', lambda f=f: nc.vector.tensor_tensor(
                    out=hT[:, f, 0:T], in0=sg[pb][:, 0:T], in1=pu[pb][:, 0:T], op=ALU.mult),
                    r=[('f_sg', tag, pb), ('f_pu', tag, pb)], w=[('f_hT', tag, f)])
            for s in range(ns):
                for half in range(2):
                    yb = ycount % 2
                    ycount += 1
                    for f in range(NF):
                        P.op('pe', lambda: nc.tensor.matmul(
                            py[yb][:, :], lhsT=hT[:, f, s * 128:(s + 1) * 128],
                            rhs=wd_all[:, f, half * 512:(half + 1) * 512], start=(f == 0), stop=(f == NF - 1)),
                            r=[('f_hT', tag, f), ('f_wdall', tag)], w=[('f_py', tag, yb)], inc=(f == NF - 1))
                    P.op('dve', lambda: nc.vector.scalar_tensor_tensor(
                        out=xb[:, s, half * 512:(half + 1) * 512], in0=py[yb][:, :], scalar=0.5,
                        in1=xb[:, s, half * 512:(half + 1) * 512], op0=ALU.mult, op1=ALU.add),
                        r=[('f_py', tag, yb), kx], w=[kx])
            if final_gain is not None:
                for s in range(ns):
                    P.op('act', lambda s=s: nc.scalar.activation(
                        out=junk[:, :], in_=xb[:, s, :], func=AF.Square, accum_out=ss[:, s:s + 1]),
                        r=[kx], w=[('f_junk', tag), ('f_ss', tag, s)])
                P.op('act', lambda: nc.scalar.activation(
                    out=rstd[:, 0:ns], in_=ss[:, 0:ns], func=AF.Sqrt, bias=EPS, scale=1.0 / D),
                    r=[('f_ss', tag, s) for s in range(ns)], w=[('f_rstd', tag)])
                P.op('dve', lambda: nc.vector.reciprocal(out=rstd[:, 0:ns], in_=rstd[:, 0:ns]),
                     r=[('f_rstd', tag)], w=[('f_rstd', tag)])
                for s in range(ns):
                    P.op('dve', lambda s=s: nc.vector.scalar_tensor_tensor(
                        out=xb[:, s, :], in0=xb[:, s, :], scalar=rstd[:, s:s + 1], in1=fg[:, :],
                        op0=ALU.mult, op1=ALU.mult),
                        r=[kx, ('f_rstd', tag), ('f_fg', tag)], w=[kx])
            P.dma('pool', lambda: nc.gpsimd.dma_start(
                out=dst.rearrange("(s p) d -> p s d", p=128), in_=xb[:, 0:ns, :]), r=[kx], w=dkeys)
    P.barrier()


def build_program(cfg, stage="full"):
    SEQ, NB, NPG, NPHYS = cfg['SEQ'], cfg['NB'], cfg['NPG'], cfg['NPHYS']
    NSLOT = SEQ // 128 // 4
    NOWN = NSLOT * 128
    NS = NB * 8
    NG = NPG // 8
    nc = bass.Bass("TRN2", target_bir_lowering=False)

    def din(name, shape, dt=F32):
        return nc.dram_tensor(name, list(shape), dt, kind="ExternalInput").ap()

    def dout(name, shape, dt=F32):
        return nc.dram_tensor(name, list(shape), dt, kind="ExternalOutput").ap()

    def dscr(name, shape, dt):
        return nc.dram_tensor(name, list(shape), dt, kind="Internal").ap()

    I = {}
    I['xp'] = din('xp', [SEQ, D])
    I['xs'] = din('xs', [NS, D])
    for n in ('ffn1', 'ffn2'):
        I[n + '_wg'] = din(n + '_wg', [D, DFF])
        I[n + '_wu'] = din(n + '_wu', [D, DFF])
        I[n + '_wd'] = din(n + '_wd', [DFF, D])
        I[n + '_gc'] = din(n + '_gc', [128, 8])
    I['final_norm'] = din('final_norm', [D])
    I['identb'] = din('identb', [128, 128], BF16)
    I['identf'] = din('identf', [128, 128])
    I['w_in'] = din('w_in', [D, DIN]); I['mix_gc'] = din('mix_gc', [128, 8])
    I['w_q_up'] = din('w_q_up', [QL, 768]); I['q_gc'] = din('q_gc', [128, 2])
    I['w_out'] = din('w_out', [D, D]); I['out_gc'] = din('out_gc', [128, 8])
    I['w_uk'] = din('w_uk', [KVL, 8, 64]); I['w_uv'] = din('w_uv', [KVL, 8, 64])
    I['lru_w_a'] = din('lru_w_a', [8, 64, 64]); I['lru_w_i'] = din('lru_w_i', [8, 64, 64])
    I['chv'] = din('chv', [128, 32]); I['kv_norm'] = din('kv_norm', [KVL])
    I['cosp'] = din('cosp', [SEQ, 16]); I['sinp'] = din('sinp', [SEQ, 16])
    I['coss'] = din('coss', [NS, 16]); I['sins'] = din('sins', [NS, 16])
    I['conv0'] = din('conv0', [NB, 3, DL]); I['h0'] = din('h0', [NB, DL])
    I['own_idx'] = din('own_idx', [128, NSLOT], I32)
    I['masks'] = din('masks', [128, 4, 128])
    I['pt_l'] = din('pt_l', [8, NB, NG], I32)
    I['qcol'] = din('qcol', [128, 1]); I['mask_own'] = din('mask_own', [8, 64])
    I['ckv_cache'] = din('ckv_cache', [NPHYS * 128, KVL]); I['ckr_cache'] = din('ckr_cache', [NPHYS * 128, RP])
    O = {}
    O['y_own'] = dout('y_own', [NOWN, D]); O['ys'] = dout('ys', [NS, D])
    O['kvp'] = dout('kvp', [SEQ, KVL]); O['krp'] = dout('krp', [SEQ, RP])
    O['hp'] = dout('hp', [1, DL]); O['cvp'] = dout('cvp', [3, DL])
    O['kvs'] = dout('kvs', [NS, KVL]); O['krs'] = dout('krs', [NS, RP])
    O['hs_'] = dout('hs_', [NB, DL]); O['cvs'] = dout('cvs', [NB, 3, DL])
    if cfg.get('debug'):
        O['dbg_idx'] = dout('dbg_idx', [128, NB * NG], I32)
        O['dbg_G'] = dout('dbg_G', [128, 1024])
        O['dbg_GR'] = dout('dbg_GR', [128, 256])
        O['dbg_YM'] = dout('dbg_YM', [NS, DL])
        O['dbg_ps'] = dout('dbg_ps', [128, 512])
        O['dbg_ov'] = dout('dbg_ov', [64, 129])
    S = {}
    for n in ('ffn1', 'ffn2'):
        S[n + '_WGU'] = dscr(n + '_WGU', [NF, 128, 2, 8, 128], BF16)
        S[n + '_WD'] = dscr(n + '_WD', [NF, 128, 1024], BF16)
    S['X1'] = dscr('X1', [SEQ + NS, D], F32)
    S['X2'] = dscr('X2', [NOWN + NS, D], F32)
    S['YL'] = dscr('YL', [SEQ + NS, DL], F32)
    S['QS'] = dscr('QS', [SEQ + NS, 768], BF16)
    S['CKS'] = dscr('CKS', [NS, KVL], BF16)
    S['YM'] = dscr('YM', [NS, DL], F32)

    with ExitStack() as es:
        P = Prog(nc, es)
        W = {}
        W['identb'] = _tile(nc, es, "identb_sb", [128, 128], BF16)
        W['identf'] = _tile(nc, es, "identf_sb", [128, 128], F32)
        identb = W['identb']
        P.dma('sp', lambda: nc.sync.dma_start(out=identb[:, :], in_=I['identb']), w=['identb'])
        P.dma('sp', lambda: nc.sync.dma_start(out=W['identf'][:, :], in_=I['identf']), w=['identf'])
        for n in ('ffn1', 'ffn2'):
            emit_ffn_weight_prep(P, nc, es, I[n + '_wg'], I[n + '_wu'], I[n + '_wd'], I[n + '_gc'],
                                 S[n + '_WGU'], S[n + '_WD'], n)
        TT = 512
        tiles = []
        for t in range(SEQ // TT):
            tiles.append((I['xp'][t * TT:(t + 1) * TT, :], S['X1'][t * TT:(t + 1) * TT, :], TT, [],
                          [('X1', t * 4 + i) for i in range(4)]))
        tiles.append((I['xs'], S['X1'][SEQ:SEQ + NS, :], NS, [], [('X1', 's')]))
        emit_ffn_phase(P, nc, identb, tiles, S['ffn1_WGU'], S['ffn1_WD'], 'ffn1')
        with ExitStack() as es2:
            W['wq'] = _tile(nc, es2, "W_wq", [128, 2, 768], BF16)
            W['wout'] = _tile(nc, es2, "W_wout", [128, 8, D], BF16)
            W['wuk'] = _tile(nc, es2, "W_wuk", [128, 8, 64], BF16)
            W['wuv'] = _tile(nc, es2, "W_wuv", [128, 8, 64], BF16)
            W['wukT'] = _tile(nc, es2, "W_wukT", [64, 8, 128], BF16)
            W['bd'] = _tile(nc, es2, "W_bd", [128, 2, 4, 128], BF16)
            W['chv'] = _tile(nc, es2, "W_chv", [128, 32], F32)
            W['cch'] = _tile(nc, es2, "W_cch", [128, 4], F32)
            W['kvg'] = _tile(nc, es2, "W_kvg", [128, KVL], F32)
            St = {}
            St['ckvT'] = _tile(nc, es2, "St_ckvT", [128, SEQ], BF16)
            St['KhT'] = [_tile(nc, es2, f"St_KhT{i}", [96, SEQ], BF16) for i in range(2)]
            St['ckvT_s'] = _tile(nc, es2, "St_ckvTs", [128, 128], BF16)
            St['krT_s'] = _tile(nc, es2, "St_krTs", [96, 128], BF16)
            with ExitStack() as es3:
                W['win'] = _tile(nc, es3, "W_win", [128, 8, DIN], BF16)
                emit_small_prep(P, nc, es2, I, W)
                emit_mixer_prompt(P, nc, W, St, I, O, S, cfg)
                emit_mixer_in(P, nc, W, St, I, O, S, cfg, sample=True)
            emit_prompt_attn(P, nc, W, St, I, S, cfg)
            emit_sample_attn(P, nc, W, St, I, S, cfg, O)
        P.barrier()
        tiles = []
        t0 = 0
        while t0 < NOWN:
            tt = min(TT, NOWN - t0)
            tiles.append((S['X2'][t0:t0 + tt, :], O['y_own'][t0:t0 + tt, :], tt,
                          [('X2', t0 // 128 + i) for i in range(tt // 128)], [('y_own', t0)]))
            t0 += tt
        tiles.append((S['X2'][NOWN:NOWN + NS, :], O['ys'], NS, [('X2', 's')], [('ys',)]))
        emit_ffn_phase(P, nc, identb, tiles, S['ffn2_WGU'], S['ffn2_WD'], 'ffn2', final_gain=I['final_norm'])
        P.finish()
    return nc


def emit_small_prep(P, nc, es0, I, W):
    with ExitStack() as es:
        stg = [_tile(nc, es, f"sp_stg{i}", [128, 1440], F32) for i in range(2)]
        gc = _tile(nc, es, "sp_gc", [128, 18], F32)
        P.dma('sp', lambda: nc.sync.dma_start(out=gc[:, 0:8], in_=I['mix_gc']), w=['sp_gc'])
        P.dma('sp', lambda: nc.sync.dma_start(out=gc[:, 8:10], in_=I['q_gc']), w=['sp_gc'])
        P.dma('sp', lambda: nc.sync.dma_start(out=gc[:, 10:18], in_=I['out_gc']), w=['sp_gc'])
        n = 0

        def cast_rows(dst, src, ncols, gcol):
            nonlocal n
            b = n % 2
            n += 1
            P.dma('sp', lambda: nc.sync.dma_start(out=stg[b][:, 0:ncols], in_=src), w=[('sp_stg', b)])
            if gcol is None:
                P.op('dve', lambda: nc.vector.tensor_copy(out=dst, in_=stg[b][:, 0:ncols]),
                     r=[('sp_stg', b)], w=['Wsmall'])
            else:
                P.op('dve', lambda: nc.vector.tensor_scalar(
                    out=dst, in0=stg[b][:, 0:ncols], scalar1=gc[:, gcol:gcol + 1], scalar2=None, op0=ALU.mult),
                    r=[('sp_stg', b), 'sp_gc'], w=['Wsmall'])
        for k in range(8):
            cast_rows(W['win'][:, k, :], I['w_in'][k * 128:(k + 1) * 128, :], DIN, k)
        for k in range(2):
            cast_rows(W['wq'][:, k, :], I['w_q_up'][k * 128:(k + 1) * 128, :], 768, 8 + k)
        for k in range(8):
            cast_rows(W['wout'][:, k, :], I['w_out'][k * 128:(k + 1) * 128, :], D, 10 + k)
        cast_rows(W['wuk'][:, :, :], I['w_uk'].rearrange("c h n -> c (h n)"), 512, None)
        cast_rows(W['wuv'][:, :, :], I['w_uv'].rearrange("c h n -> c (h n)"), 512, None)
        bd = _tile(nc, es, "sp_bd", [128, 2, 4, 128], F32)
        P.op('dve', lambda: nc.vector.memset(bd[:, :, :, :], 0.0), w=['sp_bd'])
        for gi, nm in enumerate(('lru_w_a', 'lru_w_i')):
            for j in range(4):
                P.dma('sp', lambda: nc.sync.dma_start(out=bd[0:64, gi, j, 0:64], in_=I[nm][2 * j]), r=[], w=['sp_bd'])
                P.dma('sp', lambda: nc.sync.dma_start(out=bd[64:128, gi, j, 64:128], in_=I[nm][2 * j + 1]), r=[], w=['sp_bd'])
        P.op('dve', lambda: nc.vector.tensor_copy(out=W['bd'][:, :, :, :], in_=bd[:, :, :, :]), r=['sp_bd'], w=['Wsmall'])
        P.dma('sp', lambda: nc.sync.dma_start(out=W['chv'][:, :], in_=I['chv']), w=['Wsmall'])
        P.dma('sp', lambda: nc.sync.dma_start(out=W['kvg'][:, :], in_=I['kv_norm'].partition_broadcast(128)), w=['Wsmall'])
        P.op('act', lambda: nc.scalar.activation(out=W['cch'][:, :], in_=W['chv'][:, 28:32], func=AF.Exp, scale=-1.0),
             r=['Wsmall'], w=['cch'])
        P.op('act', lambda: nc.scalar.activation(out=W['cch'][:, :], in_=W['cch'][:, :], func=AF.Ln, bias=1.0, scale=1.0),
             r=['cch'], w=['cch'])
        P.op('dve', lambda: nc.vector.tensor_scalar(out=W['cch'][:, :], in0=W['cch'][:, :], scalar1=-8.0, scalar2=None, op0=ALU.mult),
             r=['cch'], w=['cch', 'Wsmall'])
        pT = _psum(nc, es, "sp_pT", [64, 8, 128], BF16)
        for h in range(8):
            P.op('pe', lambda: nc.tensor.transpose(out=pT[:, h, :], in_=W['wuk'][:, h, :], identity=W['identb'][:, :]),
                 r=['Wsmall', 'identb'], w=['sp_pT'], inc=(h == 7))
        P.op('act', lambda: nc.scalar.copy(out=W['wukT'][:, :, :], in_=pT[:, :, :]), r=['sp_pT'], w=['Wsmall'])
    P.barrier()


def emit_mixer_in(P, nc, W, St, I, O, S, cfg, sample):
    SEQ, NB = cfg['SEQ'], cfg['NB']
    NS = NB * 8
    nblk = 1 if sample else SEQ // 128
    tg = 's' if sample else 'p'
    with ExitStack() as es:
        x1 = _tile(nc, es, f"m_x1{tg}", [128, D], F32)
        junk = _tile(nc, es, f"m_junk{tg}", [128, 1536], F32)
        xn = _tile(nc, es, f"m_xn{tg}", [128, D], BF16)
        xnT = _tile(nc, es, f"m_xnT{tg}", [128, 8, 128], BF16)
        st = _tile(nc, es, f"m_st{tg}", [128, 8], F32)
        XW = 11 if sample else 131
        xlx = _tile(nc, es, f"m_xlx{tg}", [128, 4, (16 * 11) if sample else 131], F32)
        xc = _tile(nc, es, f"m_xc{tg}", [128, 4, 128], F32)
        xcb = _tile(nc, es, f"m_xcb{tg}", [128, 4, 128], BF16)
        rr = _tile(nc, es, f"m_rr{tg}", [128, 4, 128], F32)
        ii = _tile(nc, es, f"m_ii{tg}", [128, 4, 128], F32)
        aa = _tile(nc, es, f"m_aa{tg}", [128, 4, 128], F32)
        mm = _tile(nc, es, f"m_mm{tg}", [128, 4, 128], F32)
        uu = _tile(nc, es, f"m_uu{tg}", [128, 4, 128], F32)
        hs = _tile(nc, es, f"m_hs{tg}", [128, 4, 128], F32)
        hp = _tile(nc, es, f"m_hp{tg}", [128, 4, 16], F32)
        g1 = _tile(nc, es, f"m_g1{tg}", [128, 4, 128], F32)
        g2 = _tile(nc, es, f"m_g2{tg}", [128, 4, 128], F32)
        yl = _tile(nc, es, f"m_yl{tg}", [128, DL], F32)
        cs = _tile(nc, es, f"m_cs{tg}", [128, 2, 16], F32)
        cqn = _tile(nc, es, f"m_cqn{tg}", [128, QL], BF16)
        cqT = _tile(nc, es, f"m_cqT{tg}", [128, 2, 128], BF16)
        qf = _tile(nc, es, f"m_qf{tg}", [128, 8, 96], F32)
        qb = _tile(nc, es, f"m_qb{tg}", [128, 8, 96], BF16)
        rt = _tile(nc, es, f"m_rt{tg}", [128, 4, 8, 16], F32)
        ckv = _tile(nc, es, f"m_ckv{tg}", [128, KVL], F32)
        ckb = _tile(nc, es, f"m_ckb{tg}", [128, KVL], BF16)
        krf = _tile(nc, es, f"m_krf{tg}", [128, RP], F32)
        krt = _tile(nc, es, f"m_krt{tg}", [128, 4, 16], F32)
        kq = _tile(nc, es, f"m_kq{tg}", [128, 96], BF16)
        p_xl = _psum(nc, es, f"m_pxl{tg}", [128, 4, 128], F32)
        p_gl = _psum(nc, es, f"m_pgl{tg}", [128, 4, 128], F32)
        p_ga = _psum(nc, es, f"m_pga{tg}", [128, 4, 128], F32)
        p_gi = _psum(nc, es, f"m_pgi{tg}", [128, 4, 128], F32)
        p_zc = _psum(nc, es, f"m_pzc{tg}", [128, 512], F32)
        p_q = _psum(nc, es, f"m_pq{tg}", [128, 2, 512], F32)
        p_tb = _psum(nc, es, f"m_ptb{tg}", [128, 8, 128], BF16)
        p_tf = p_xl
        chv, cch = W['chv'], W['cch']
        identb, identf = W['identb'], W['identf']
        k_ = lambda n: (n, tg)
        P.op('dve', lambda: nc.vector.memset(kq[:, :], 0.0), w=[k_('kq')])
        if not sample:
            P.op('dve', lambda: nc.vector.memset(xlx[:, :, :], 0.0), w=[k_('xlx')])
            P.op('dve', lambda: nc.vector.memset(hp[:, :, :], 0.0), w=[k_('hp')])
        else:
            cst = _tile(nc, es, "m_cst", [48, DL], F32)
            h0t = _tile(nc, es, "m_h0t", [16, DL], F32)
            P.dma('sp', lambda: nc.sync.dma_start(out=cst[0:NB * 3, :], in_=I['conv0'].rearrange("b k d -> (b k) d")), w=['cst'])
            P.dma('sp', lambda: nc.sync.dma_start(out=h0t[0:NB, :], in_=I['h0']), w=['h0t'])
            for j in range(4):
                P.op('pe', lambda: nc.tensor.transpose(out=p_xl[:, j, 0:NB * 3], in_=cst[0:NB * 3, j * 128:(j + 1) * 128],
                                                       identity=identf[0:NB * 3, 0:NB * 3]),
                     r=['cst', 'identf'], w=[k_('pxl')], inc=(j == 3))
            P.op('act', lambda: nc.scalar.copy(
                out=xlx[:, :, :].rearrange("p j (b t) -> p j b t", t=11)[:, :, 0:NB, 0:3],
                in_=p_xl[:, :, 0:NB * 3].rearrange("p j (b k) -> p j b k", k=3)), r=[k_('pxl')], w=[k_('xlx')])
            for j in range(4):
                P.op('pe', lambda: nc.tensor.transpose(out=p_gl[:, j, 0:NB], in_=h0t[0:NB, j * 128:(j + 1) * 128],
                                                       identity=identf[0:NB, 0:NB]),
                     r=['h0t', 'identf'], w=[k_('pgl')], inc=(j == 3))
            P.op('act', lambda: nc.scalar.copy(out=hp[:, :, 0:NB], in_=p_gl[:, :, 0:NB]), r=[k_('pgl')], w=[k_('hp')])
        for blk in range(nblk):
            r0 = SEQ if sample else blk * 128
            nt = NS if sample else 128
            P.dma('sp', lambda: nc.sync.dma_start(out=x1[0:nt, :], in_=S['X1'][r0:r0 + nt, :]),
                  r=[('X1', 's' if sample else blk)], w=[k_('x1')])
            P.dma('sp', lambda: nc.sync.dma_start(out=cs[0:nt, 0, :], in_=(I['coss'] if sample else I['cosp'][r0:r0 + nt, :])), w=[k_('cs')])
            P.dma('sp', lambda: nc.sync.dma_start(out=cs[0:nt, 1, :], in_=(I['sins'] if sample else I['sinp'][r0:r0 + nt, :])), w=[k_('cs')])
            P.op('act', lambda: nc.scalar.activation(out=junk[:, 0:D], in_=x1[:, :], func=AF.Square, accum_out=st[:, 0:1]),
                 r=[k_('x1')], w=[k_('junk'), k_('st0')])
            P.op('act', lambda: nc.scalar.activation(out=st[:, 0:1], in_=st[:, 0:1], func=AF.Sqrt, bias=EPS, scale=1.0 / D),
                 r=[k_('st0')], w=[k_('st0')])
            P.op('dve', lambda: nc.vector.reciprocal(out=st[:, 0:1], in_=st[:, 0:1]), r=[k_('st0')], w=[k_('st0')])
            P.op('dve', lambda: nc.vector.tensor_scalar(out=xn[:, :], in0=x1[:, :], scalar1=st[:, 0:1], scalar2=None, op0=ALU.mult),
                 r=[k_('x1'), k_('st0')], w=[k_('xn')])
            for k in range(8):
                P.op('pe', lambda: nc.tensor.transpose(out=p_tb[:, k, :], in_=xn[:, k * 128:(k + 1) * 128], identity=identb[:, :]),
                     r=[k_('xn'), 'identb'], w=[k_('ptb')], inc=(k == 7))
            P.op('act', lambda: nc.scalar.copy(out=xnT[:, :, :], in_=p_tb[:, :, :]), r=[k_('ptb')], w=[k_('xnT')])
            for (pp, c0, key) in ((p_xl, 0, 'pxl'), (p_gl, 512, 'pgl')):
                for j in range(4):
                    for k in range(8):
                        P.op('pe', lambda: nc.tensor.matmul(pp[:, j, :], lhsT=W['win'][:, k, c0 + j * 128:c0 + (j + 1) * 128],
                                                            rhs=xnT[:, k, :], start=(k == 0), stop=(k == 7)),
                             r=['Wsmall', k_('xnT')], w=[k_(key)], inc=(j == 3 and k == 7))
            for k in range(8):
                P.op('pe', lambda: nc.tensor.matmul(p_zc[:, 0:416], lhsT=xnT[:, k, :], rhs=W['win'][:, k, 1024:1440],
                                                    start=(k == 0), stop=(k == 7)),
                     r=['Wsmall', k_('xnT')], w=[k_('pzc')], inc=(k == 7))
            if sample:
                xv = xlx[:, :, :].rearrange("p j (b t) -> p j b t", t=11)
                P.op('act', lambda: nc.scalar.copy(out=xv[:, :, 0:NB, 3:11], in_=p_xl[:, :, :].rearrange("p j (b t) -> p j b t", t=8)),
                     r=[k_('pxl')], w=[k_('xlx')])
                sh = lambda j, k: xv[:, j, 0:NB, k:k + 8]
                ov = lambda t, j: t[:, j, :].rearrange("p (b t) -> p b t", t=8)
            else:
                P.op('act', lambda: nc.scalar.copy(out=xlx[:, :, 3:131], in_=p_xl[:, :, :]), r=[k_('pxl')], w=[k_('xlx')])
                sh = lambda j, k: xlx[:, j, k:k + 128]
                ov = lambda t, j: t[:, j, :]
            for j in range(4):
                P.op('dve', lambda: nc.vector.tensor_scalar(out=ov(xc, j), in0=sh(j, 0), scalar1=chv[:, j * 4:j * 4 + 1],
                                                            scalar2=chv[:, 16 + j:17 + j], op0=ALU.mult, op1=ALU.add),
                     r=[k_('xlx'), 'Wsmall'], w=[k_('xc')])
                for k in range(1, 4):
                    P.op('dve', lambda: nc.vector.scalar_tensor_tensor(out=ov(xc, j), in0=sh(j, k), scalar=chv[:, j * 4 + k:j * 4 + k + 1],
                                                                       in1=ov(xc, j), op0=ALU.mult, op1=ALU.add),
                         r=[k_('xlx'), 'Wsmall', k_('xc')], w=[k_('xc')])
            if not sample:
                P.op('dve', lambda: nc.vector.tensor_copy(out=xlx[:, :, 0:3], in_=xlx[:, :, 128:131]), r=[k_('xlx')], w=[k_('xlx')])
            P.op('dve', lambda: nc.vector.tensor_copy(out=xcb[:, :, :], in_=xc[:, :, :]), r=[k_('xc')], w=[k_('xcb')])
            for (pp, gi, key) in ((p_ga, 0, 'pga'), (p_gi, 1, 'pgi')):
                for j in range(4):
                    P.op('pe', lambda: nc.tensor.matmul(pp[:, j, :], lhsT=W['bd'][:, gi, j, :], rhs=xcb[:, j, :], start=True, stop=True),
                         r=['Wsmall', k_('xcb')], w=[k_(key)], inc=(j == 3))
            for j in range(4):
                P.op('act', lambda: nc.scalar.activation(out=rr[:, j, :], in_=p_ga[:, j, :], func=AF.Sigmoid, bias=chv[:, 20 + j:21 + j], scale=1.0),
                     r=[k_('pga'), 'Wsmall'], w=[k_('rr')])
                P.op('act', lambda: nc.scalar.activation(out=ii[:, j, :], in_=p_gi[:, j, :], func=AF.Sigmoid, bias=chv[:, 24 + j:25 + j], scale=1.0),
                     r=[k_('pgi'), 'Wsmall'], w=[k_('ii')])
            for j in range(4):
                P.op('act', lambda: nc.scalar.activation(out=aa[:, j, :], in_=rr[:, j, :], func=AF.Exp, scale=cch[:, j:j + 1]),
                     r=[k_('rr'), 'Wsmall'], w=[k_('aa')])
            P.op('act', lambda: nc.scalar.activation(out=mm[:, :, :], in_=aa[:, :, :], func=AF.Square), r=[k_('aa')], w=[k_('mm')])
            P.op('act', lambda: nc.scalar.activation(out=mm[:, :, :], in_=mm[:, :, :], func=AF.Sqrt, bias=1.0, scale=-1.0),
                 r=[k_('mm')], w=[k_('mm')])
            P.op('dve', lambda: nc.vector.tensor_tensor(out=uu[:, :, :], in0=ii[:, :, :], in1=xc[:, :, :], op=ALU.mult),
                 r=[k_('ii'), k_('xc')], w=[k_('uu')])
            P.op('dve', lambda: nc.vector.tensor_tensor(out=uu[:, :, :], in0=uu[:, :, :], in1=mm[:, :, :], op=ALU.mult),
                 r=[k_('uu'), k_('mm')], w=[k_('uu')])
            if sample:
                av = aa[:, :, :].rearrange("p j (b t) -> p j b t", t=8)
                uv = uu[:, :, :].rearrange("p j (b t) -> p j b t", t=8)
                hv = hs[:, :, :].rearrange("p j (b t) -> p j b t", t=8)
                for t in range(8):
                    prev = hp[:, :, 0:NB] if t == 0 else hv[:, :, 0:NB, t - 1]
                    P.op('dve', lambda: nc.vector.tensor_tensor(out=hv[:, :, 0:NB, t], in0=av[:, :, 0:NB, t], in1=prev, op=ALU.mult),
                         r=[k_('aa'), k_('hp'), k_('hs')], w=[k_('hs')])
                    P.op('dve', lambda: nc.vector.tensor_tensor(out=hv[:, :, 0:NB, t], in0=hv[:, :, 0:NB, t], in1=uv[:, :, 0:NB, t], op=ALU.add),
                         r=[k_('uu'), k_('hs')], w=[k_('hs')])
            else:
                for j in range(4):
                    P.op('dve', lambda: nc.vector.tensor_tensor_scan(out=hs[:, j, :], data0=aa[:, j, :], data1=uu[:, j, :],
                                                                    initial=hp[:, j, 0:1], op0=ALU.mult, op1=ALU.add),
                         r=[k_('aa'), k_('uu'), k_('hp')], w=[k_('hs')])
                P.op('dve', lambda: nc.vector.tensor_copy(out=hp[:, :, 0:1], in_=hs[:, :, 127:128]), r=[k_('hs')], w=[k_('hp')])
            P.op('act', lambda: nc.scalar.activation(out=g1[:, :, :], in_=p_gl[:, :, :], func=AF.Square), r=[k_('pgl')], w=[k_('g1')])
            P.op('dve', lambda: nc.vector.tensor_scalar(out=g1[:, :, :], in0=g1[:, :, :], scalar1=0.044715, scalar2=1.0, op0=ALU.mult, op1=ALU.add),
                 r=[k_('g1')], w=[k_('g1')])
            P.op('dve', lambda: nc.vector.tensor_tensor(out=g1[:, :, :], in0=g1[:, :, :], in1=p_gl[:, :, :], op=ALU.mult),
                 r=[k_('g1'), k_('pgl')], w=[k_('g1')])
            P.op('act', lambda: nc.scalar.activation(out=g2[:, :, :], in_=g1[:, :, :], func=AF.Sigmoid, scale=1.5957691216057308),
                 r=[k_('g1')], w=[k_('g2')])
            P.op('dve', lambda: nc.vector.tensor_tensor(out=g2[:, :, :], in0=g2[:, :, :], in1=p_gl[:, :, :], op=ALU.mult),
                 r=[k_('g2'), k_('pgl')], w=[k_('g2')])
            P.op('dve', lambda: nc.vector.tensor_tensor(out=g2[:, :, :], in0=g2[:, :, :], in1=hs[:, :, :], op=ALU.mult),
                 r=[k_('g2'), k_('hs')], w=[k_('g2')])
            for j in range(4):
                P.op('pe', lambda: nc.tensor.transpose(out=p_tf[:, j, :], in_=g2[:, j, :], identity=identf[:, :]),
                     r=[k_('g2'), 'identf'], w=[k_('pxl')], inc=(j == 3))
            P.op('act', lambda: nc.scalar.copy(out=yl[:, :], in_=p_tf[:, :, :].rearrange("p j t -> p (j t)")), r=[k_('pxl')], w=[k_('yl')])
            P.dma('pool', lambda: nc.gpsimd.dma_start(out=S['YL'][r0:r0 + nt, :], in_=yl[0:nt, :]), r=[k_('yl')], w=[('YL', 's' if sample else blk)])
            P.op('act', lambda: nc.scalar.activation(out=junk[:, 0:QL], in_=p_zc[:, 0:QL], func=AF.Square, accum_out=st[:, 1:2]),
                 r=[k_('pzc')], w=[k_('junk'), k_('st1')])
            P.op('act', lambda: nc.scalar.activation(out=junk[:, 0:KVL], in_=p_zc[:, QL:QL + KVL], func=AF.Square, accum_out=st[:, 2:3]),
                 r=[k_('pzc')], w=[k_('junk'), k_('st1')])
            P.op('act', lambda: nc.scalar.activation(out=st[:, 1:2], in_=st[:, 1:2], func=AF.Sqrt, bias=EPS, scale=1.0 / QL), r=[k_('st1')], w=[k_('st1')])
            P.op('act', lambda: nc.scalar.activation(out=st[:, 2:3], in_=st[:, 2:3], func=AF.Sqrt, bias=EPS, scale=1.0 / KVL), r=[k_('st1')], w=[k_('st1')])
            P.op('dve', lambda: nc.vector.reciprocal(out=st[:, 1:3], in_=st[:, 1:3]), r=[k_('st1')], w=[k_('st1')])
            P.op('dve', lambda: nc.vector.tensor_scalar(out=cqn[:, :], in0=p_zc[:, 0:QL], scalar1=st[:, 1:2], scalar2=None, op0=ALU.mult),
                 r=[k_('pzc'), k_('st1')], w=[k_('cqn')])
            for k in range(2):
                P.op('pe', lambda: nc.tensor.transpose(out=p_tb[:, k, :], in_=cqn[:, k * 128:(k + 1) * 128], identity=identb[:, :]),
                     r=[k_('cqn'), 'identb'], w=[k_('ptb')], inc=(k == 1))
            P.op('act', lambda: nc.scalar.copy(out=cqT[:, :, :], in_=p_tb[:, 0:2, :]), r=[k_('ptb')], w=[k_('cqT')])
            for (c0, c1, hb) in ((0, 512, 0), (512, 768, 1)):
                for k in range(2):
                    P.op('pe', lambda: nc.tensor.matmul(p_q[:, hb, 0:c1 - c0], lhsT=cqT[:, k, :], rhs=W['wq'][:, k, c0:c1], start=(k == 0), stop=(k == 1)),
                         r=['Wsmall', k_('cqT')], w=[k_('pq')], inc=(k == 1 and hb == 1))
            qfl = qf[:, :, :].rearrange("p h e -> p (h e)")
            P.op('act', lambda: nc.scalar.copy(out=qfl[:, 0:512], in_=p_q[:, 0, :]), r=[k_('pq')], w=[k_('qf')])
            P.op('act', lambda: nc.scalar.copy(out=qfl[:, 512:768], in_=p_q[:, 1, 0:256]), r=[k_('pq')], w=[k_('qf')])
            P.op('dve', lambda: nc.vector.tensor_copy(out=qb[:, :, 0:64], in_=qf[:, :, 0:64]), r=[k_('qf')], w=[k_('qb')])
            cosb = cs[:, 0, :].unsqueeze(1).to_broadcast([128, 8, 16])
            sinb = cs[:, 1, :].unsqueeze(1).to_broadcast([128, 8, 16])
            q1, q2 = qf[:, :, 64:80], qf[:, :, 80:96]
            for (ti, a_, b_) in ((0, q1, cosb), (1, q2, sinb), (2, q2, cosb), (3, q1, sinb)):
                P.op('dve', lambda: nc.vector.tensor_tensor(out=rt[:, ti, :, :], in0=a_, in1=b_, op=ALU.mult),
                     r=[k_('qf'), k_('cs')], w=[k_('rt')])
            P.op('dve', lambda: nc.vector.tensor_tensor(out=qb[:, :, 64:80], in0=rt[:, 0, :, :], in1=rt[:, 1, :, :], op=ALU.subtract),
                 r=[k_('rt')], w=[k_('qb')])
            P.op('dve', lambda: nc.vector.tensor_tensor(out=qb[:, :, 80:96], in0=rt[:, 2, :, :], in1=rt[:, 3, :, :], op=ALU.add),
                 r=[k_('rt')], w=[k_('qb')])
            P.dma('pool', lambda: nc.gpsimd.dma_start(out=S['QS'][r0:r0 + nt, :], in_=qb[0:nt, :, :].rearrange("p h e -> p (h e)")),
                  r=[k_('qb')], w=[('QS', 's' if sample else blk)])
            P.op('dve', lambda: nc.vector.scalar_tensor_tensor(out=ckv[:, :], in0=p_zc[:, QL:QL + KVL], scalar=st[:, 2:3], in1=W['kvg'][:, :],
                                                               op0=ALU.mult, op1=ALU.mult),
                 r=[k_('pzc'), k_('st1'), 'Wsmall'], w=[k_('ckv')])
            P.dma('pool', lambda: nc.gpsimd.dma_start(out=(O['kvs'] if sample else O['kvp'][r0:r0 + nt, :]), in_=ckv[0:nt, :]), r=[k_('ckv')], w=[('okv', tg, blk)])
            P.op('dve', lambda: nc.vector.tensor_copy(out=ckb[:, :], in_=ckv[:, :]), r=[k_('ckv')], w=[k_('ckb')])
            k1, k2 = p_zc[:, 384:400], p_zc[:, 400:416]
            for (ti, a_, b_) in ((0, k1, cs[:, 0, :]), (1, k2, cs[:, 1, :]), (2, k2, cs[:, 0, :]), (3, k1, cs[:, 1, :])):
                P.op('dve', lambda: nc.vector.tensor_tensor(out=krt[:, ti, :], in0=a_, in1=b_, op=ALU.mult),
                     r=[k_('pzc'), k_('cs')], w=[k_('krt')])
            P.op('dve', lambda: nc.vector.tensor_tensor(out=krf[:, 0:16], in0=krt[:, 0, :], in1=krt[:, 1, :], op=ALU.subtract), r=[k_('krt')], w=[k_('krf')])
            P.op('dve', lambda: nc.vector.tensor_tensor(out=krf[:, 16:32], in0=krt[:, 2, :], in1=krt[:, 3, :], op=ALU.add), r=[k_('krt')], w=[k_('krf')])
            P.dma('pool', lambda: nc.gpsimd.dma_start(out=(O['krs'] if sample else O['krp'][r0:r0 + nt, :]), in_=krf[0:nt, :]), r=[k_('krf')], w=[('okr', tg, blk)])
            P.op('dve', lambda: nc.vector.tensor_copy(out=kq[:, 64:96], in_=krf[:, :]), r=[k_('krf'), k_('kq')], w=[k_('kq')])
            P.op('pe', lambda: nc.tensor.transpose(out=p_tb[:, 0, :], in_=ckb[:, :], identity=identb[:, :]), r=[k_('ckb'), 'identb'], w=[k_('ptb')], inc=False)
            P.op('pe', lambda: nc.tensor.transpose(out=p_tb[0:96, 1, :], in_=kq[:, :], identity=identb[:, :]), r=[k_('kq'), 'identb'], w=[k_('ptb')])
            if sample:
                P.op('act', lambda: nc.scalar.copy(out=St['ckvT_s'][:, :], in_=p_tb[:, 0, :]), r=[k_('ptb')], w=['ckvT_s'])
                P.op('act', lambda: nc.scalar.copy(out=St['krT_s'][64:96, :], in_=p_tb[64:96, 1, :]), r=[k_('ptb')], w=['krT_s'])
                P.dma('pool', lambda: nc.gpsimd.dma_start(out=S['CKS'], in_=ckb[0:nt, :]), r=[k_('ckb')], w=['CKS'])
            else:
                P.op('act', lambda: nc.scalar.copy(out=St['ckvT'][:, blk * 128:(blk + 1) * 128], in_=p_tb[:, 0, :]), r=[k_('ptb')], w=[('ckvT', blk)])
                for kb in range(2):
                    P.op('act', lambda: nc.scalar.copy(out=St['KhT'][kb][64:96, blk * 128:(blk + 1) * 128], in_=p_tb[64:96, 1, :]),
                         r=[k_('ptb')], w=[('KhT', kb, blk)])
        if sample:
            for j in range(4):
                P.op('pe', lambda: nc.tensor.transpose(out=p_ga[0:NB, j, :], in_=hs[:, j, :].rearrange("p (b t) -> p b t", t=8)[:, 0:NB, 7],
                                                       identity=identf[:, :]), r=[k_('hs'), 'identf'], w=[k_('pga')], inc=(j == 3))
            P.op('act', lambda: nc.scalar.copy(out=yl[0:NB, :], in_=p_ga[0:NB, :, :].rearrange("p j t -> p (j t)")), r=[k_('pga')], w=[k_('yl')])
            P.dma('pool', lambda: nc.gpsimd.dma_start(out=O['hs_'], in_=yl[0:NB, :]), r=[k_('yl')], w=['o_hs'])
            for k in range(3):
                for j in range(4):
                    P.op('pe', lambda: nc.tensor.transpose(out=p_gi[0:NB, j, :], in_=xlx[:, j, :].rearrange("p (b t) -> p b t", t=11)[:, 0:NB, 8 + k],
                                                           identity=identf[:, :]), r=[k_('xlx'), 'identf'], w=[k_('pgi')], inc=(j == 3))
                P.op('act', lambda: nc.scalar.copy(out=junk[0:NB, k * DL:(k + 1) * DL], in_=p_gi[0:NB, :, :].rearrange("p j t -> p (j t)")),
                     r=[k_('pgi')], w=[k_('junk')])
            P.dma('pool', lambda: nc.gpsimd.dma_start(out=O['cvs'].rearrange("b k d -> b (k d)"), in_=junk[0:NB, 0:3 * DL]), r=[k_('junk')], w=['o_cvs'])
        else:
            for j in range(4):
                P.op('pe', lambda: nc.tensor.transpose(out=p_ga[0:1, j, :], in_=hs[:, j, 127:128], identity=identf[:, :]),
                     r=[k_('hs'), 'identf'], w=[k_('pga')], inc=(j == 3))
            P.op('act', lambda: nc.scalar.copy(out=yl[0:1, :], in_=p_ga[0:1, :, :].rearrange("p j t -> p (j t)")), r=[k_('pga')], w=[k_('yl')])
            P.dma('pool', lambda: nc.gpsimd.dma_start(out=O['hp'], in_=yl[0:1, :]), r=[k_('yl')], w=['o_hp'])
            for j in range(4):
                P.op('pe', lambda: nc.tensor.transpose(out=p_gi[0:3, j, :], in_=xlx[:, j, 0:3], identity=identf[:, :]),
                     r=[k_('xlx'), 'identf'], w=[k_('pgi')], inc=(j == 3))
            P.op('act', lambda: nc.scalar.copy(out=junk[0:3, 0:DL], in_=p_gi[0:3, :, :].rearrange("p j t -> p (j t)")), r=[k_('pgi')], w=[k_('junk')])
            P.dma('pool', lambda: nc.gpsimd.dma_start(out=O['cvp'], in_=junk[0:3, 0:DL]), r=[k_('junk')], w=['o_cvp'])
    P.barrier()


def emit_outproj(P, nc, es, W, tg, yl_f, ym_f, x1g, x2_dst, x2_keys, r_keys):
    k_ = lambda n: (n, 'op' + tg)
    T = W['op_tiles']
    junk, st, yc, ycT, p_tb, p_o = T['junk'], T['st'], T['yc'], T['ycT'], T['p_tb'], T['p_o']
    for (i, src) in ((0, yl_f), (1, ym_f)):
        P.op('act', lambda: nc.scalar.activation(out=junk[:, :], in_=src, func=AF.Square, accum_out=st[:, i:i + 1]),
             r=r_keys, w=[k_('junk'), k_('st')])
    P.op('act', lambda: nc.scalar.activation(out=st[:, 0:2], in_=st[:, 0:2], func=AF.Sqrt, bias=EPS, scale=1.0 / DL), r=[k_('st')], w=[k_('st')])
    P.op('dve', lambda: nc.vector.reciprocal(out=st[:, 0:2], in_=st[:, 0:2]), r=[k_('st')], w=[k_('st')])
    for (i, src) in ((0, yl_f), (1, ym_f)):
        P.op('dve', lambda: nc.vector.tensor_scalar(out=yc[:, i * DL:(i + 1) * DL], in0=src, scalar1=st[:, i:i + 1], scalar2=None, op0=ALU.mult),
             r=r_keys + [k_('st')], w=[k_('yc')])
    for k in range(8):
        P.op('pe', lambda: nc.tensor.transpose(out=p_tb[:, k, :], in_=yc[:, k * 128:(k + 1) * 128], identity=W['identb'][:, :]),
             r=[k_('yc'), 'identb'], w=[k_('ptb')], inc=(k == 7))
    P.op('act', lambda: nc.scalar.copy(out=ycT[:, :, :], in_=p_tb[:, :, :]), r=[k_('ptb')], w=[k_('ycT')])
    for half in range(2):
        for k in range(8):
            P.op('pe', lambda: nc.tensor.matmul(p_o[:, half, :], lhsT=ycT[:, k, :], rhs=W['wout'][:, k, half * 512:(half + 1) * 512],
                                                start=(k == 0), stop=(k == 7)),
                 r=[k_('ycT'), 'Wsmall'], w=[k_('po')], inc=(half == 1 and k == 7))
    P.op('dve', lambda: nc.vector.tensor_tensor(out=x1g, in0=x1g, in1=p_o[:, :, :].rearrange("p a b -> p (a b)"), op=ALU.add),
         r=r_keys + [k_('po')], w=r_keys[-1:])
    P.dma('pool', lambda: nc.gpsimd.dma_start(out=x2_dst, in_=x1g), r=r_keys[-1:], w=x2_keys)


def emit_prompt_attn(P, nc, W, St, I, S, cfg):
    SEQ = cfg['SEQ']
    NBLK = SEQ // 128
    NSLOT = NBLK // 4
    scale = 96.0 ** -0.5
    with ExitStack() as es:
        oidx = _tile(nc, es, "a_oidx", [128, NSLOT], I32)
        mk = _tile(nc, es, "a_mk", [128, 4, 128], BF16)
        mkf = _tile(nc, es, "a_mkf", [128, 4, 128], F32)
        qg = _tile(nc, es, "a_qg", [128, NSLOT, 768], BF16)
        ym = _tile(nc, es, "a_ym", [128, NSLOT, DL], BF16)
        Vh = [_tile(nc, es, f"a_Vh{i}", [128, NBLK, 65], BF16) for i in range(2)]
        qT = [_tile(nc, es, f"a_qT{i}", [96, 128], BF16) for i in range(2)]
        pT = [_tile(nc, es, f"a_pT{i}", [128, 4, 128], BF16) for i in range(2)]
        rc = _tile(nc, es, "a_rc", [128, 2], F32)
        ylg = _tile(nc, es, "a_ylg", [128, DL], F32)
        x1g = _tile(nc, es, "a_x1g", [128, D], F32)
        T = dict(junk=_tile(nc, es, "a_junk", [128, DL], F32), st=_tile(nc, es, "a_st", [128, 2], F32),
                 yc=_tile(nc, es, "a_yc", [128, D], BF16), ycT=_tile(nc, es, "a_ycT", [128, 8, 128], BF16),
                 p_tb=_psum(nc, es, "a_ptb", [128, 8, 128], BF16), p_o=_psum(nc, es, "a_po", [128, 2, 512], F32))
        W['op_tiles'] = T
        p_s = [_psum(nc, es, f"a_ps{i}", [128, 4, 128], F32) for i in range(2)]
        p_k = _psum(nc, es, "a_pk", [128, 512], F32)
        p_ov = [_psum(nc, es, f"a_pov{i}", [128, 512], F32) for i in range(2)]
        P.dma('sp', lambda: nc.sync.dma_start(out=oidx[:, :], in_=I['own_idx']), w=['oidx'])
        P.dma('sp', lambda: nc.sync.dma_start(out=mkf[:, :, :], in_=I['masks']), w=['mkf'])
        P.op('dve', lambda: nc.vector.tensor_copy(out=mk[:, :, :], in_=mkf[:, :, :]), r=['mkf'], w=['mk'])
        oix = [_tile(nc, es, f"a_oix{i}", [128, 1], I32) for i in range(NSLOT)]
        for s in range(NSLOT):
            P.op('dve', lambda: nc.vector.tensor_copy(out=oix[s][:, :], in_=oidx[:, s:s + 1]), r=['oidx'], w=[('oix', s)])
            P.dma('pool', lambda: nc.gpsimd.indirect_dma_start(
                out=qg[:, s, :], out_offset=None, in_=S['QS'], in_offset=bass.IndirectOffsetOnAxis(ap=oix[s][:, 0:1], axis=0)),
                r=[('oix', s)] + [('QS', b) for b in range(NBLK)], w=[('qg', s)])
        for i in range(2):
            P.op('dve', lambda: nc.vector.memset(Vh[i][:, :, 64:65], 1.0), w=[('Vh', i)])
        cnt = 0
        for h in range(NH):
            hb = h % 2
            for c in range(0, SEQ, 512):
                wdt = min(512, SEQ - c)
                P.op('pe', lambda: nc.tensor.matmul(p_k[0:64, 0:wdt], lhsT=W['wuk'][:, h, :], rhs=St['ckvT'][:, c:c + wdt], start=True, stop=True),
                     r=['Wsmall'] + [('ckvT', b) for b in range(c // 128, (c + wdt) // 128)], w=['pk'])
                P.op('act', lambda: nc.scalar.copy(out=St['KhT'][hb][0:64, c:c + wdt], in_=p_k[0:64, 0:wdt]), r=['pk'],
                     w=[('KhT', hb, b) for b in range(c // 128, (c + wdt) // 128)])
            for b0 in range(0, NBLK, 8):
                nb = min(8, NBLK - b0)
                pv = p_ov[0]
                for i in range(nb):
                    P.op('pe', lambda: nc.tensor.matmul(pv[:, i * 64:(i + 1) * 64], lhsT=St['ckvT'][:, (b0 + i) * 128:(b0 + i + 1) * 128],
                                                        rhs=W['wuv'][:, h, :], start=True, stop=True),
                         r=['Wsmall', ('ckvT', b0 + i)], w=['pov0'], inc=(i == nb - 1))
                P.op('dve', lambda: nc.vector.tensor_copy(out=Vh[hb][:, b0:b0 + nb, 0:64], in_=pv[:, 0:nb * 64].rearrange("p (b v) -> p b v", v=64)),
                     r=['pov0'], w=[('Vh', hb)])
            for s in range(NSLOT):
                qb_ = cnt % 2
                cnt += 1
                P.op('pe', lambda: nc.tensor.transpose(out=T['p_tb'][0:96, 0, :], in_=qg[:, s, h * 96:(h + 1) * 96], identity=W['identb'][:, :]),
                     r=[('qg', s), 'identb'], w=[('ptb', 'op')])
                P.op('act', lambda: nc.scalar.copy(out=qT[qb_][:, :], in_=T['p_tb'][0:96, 0, :]), r=[('ptb', 'op')], w=[('qT', qb_)])
                nkb = 4 * s + 4
                po = p_ov[1]
                for g in range(nkb // 4):
                    sb = (cnt + g) % 2
                    for i in range(4):
                        kb = 4 * g + i
                        P.op('pe', lambda: nc.tensor.matmul(p_s[sb][:, i, :], lhsT=St['KhT'][hb][0:96, kb * 128:(kb + 1) * 128], rhs=qT[qb_][:, :],
                                                            start=True, stop=True),
                             r=[('KhT', hb, kb), ('qT', qb_)], w=[('ps', sb)], inc=(i == 3))
                    P.op('act', lambda: nc.scalar.activation(out=pT[sb][:, :, :], in_=p_s[sb][:, :, :], func=AF.Exp, scale=scale),
                         r=[('ps', sb)], w=[('pT', sb)])
                    if g == nkb // 4 - 1:
                        P.op('dve', lambda: nc.vector.tensor_tensor(out=pT[sb][:, :, :], in0=pT[sb][:, :, :], in1=mk[:, :, :], op=ALU.mult),
                             r=[('pT', sb), 'mk'], w=[('pT', sb)])
                    for i in range(4):
                        kb = 4 * g + i
                        P.op('pe', lambda: nc.tensor.matmul(po[:, 0:65], lhsT=pT[sb][:, i, :], rhs=Vh[hb][:, kb, :], start=(kb == 0), stop=(kb == nkb - 1)),
                             r=[('pT', sb), ('Vh', hb)], w=['pov1'], inc=(kb == nkb - 1))
                P.op('dve', lambda: nc.vector.reciprocal(out=rc[:, 0:1], in_=po[:, 64:65]), r=['pov1'], w=['rc'])
                P.op('dve', lambda: nc.vector.tensor_scalar(out=ym[:, s, h * 64:(h + 1) * 64], in0=po[:, 0:64], scalar1=rc[:, 0:1], scalar2=None, op0=ALU.mult),
                     r=['pov1', 'rc'], w=[('ym', s)])
        for s in range(NSLOT):
            P.dma('pool', lambda: nc.gpsimd.indirect_dma_start(
                out=ylg[:, :], out_offset=None, in_=S['YL'], in_offset=bass.IndirectOffsetOnAxis(ap=oix[s][:, 0:1], axis=0)),
                r=[('oix', s)] + [('YL', b) for b in range(NBLK)], w=['ylg'])
            P.dma('pool', lambda: nc.gpsimd.indirect_dma_start(
                out=x1g[:, :], out_offset=None, in_=S['X1'], in_offset=bass.IndirectOffsetOnAxis(ap=oix[s][:, 0:1], axis=0)),
                r=[('oix', s)] + [('X1', b) for b in range(NBLK)], w=['x1g'])
            emit_outproj(P, nc, es, W, 'p', ylg[:, :], ym[:, s, :], x1g[:, :], S['X2'][s * 128:(s + 1) * 128, :], [('X2', s)],
                         [('ym', s), 'ylg', 'x1g'])
    P.barrier()


def emit_sample_attn(P, nc, W, St, I, S, cfg, O=None):
    SEQ, NB, NPG = cfg['SEQ'], cfg['NB'], cfg['NPG']
    NS = NB * 8
    NG = NPG // 8
    scale = 96.0 ** -0.5
    with ExitStack() as es:
        qs = _tile(nc, es, "s_qs", [128, 768], BF16)
        qn = _tile(nc, es, "s_qn", [64, 8, 128], BF16)
        QLT = _tile(nc, es, "s_QLT", [128, NB, 8, 8], BF16)
        QRT = _tile(nc, es, "s_QRT", [96, NB, 8, 8], BF16)
        ptr = _tile(nc, es, "s_ptr", [128, NB, NG], I32)
        ptf = _tile(nc, es, "s_ptf", [128, NB, NG], F32)
        qcol = _tile(nc, es, "s_qcol", [128, 1], F32)
        idx = _tile(nc, es, "s_idx", [128, NB, NG], I32)
        G = [_tile(nc, es, f"s_G{i}", [128, 8, 129], BF16) for i in range(2)]
        GR = [_tile(nc, es, f"s_GR{i}", [128, 8, 96], BF16) for i in range(2)]
        Gf = [_tile(nc, es, f"s_Gf{i}", [128, 8, 128], F32) for i in range(2)]
        six = [_tile(nc, es, f"s_six{i}", [128, 1], I32) for i in range(2)]
        GRf = [_tile(nc, es, f"s_GRf{i}", [128, 8, 32], F32) for i in range(2)]
        cT = [_tile(nc, es, f"s_cT{i}", [128, 8, 128], BF16) for i in range(2)]
        rT = [_tile(nc, es, f"s_rT{i}", [96, 8, 128], BF16) for i in range(2)]
        pT = [_tile(nc, es, f"s_pT{i}", [128, 8, 64], BF16) for i in range(2)]
        pTo = _tile(nc, es, "s_pTo", [8, 64], BF16)
        cks = _tile(nc, es, "s_cks", [8, NB, 129], BF16)
        mo = _tile(nc, es, "s_mo", [8, 64], F32)
        on = _tile(nc, es, "s_on", [64, 128], BF16)
        onT = _tile(nc, es, "s_onT", [128, 64], BF16)
        rc = _tile(nc, es, "s_rc", [64, 1], F32)
        ymb = _tile(nc, es, "s_ymb", [8, DL], F32)
        ymg = _tile(nc, es, "s_ymg", [128, DL], F32)
        ylg = _tile(nc, es, "s_ylg", [128, DL], F32)
        x1g = _tile(nc, es, "s_x1g", [128, D], F32)
        T = dict(junk=_tile(nc, es, "s_junk", [128, DL], F32), st=_tile(nc, es, "s_st", [128, 2], F32),
                 yc=_tile(nc, es, "s_yc", [128, D], BF16), ycT=_tile(nc, es, "s_ycT", [128, 8, 128], BF16),
                 p_tb=_psum(nc, es, "s_ptb", [128, 8, 128], BF16), p_o=_psum(nc, es, "s_po", [128, 2, 512], F32))
        W['op_tiles'] = T
        p_tb = T['p_tb']
        p_tr = _psum(nc, es, "s_ptr2", [128, 8, 128], BF16)
        p_s = [_psum(nc, es, f"s_ps{i}", [128, 8, 64], F32) for i in range(2)]
        p_ov = _psum(nc, es, "s_pov", [128, 512], F32)
        p_m = _psum(nc, es, "s_pm", [128, 512], F32)
        identb = W['identb']
        for pl in range(8):
            P.dma('sp', lambda: nc.sync.dma_start(
                out=ptr[pl * 16:(pl + 1) * 16, :, :],
                in_=I['pt_l'][pl:pl + 1, :, :].to_broadcast([16, NB, NG])), w=['s_ptr'])
        P.dma('sp', lambda: nc.sync.dma_start(out=qcol[:, :], in_=I['qcol']), w=['s_qcol'])
        P.dma('sp', lambda: nc.sync.dma_start(out=mo[:, :], in_=I['mask_own']), w=['s_mo'])
        P.op('dve', lambda: nc.vector.tensor_copy(out=ptf[:, :, :], in_=ptr[:, :, :]), r=['s_ptr'], w=['s_ptf'])
        P.op('dve', lambda: nc.vector.tensor_scalar(out=ptf[:, :, :], in0=ptf[:, :, :], scalar1=16.0, scalar2=qcol[:, 0:1], op0=ALU.mult, op1=ALU.add),
             r=['s_ptf', 's_qcol'], w=['s_ptf'])
        P.op('dve', lambda: nc.vector.tensor_copy(out=idx[:, :, :], in_=ptf[:, :, :]), r=['s_ptf'], w=['s_idx'])
        dbg = cfg.get('debug')
        if dbg:
            P.dma('sp', lambda: nc.sync.dma_start(out=O['dbg_idx'], in_=idx[:, :, :].rearrange("p b g -> p (b g)")), r=['s_idx'], w=['dbg_idx'])
            dps = _tile(nc, es, "s_dps", [128, 512], F32)
            dov = _tile(nc, es, "s_dov", [64, 129], F32)
        P.dma('sp', lambda: nc.sync.dma_start(out=qs[:, :], in_=S['QS'][SEQ:SEQ + NS, :]), r=[('QS', 's')], w=['s_qs'])
        P.dma('sp', lambda: nc.sync.dma_start(out=cks[:, :, 0:128], in_=S['CKS'].rearrange("(b t) c -> t b c", t=8)), r=['CKS'], w=['s_cks'])
        P.op('dve', lambda: nc.vector.memset(cks[:, :, 128:129], 1.0), w=['s_cks'])
        for i in range(2):
            P.op('dve', lambda: nc.vector.memset(G[i][:, :, 128:129], 1.0), w=[('s_G', i)])
            P.op('dve', lambda: nc.vector.memset(GR[i][:, :, :], 0.0), w=[('s_GR', i)])
        for h in range(8):
            P.op('pe', lambda: nc.tensor.transpose(out=p_tb[0:96, h, :], in_=qs[:, h * 96:(h + 1) * 96], identity=identb[:, :]),
                 r=['s_qs', 'identb'], w=[('ptb', 'ops')], inc=(h == 7))
        P.op('act', lambda: nc.scalar.copy(out=qn[:, :, :], in_=p_tb[0:64, :, :]), r=[('ptb', 'ops')], w=['s_qn'])
        P.op('act', lambda: nc.scalar.copy(out=QRT[64:96, :, :, :].rearrange("p b h t -> p h b t"),
                                           in_=p_tb[64:96, :, :].rearrange("p h (b t) -> p h b t", t=8)), r=[('ptb', 'ops')], w=['s_QRT'])
        for h in range(8):
            P.op('pe', lambda: nc.tensor.matmul(p_m[:, 0:128], lhsT=W['wukT'][:, h, :], rhs=qn[:, h, :], start=True, stop=True),
                 r=['Wsmall', 's_qn'], w=['s_pm'])
            P.op('act', lambda: nc.scalar.copy(out=QLT[:, :, h, :], in_=p_m[:, 0:128].rearrange("p (b t) -> p b t", t=8)), r=['s_pm'], w=['s_QLT'])
        n = 0
        for b in range(NB):
            qlt = QLT[:, b, :, :].rearrange("p h t -> p (h t)")
            qrt = QRT[64:96, b, :, :].rearrange("p h t -> p (h t)")
            first = True
            for g in range(NG):
                gb = n % 2
                n += 1
                P.op('dve', lambda: nc.vector.tensor_copy(out=six[gb][:, :], in_=idx[:, b, g:g + 1]), r=['s_idx'], w=[('s_six', gb)])
                P.dma('pool', lambda: nc.gpsimd.indirect_dma_start(
                    out=Gf[gb][:, :, :].rearrange("p a c -> p (a c)"), out_offset=None, in_=I['ckv_cache'].rearrange("(r g) c -> r (g c)", g=8),
                    in_offset=bass.IndirectOffsetOnAxis(ap=six[gb][:, 0:1], axis=0)), r=[('s_six', gb)], w=[('s_Gf', gb)])
                P.dma('pool', lambda: nc.gpsimd.indirect_dma_start(
                    out=GRf[gb][:, :, :].rearrange("p a c -> p (a c)"), out_offset=None, in_=I['ckr_cache'].rearrange("(r g) c -> r (g c)", g=8),
                    in_offset=bass.IndirectOffsetOnAxis(ap=six[gb][:, 0:1], axis=0)), r=[('s_six', gb)], w=[('s_GRf', gb)])
                if dbg and b == 0 and g == 0:
                    P.dma('sp', lambda: nc.sync.dma_start(out=O['dbg_G'], in_=Gf[gb][:, :, :].rearrange("p a c -> p (a c)")), r=[('s_Gf', gb)], w=['dbg_G'])
                    P.dma('sp', lambda: nc.sync.dma_start(out=O['dbg_GR'], in_=GRf[gb][:, :, :].rearrange("p a c -> p (a c)")), r=[('s_GRf', gb)], w=['dbg_GR'])
                P.op('dve', lambda: nc.vector.tensor_copy(out=G[gb][:, :, 0:128], in_=Gf[gb][:, :, :]), r=[('s_Gf', gb)], w=[('s_G', gb)])
                P.op('dve', lambda: nc.vector.tensor_copy(out=GR[gb][:, :, 64:96], in_=GRf[gb][:, :, :]), r=[('s_GRf', gb)], w=[('s_GR', gb)])
                for i in range(8):
                    P.op('pe', lambda: nc.tensor.transpose(out=p_tb[:, i, :], in_=G[gb][:, i, 0:128], identity=identb[:, :]),
                         r=[('s_G', gb), 'identb'], w=[('ptb', 'ops')], inc=(i == 7))
                P.op('act', lambda: nc.scalar.copy(out=cT[gb][:, :, :], in_=p_tb[:, :, :]), r=[('ptb', 'ops')], w=[('s_cT', gb)])
                for i in range(8):
                    P.op('pe', lambda: nc.tensor.transpose(out=p_tr[0:96, i, :], in_=GR[gb][:, i, :], identity=identb[:, :]),
                         r=[('s_GR', gb), 'identb'], w=['s_ptr2'], inc=(i == 7))
                P.op('dve', lambda: nc.vector.tensor_copy(out=rT[gb][64:96, :, :], in_=p_tr[64:96, :, :]), r=['s_ptr2'], w=[('s_rT', gb)])
                for i in range(8):
                    P.op('pe', lambda: nc.tensor.matmul(p_s[gb][:, i, :], lhsT=cT[gb][:, i, :], rhs=qlt, start=True, stop=False),
                         r=[('s_cT', gb), 's_QLT'], w=[('s_ps', gb)], inc=False)
                    P.op('pe', lambda: nc.tensor.matmul(p_s[gb][:, i, :], lhsT=rT[gb][64:96, i, :], rhs=qrt, start=False, stop=True),
                         r=[('s_rT', gb), 's_QRT'], w=[('s_ps', gb)], inc=(i == 7))
                if dbg and b == 0 and g == 0:
                    P.op('dve', lambda: nc.vector.tensor_copy(out=dps[:, :], in_=p_s[gb][:, :, :].rearrange("p a c -> p (a c)")), r=[('s_ps', gb)], w=['s_dps'])
                    P.dma('sp', lambda: nc.sync.dma_start(out=O['dbg_ps'], in_=dps[:, :]), r=['s_dps'], w=['dbg_ps'])
                P.op('act', lambda: nc.scalar.activation(out=pT[gb][:, :, :], in_=p_s[gb][:, :, :], func=AF.Exp, scale=scale),
                     r=[('s_ps', gb)], w=[('s_pT', gb)])
                for i in range(8):
                    P.op('pe', lambda: nc.tensor.matmul(p_ov[0:64, 0:129], lhsT=pT[gb][:, i, :], rhs=G[gb][:, i, :], start=first, stop=False),
                         r=[('s_pT', gb), ('s_G', gb)], w=['s_pov'], inc=False)
                    first = False
            P.op('pe', lambda: nc.tensor.matmul(p_m[0:8, 0:64], lhsT=St['ckvT_s'][:, b * 8:(b + 1) * 8], rhs=qlt, start=True, stop=False),
                 r=['ckvT_s', 's_QLT'], w=['s_pm'], inc=False)
            P.op('pe', lambda: nc.tensor.matmul(p_m[0:8, 0:64], lhsT=St['krT_s'][64:96, b * 8:(b + 1) * 8], rhs=qrt, start=False, stop=True),
                 r=['krT_s', 's_QRT'], w=['s_pm'])
            P.op('act', lambda: nc.scalar.activation(out=pTo[:, :], in_=p_m[0:8, 0:64], func=AF.Exp, scale=scale), r=['s_pm'], w=['s_pTo'])
            P.op('dve', lambda: nc.vector.tensor_tensor(out=pTo[:, :], in0=pTo[:, :], in1=mo[:, :], op=ALU.mult), r=['s_pTo', 's_mo'], w=['s_pTo'])
            P.op('pe', lambda: nc.tensor.matmul(p_ov[0:64, 0:129], lhsT=pTo[:, :], rhs=cks[:, b, :], start=False, stop=True),
                 r=['s_pTo', 's_cks'], w=['s_pov'])
            if dbg and b == 0:
                P.op('dve', lambda: nc.vector.tensor_copy(out=dov[:, :], in_=p_ov[0:64, 0:129]), r=['s_pov'], w=['s_dov'])
                P.dma('sp', lambda: nc.sync.dma_start(out=O['dbg_ov'], in_=dov[:, :]), r=['s_dov'], w=['dbg_ov'])
            P.op('dve', lambda: nc.vector.reciprocal(out=rc[:, :], in_=p_ov[0:64, 128:129]), r=['s_pov'], w=['s_rc'])
            P.op('dve', lambda: nc.vector.tensor_scalar(out=on[:, :], in0=p_ov[0:64, 0:128], scalar1=rc[:, 0:1], scalar2=None, op0=ALU.mult),
                 r=['s_pov', 's_rc'], w=['s_on'])
            P.op('pe', lambda: nc.tensor.transpose(out=p_tr[:, 0, 0:64], in_=on[:, :], identity=identb[0:64, 0:64]), r=['s_on', 'identb'], w=['s_ptr2'])
            P.op('act', lambda: nc.scalar.copy(out=onT[:, :], in_=p_tr[:, 0, 0:64]), r=['s_ptr2'], w=['s_onT'])
            for h in range(8):
                P.op('pe', lambda: nc.tensor.matmul(p_o_s(p_m, h), lhsT=onT[:, h * 8:(h + 1) * 8], rhs=W['wuv'][:, h, :], start=True, stop=True),
                     r=['s_onT', 'Wsmall'], w=['s_pm'], inc=(h == 7))
            P.op('act', lambda: nc.scalar.copy(out=ymb[:, :], in_=p_m[0:8, 0:512]), r=['s_pm'], w=['s_ymb'])
            P.dma('pool', lambda: nc.gpsimd.dma_start(out=S['YM'][b * 8:(b + 1) * 8, :], in_=ymb[:, :]), r=['s_ymb'], w=['YM'])
        P.dma('sp', lambda: nc.sync.dma_start(out=ymg[:, :], in_=S['YM']), r=['YM'], w=['s_ymg'])
        if dbg:
            P.dma('sp', lambda: nc.sync.dma_start(out=O['dbg_YM'], in_=ymg[:, :]), r=['s_ymg'], w=['dbg_YM'])
        P.dma('sp', lambda: nc.sync.dma_start(out=ylg[:, :], in_=S['YL'][SEQ:SEQ + NS, :]), r=[('YL', 's')], w=['s_ylg'])
        P.dma('sp', lambda: nc.sync.dma_start(out=x1g[:, :], in_=S['X1'][SEQ:SEQ + NS, :]), r=[('X1', 's')], w=['s_x1g'])
        NOWN = SEQ // 4
        emit_outproj(P, nc, es, W, 's', ylg[:, :], ymg[:, :], x1g[:, :], S['X2'][NOWN:NOWN + NS, :], [('X2', 's')], ['s_ymg', 's_ylg', 's_x1g'])
    P.barrier()


def p_o_s(p_m, h):
    return p_m[0:8, h * 64:(h + 1) * 64]


def emit_mixer_prompt(P, nc, W, St, I, O, S, cfg):
    SEQ = cfg['SEQ']
    nblk = SEQ // 128
    tg = 'p'
    with ExitStack() as es:
        def two(name, shape, dt):
            return [_tile(nc, es, f"mp_{name}{i}", shape, dt) for i in range(2)]
        x1 = two('x1', [128, D], F32)
        junk = _tile(nc, es, "mp_junk", [128, D], F32)
        xn = two('xn', [128, D], BF16)
        xnT = two('xnT', [128, 8, 128], BF16)
        st = two('st', [128, 4], F32)
        xlx = two('xlx', [128, 4, 131], F32)
        gl = two('gl', [128, 4, 128], F32)
        zc = two('zc', [128, 416], F32)
        xc = two('xc', [128, 4, 128], F32)
        xcb = two('xcb', [128, 4, 128], BF16)
        rr = two('rr', [128, 4, 128], F32)
        ii = two('ii', [128, 4, 128], F32)
        aa = two('aa', [128, 4, 128], F32)
        uu = two('uu', [128, 4, 128], F32)
        hs = two('hs', [128, 4, 128], F32)
        hp = _tile(nc, es, "mp_hp", [128, 4, 1], F32)
        halo = _tile(nc, es, "mp_halo", [128, 4, 3], F32)
        g1 = two('g1', [128, 4, 128], F32)
        yl = two('yl', [128, DL], F32)
        cs = two('cs', [128, 2, 16], F32)
        cqn = two('cqn', [128, QL], BF16)
        cqT = two('cqT', [128, 2, 128], BF16)
        qf = two('qf', [128, 8, 96], F32)
        qb = two('qb', [128, 8, 96], BF16)
        rt = two('rt', [128, 4, 8, 16], F32)
        ckv = two('ckv', [128, KVL], F32)
        ckb = two('ckb', [128, KVL], BF16)
        krf = two('krf', [128, RP], F32)
        krt = two('krt', [128, 4, 16], F32)
        kq = two('kq', [128, 96], BF16)
        p_xl = _psum(nc, es, "mp_pxl", [128, 4, 128], F32)
        p_gl = _psum(nc, es, "mp_pgl", [128, 4, 128], F32)
        p_ga = _psum(nc, es, "mp_pga", [128, 4, 128], F32)
        p_gi = _psum(nc, es, "mp_pgi", [128, 4, 128], F32)
        p_zc = _psum(nc, es, "mp_pzc", [128, 512], F32)
        p_q = _psum(nc, es, "mp_pq", [128, 2, 512], F32)
        p_tb = _psum(nc, es, "mp_ptb", [128, 8, 128], BF16)
        chv, cch = W['chv'], W['cch']
        identb, identf = W['identb'], W['identf']
        for i in range(2):
            P.op('dve', lambda: nc.vector.memset(kq[i][:, :], 0.0), w=[('kq', i)])
            P.op('dve', lambda: nc.vector.memset(xlx[i][:, :, :], 0.0), w=[('xlx', i)])
        P.op('dve', lambda: nc.vector.memset(hp[:, :, :], 0.0), w=['hp'])

        def stageA(blk):
            b = blk % 2
            k_ = lambda n: (n, b)
            r0 = blk * 128
            P.dma('sp', lambda: nc.sync.dma_start(out=x1[b][:, :], in_=S['X1'][r0:r0 + 128, :]), r=[('X1', blk)], w=[k_('x1')])
            P.dma('sp', lambda: nc.sync.dma_start(out=cs[b][:, 0, :], in_=I['cosp'][r0:r0 + 128, :]), w=[k_('cs')])
            P.dma('sp', lambda: nc.sync.dma_start(out=cs[b][:, 1, :], in_=I['sinp'][r0:r0 + 128, :]), w=[k_('cs')])
            P.op('act', lambda: nc.scalar.activation(out=junk[:, :], in_=x1[b][:, :], func=AF.Square, accum_out=st[b][:, 0:1]),
                 r=[k_('x1')], w=['junk', k_('st0')])
            P.op('act', lambda: nc.scalar.activation(out=st[b][:, 0:1], in_=st[b][:, 0:1], func=AF.Sqrt, bias=EPS, scale=1.0 / D),
                 r=[k_('st0')], w=[k_('st0')])
            P.op('dve', lambda: nc.vector.reciprocal(out=st[b][:, 0:1], in_=st[b][:, 0:1]), r=[k_('st0')], w=[k_('st0')])
            P.op('dve', lambda: nc.vector.tensor_scalar(out=xn[b][:, :], in0=x1[b][:, :], scalar1=st[b][:, 0:1], scalar2=None, op0=ALU.mult),
                 r=[k_('x1'), k_('st0')], w=[k_('xn')])
            for k in range(8):
                P.op('pe', lambda: nc.tensor.transpose(out=p_tb[:, k, :], in_=xn[b][:, k * 128:(k + 1) * 128], identity=identb[:, :]),
                     r=[k_('xn'), 'identb'], w=['ptb'], inc=(k == 7))
            P.op('act', lambda: nc.scalar.copy(out=xnT[b][:, :, :], in_=p_tb[:, :, :]), r=['ptb'], w=[k_('xnT')])
            for (pp, c0, key) in ((p_xl, 0, 'pxl'), (p_gl, 512, 'pgl')):
                for j in range(4):
                    for k in range(8):
                        P.op('pe', lambda: nc.tensor.matmul(pp[:, j, :], lhsT=W['win'][:, k, c0 + j * 128:c0 + (j + 1) * 128],
                                                            rhs=xnT[b][:, k, :], start=(k == 0), stop=(k == 7)),
                             r=['Wsmall', k_('xnT')], w=[key], inc=(j == 3 and k == 7))
            for k in range(8):
                P.op('pe', lambda: nc.tensor.matmul(p_zc[:, 0:416], lhsT=xnT[b][:, k, :], rhs=W['win'][:, k, 1024:1440],
                                                    start=(k == 0), stop=(k == 7)),
                     r=['Wsmall', k_('xnT')], w=['pzc'], inc=(k == 7))
            P.op('act', lambda: nc.scalar.copy(out=xlx[b][:, :, 3:131], in_=p_xl[:, :, :]), r=['pxl'], w=[k_('xlx')])
            P.op('dve', lambda: nc.vector.tensor_copy(out=gl[b][:, :, :], in_=p_gl[:, :, :]), r=['pgl'], w=[k_('gl')])
            P.op('act', lambda: nc.scalar.copy(out=zc[b][:, :], in_=p_zc[:, 0:416]), r=['pzc'], w=[k_('zc')])

        def stageB(blk):
            b = blk % 2
            k_ = lambda n: (n, b)
            r0 = blk * 128
            if blk > 0:
                P.op('dve', lambda: nc.vector.tensor_copy(out=xlx[b][:, :, 0:3], in_=halo[:, :, :]),
                     r=['halo', k_('xlx')], w=[k_('xlx')])
            P.op('dve', lambda: nc.vector.tensor_copy(out=halo[:, :, :], in_=xlx[b][:, :, 128:131]), r=[k_('xlx')], w=['halo'])
            for j in range(4):
                P.op('dve', lambda: nc.vector.tensor_scalar(out=xc[b][:, j, :], in0=xlx[b][:, j, 0:128], scalar1=chv[:, j * 4:j * 4 + 1],
                                                            scalar2=chv[:, 16 + j:17 + j], op0=ALU.mult, op1=ALU.add),
                     r=[k_('xlx'), 'Wsmall'], w=[k_('xc')])
                for k in range(1, 4):
                    P.op('dve', lambda: nc.vector.scalar_tensor_tensor(out=xc[b][:, j, :], in0=xlx[b][:, j, k:k + 128], scalar=chv[:, j * 4 + k:j * 4 + k + 1],
                                                                       in1=xc[b][:, j, :], op0=ALU.mult, op1=ALU.add),
                         r=[k_('xlx'), 'Wsmall', k_('xc')], w=[k_('xc')])
            P.op('dve', lambda: nc.vector.tensor_copy(out=xcb[b][:, :, :], in_=xc[b][:, :, :]), r=[k_('xc')], w=[k_('xcb')])
            for (pp, gi, key) in ((p_ga, 0, 'pga'), (p_gi, 1, 'pgi')):
                for j in range(4):
                    P.op('pe', lambda: nc.tensor.matmul(pp[:, j, :], lhsT=W['bd'][:, gi, j, :], rhs=xcb[b][:, j, :], start=True, stop=True),
                         r=['Wsmall', k_('xcb')], w=[key], inc=(j == 3))
            for j in range(4):
                P.op('act', lambda: nc.scalar.activation(out=rr[b][:, j, :], in_=p_ga[:, j, :], func=AF.Sigmoid, bias=chv[:, 20 + j:21 + j], scale=1.0),
                     r=['pga', 'Wsmall'], w=[k_('rr')])
                P.op('act', lambda: nc.scalar.activation(out=ii[b][:, j, :], in_=p_gi[:, j, :], func=AF.Sigmoid, bias=chv[:, 24 + j:25 + j], scale=1.0),
                     r=['pgi', 'Wsmall'], w=[k_('ii')])
            for j in range(4):
                P.op('act', lambda: nc.scalar.activation(out=aa[b][:, j, :], in_=rr[b][:, j, :], func=AF.Exp, scale=cch[:, j:j + 1]),
                     r=[k_('rr'), 'Wsmall'], w=[k_('aa')])
            P.op('act', lambda: nc.scalar.activation(out=rr[b][:, :, :], in_=aa[b][:, :, :], func=AF.Square), r=[k_('aa')], w=[k_('rr')])
            P.op('act', lambda: nc.scalar.activation(out=rr[b][:, :, :], in_=rr[b][:, :, :], func=AF.Sqrt, bias=1.0, scale=-1.0),
                 r=[k_('rr')], w=[k_('rr')])
            P.op('dve', lambda: nc.vector.tensor_tensor(out=uu[b][:, :, :], in0=ii[b][:, :, :], in1=xc[b][:, :, :], op=ALU.mult),
                 r=[k_('ii'), k_('xc')], w=[k_('uu')])
            P.op('dve', lambda: nc.vector.tensor_tensor(out=uu[b][:, :, :], in0=uu[b][:, :, :], in1=rr[b][:, :, :], op=ALU.mult),
                 r=[k_('uu'), k_('rr')], w=[k_('uu')])
            for j in range(4):
                P.op('dve', lambda: nc.vector.tensor_tensor_scan(out=hs[b][:, j, :], data0=aa[b][:, j, :], data1=uu[b][:, j, :],
                                                                initial=hp[:, j, 0:1], op0=ALU.mult, op1=ALU.add),
                     r=[k_('aa'), k_('uu'), 'hp'], w=[k_('hs')])
            P.op('dve', lambda: nc.vector.tensor_copy(out=hp[:, :, 0:1], in_=hs[b][:, :, 127:128]), r=[k_('hs')], w=['hp'])
            P.op('act', lambda: nc.scalar.activation(out=g1[b][:, :, :], in_=gl[b][:, :, :], func=AF.Square), r=[k_('gl')], w=[k_('g1')])
            P.op('dve', lambda: nc.vector.tensor_scalar(out=g1[b][:, :, :], in0=g1[b][:, :, :], scalar1=0.044715, scalar2=1.0, op0=ALU.mult, op1=ALU.add),
                 r=[k_('g1')], w=[k_('g1')])
            P.op('dve', lambda: nc.vector.tensor_tensor(out=g1[b][:, :, :], in0=g1[b][:, :, :], in1=gl[b][:, :, :], op=ALU.mult),
                 r=[k_('g1'), k_('gl')], w=[k_('g1')])
            P.op('act', lambda: nc.scalar.activation(out=g1[b][:, :, :], in_=g1[b][:, :, :], func=AF.Sigmoid, scale=1.5957691216057308),
                 r=[k_('g1')], w=[k_('g1')])
            P.op('dve', lambda: nc.vector.tensor_tensor(out=g1[b][:, :, :], in0=g1[b][:, :, :], in1=gl[b][:, :, :], op=ALU.mult),
                 r=[k_('g1'), k_('gl')], w=[k_('g1')])
            P.op('dve', lambda: nc.vector.tensor_tensor(out=g1[b][:, :, :], in0=g1[b][:, :, :], in1=hs[b][:, :, :], op=ALU.mult),
                 r=[k_('g1'), k_('hs')], w=[k_('g1')])
            for j in range(4):
                P.op('pe', lambda: nc.tensor.transpose(out=p_ga[:, j, :], in_=g1[b][:, j, :], identity=identf[:, :]),
                     r=[k_('g1'), 'identf'], w=['pga'], inc=(j == 3))
            P.op('act', lambda: nc.scalar.copy(out=yl[b][:, :], in_=p_ga[:, :, :].rearrange("p j t -> p (j t)")), r=['pga'], w=[k_('yl')])
            P.dma('pool', lambda: nc.gpsimd.dma_start(out=S['YL'][r0:r0 + 128, :], in_=yl[b][:, :]), r=[k_('yl')], w=[('YL', blk)])
            P.op('act', lambda: nc.scalar.activation(out=junk[:, 0:QL], in_=zc[b][:, 0:QL], func=AF.Square, accum_out=st[b][:, 1:2]),
                 r=[k_('zc')], w=['junk', k_('st1')])
            P.op('act', lambda: nc.scalar.activation(out=junk[:, 0:KVL], in_=zc[b][:, QL:QL + KVL], func=AF.Square, accum_out=st[b][:, 2:3]),
                 r=[k_('zc')], w=['junk', k_('st1')])
            P.op('act', lambda: nc.scalar.activation(out=st[b][:, 1:2], in_=st[b][:, 1:2], func=AF.Sqrt, bias=EPS, scale=1.0 / QL), r=[k_('st1')], w=[k_('st1')])
            P.op('act', lambda: nc.scalar.activation(out=st[b][:, 2:3], in_=st[b][:, 2:3], func=AF.Sqrt, bias=EPS, scale=1.0 / KVL), r=[k_('st1')], w=[k_('st1')])
            P.op('dve', lambda: nc.vector.reciprocal(out=st[b][:, 1:3], in_=st[b][:, 1:3]), r=[k_('st1')], w=[k_('st1')])
            P.op('dve', lambda: nc.vector.tensor_scalar(out=cqn[b][:, :], in0=zc[b][:, 0:QL], scalar1=st[b][:, 1:2], scalar2=None, op0=ALU.mult),
                 r=[k_('zc'), k_('st1')], w=[k_('cqn')])
            for k in range(2):
                P.op('pe', lambda: nc.tensor.transpose(out=p_tb[:, k, :], in_=cqn[b][:, k * 128:(k + 1) * 128], identity=identb[:, :]),
                     r=[k_('cqn'), 'identb'], w=['ptb'], inc=(k == 1))
            P.op('act', lambda: nc.scalar.copy(out=cqT[b][:, :, :], in_=p_tb[:, 0:2, :]), r=['ptb'], w=[k_('cqT')])
            for (c0, c1, hb) in ((0, 512, 0), (512, 768, 1)):
                for k in range(2):
                    P.op('pe', lambda: nc.tensor.matmul(p_q[:, hb, 0:c1 - c0], lhsT=cqT[b][:, k, :], rhs=W['wq'][:, k, c0:c1], start=(k == 0), stop=(k == 1)),
                         r=['Wsmall', k_('cqT')], w=['pq'], inc=(k == 1 and hb == 1))
            qfl = qf[b][:, :, :].rearrange("p h e -> p (h e)")
            P.op('act', lambda: nc.scalar.copy(out=qfl[:, 0:512], in_=p_q[:, 0, :]), r=['pq'], w=[k_('qf')])
            P.op('act', lambda: nc.scalar.copy(out=qfl[:, 512:768], in_=p_q[:, 1, 0:256]), r=['pq'], w=[k_('qf')])
            P.op('dve', lambda: nc.vector.tensor_copy(out=qb[b][:, :, 0:64], in_=qf[b][:, :, 0:64]), r=[k_('qf')], w=[k_('qb')])
            cosb = cs[b][:, 0, :].unsqueeze(1).to_broadcast([128, 8, 16])
            sinb = cs[b][:, 1, :].unsqueeze(1).to_broadcast([128, 8, 16])
            q1, q2 = qf[b][:, :, 64:80], qf[b][:, :, 80:96]
            for (ti, a_, b_) in ((0, q1, cosb), (1, q2, sinb), (2, q2, cosb), (3, q1, sinb)):
                P.op('dve', lambda: nc.vector.tensor_tensor(out=rt[b][:, ti, :, :], in0=a_, in1=b_, op=ALU.mult),
                     r=[k_('qf'), k_('cs')], w=[k_('rt')])
            P.op('dve', lambda: nc.vector.tensor_tensor(out=qb[b][:, :, 64:80], in0=rt[b][:, 0, :, :], in1=rt[b][:, 1, :, :], op=ALU.subtract),
                 r=[k_('rt')], w=[k_('qb')])
            P.op('dve', lambda: nc.vector.tensor_tensor(out=qb[b][:, :, 80:96], in0=rt[b][:, 2, :, :], in1=rt[b][:, 3, :, :], op=ALU.add),
                 r=[k_('rt')], w=[k_('qb')])
            P.dma('pool', lambda: nc.gpsimd.dma_start(out=S['QS'][r0:r0 + 128, :], in_=qb[b][:, :, :].rearrange("p h e -> p (h e)")),
                  r=[k_('qb')], w=[('QS', blk)])
            P.op('dve', lambda: nc.vector.scalar_tensor_tensor(out=ckv[b][:, :], in0=zc[b][:, QL:QL + KVL], scalar=st[b][:, 2:3], in1=W['kvg'][:, :],
                                                               op0=ALU.mult, op1=ALU.mult),
                 r=[k_('zc'), k_('st1'), 'Wsmall'], w=[k_('ckv')])
            P.dma('pool', lambda: nc.gpsimd.dma_start(out=O['kvp'][r0:r0 + 128, :], in_=ckv[b][:, :]), r=[k_('ckv')], w=[('okv', tg, blk)])
            P.op('dve', lambda: nc.vector.tensor_copy(out=ckb[b][:, :], in_=ckv[b][:, :]), r=[k_('ckv')], w=[k_('ckb')])
            k1, k2 = zc[b][:, 384:400], zc[b][:, 400:416]
            for (ti, a_, b_) in ((0, k1, cs[b][:, 0, :]), (1, k2, cs[b][:, 1, :]), (2, k2, cs[b][:, 0, :]), (3, k1, cs[b][:, 1, :])):
                P.op('dve', lambda: nc.vector.tensor_tensor(out=krt[b][:, ti, :], in0=a_, in1=b_, op=ALU.mult),
                     r=[k_('zc'), k_('cs')], w=[k_('krt')])
            P.op('dve', lambda: nc.vector.tensor_tensor(out=krf[b][:, 0:16], in0=krt[b][:, 0, :], in1=krt[b][:, 1, :], op=ALU.subtract), r=[k_('krt')], w=[k_('krf')])
            P.op('dve', lambda: nc.vector.tensor_tensor(out=krf[b][:, 16:32], in0=krt[b][:, 2, :], in1=krt[b][:, 3, :], op=ALU.add), r=[k_('krt')], w=[k_('krf')])
            P.dma('pool', lambda: nc.gpsimd.dma_start(out=O['krp'][r0:r0 + 128, :], in_=krf[b][:, :]), r=[k_('krf')], w=[('okr', tg, blk)])
            P.op('dve', lambda: nc.vector.tensor_copy(out=kq[b][:, 64:96], in_=krf[b][:, :]), r=[k_('krf'), k_('kq')], w=[k_('kq')])
            P.op('pe', lambda: nc.tensor.transpose(out=p_tb[:, 0, :], in_=ckb[b][:, :], identity=identb[:, :]), r=[k_('ckb'), 'identb'], w=['ptb'], inc=False)
            P.op('pe', lambda: nc.tensor.transpose(out=p_tb[0:96, 1, :], in_=kq[b][:, :], identity=identb[:, :]), r=[k_('kq'), 'identb'], w=['ptb'])
            P.op('act', lambda: nc.scalar.copy(out=St['ckvT'][:, blk * 128:(blk + 1) * 128], in_=p_tb[:, 0, :]), r=['ptb'], w=[('ckvT', blk)])
            for kb in range(2):
                P.op('act', lambda: nc.scalar.copy(out=St['KhT'][kb][64:96, blk * 128:(blk + 1) * 128], in_=p_tb[64:96, 1, :]),
                     r=['ptb'], w=[('KhT', kb, blk)])

        stageA(0)
        for blk in range(nblk):
            if blk + 1 < nblk:
                stageA(blk + 1)
            stageB(blk)
        lb = (nblk - 1) % 2
        for j in range(4):
            P.op('pe', lambda: nc.tensor.transpose(out=p_ga[0:1, j, :], in_=hs[lb][:, j, 127:128], identity=identf[:, :]),
                 r=[('hs', lb), 'identf'], w=['pga'], inc=(j == 3))
        P.op('act', lambda: nc.scalar.copy(out=yl[0][0:1, :], in_=p_ga[0:1, :, :].rearrange("p j t -> p (j t)")), r=['pga'], w=[('yl', 0)])
        P.dma('pool', lambda: nc.gpsimd.dma_start(out=O['hp'], in_=yl[0][0:1, :]), r=[('yl', 0)], w=['o_hp'])
        for j in range(4):
            P.op('pe', lambda: nc.tensor.transpose(out=p_gi[0:3, j, :], in_=xlx[lb][:, j, 128:131], identity=identf[:, :]),
                 r=[('xlx', lb), 'identf'], w=['pgi'], inc=(j == 3))
        P.op('act', lambda: nc.scalar.copy(out=junk[0:3, 0:DL], in_=p_gi[0:3, :, :].rearrange("p j t -> p (j t)")), r=['pgi'], w=['junk'])
        P.dma('pool', lambda: nc.gpsimd.dma_start(out=O['cvp'], in_=junk[0:3, 0:DL]), r=['junk'], w=['o_cvp'])
    P.barrier()


def _bf16():
    import ml_dtypes
    return ml_dtypes.bfloat16


def make_in_maps(inp, cfg):
    SEQ, NB, NPG, NPHYS = cfg['SEQ'], cfg['NB'], cfg['NPG'], cfg['NPHYS']
    NBLK = SEQ // 128
    NSLOT = NBLK // 4
    NS = NB * 8
    f32 = np.float32
    a = {k: np.asarray(v) for k, v in inp.items()}
    past_len = NPG * 128

    def cols(g, n):
        return np.ascontiguousarray(np.asarray(g, f32).reshape(n, 128).T)
    half = 16
    freqs = (10000.0 ** (-np.arange(half, dtype=f32) / f32(half))).astype(f32)
    pos_p = np.arange(SEQ, dtype=f32)
    pos_s = (past_len + np.arange(8)).astype(f32)
    angp = (pos_p[:, None] * freqs[None, :]).astype(f32)
    angs = np.tile((pos_s[:, None] * freqs[None, :]).astype(f32), (NB, 1))
    common = {
        'final_norm': a['final_norm'].astype(f32),
        'identb': np.eye(128, dtype=f32).astype(_bf16()), 'identf': np.eye(128, dtype=f32),
        'w_in': a['w_in'][0], 'mix_gc': cols(a['mix_norm'][0], 8),
        'w_q_up': a['w_q_up'][0], 'q_gc': cols(a['q_norm'][0], 2),
        'w_out': a['w_out'][0], 'out_gc': cols(np.concatenate([a['out_norm_lru'][0], a['out_norm_mla'][0]]), 8),
        'w_uk': a['w_uk'][0], 'w_uv': a['w_uv'][0], 'lru_w_a': a['lru_w_a'][0], 'lru_w_i': a['lru_w_i'][0],
        'kv_norm': a['kv_norm'][0],
        'cosp': np.cos(angp).astype(f32), 'sinp': np.sin(angp).astype(f32),
        'coss': np.cos(angs).astype(f32), 'sins': np.sin(angs).astype(f32),
        'qcol': (np.arange(128) % 16).astype(f32).reshape(128, 1),
        'ckv_cache': a['cache_kv_latent'][0].reshape(NPHYS * 128, KVL),
        'ckr_cache': a['cache_k_rope'][0].reshape(NPHYS * 128, RP),
    }
    for n in ('ffn1', 'ffn2'):
        common[n + '_wg'] = a[n + '_w_gate'][0]
        common[n + '_wu'] = a[n + '_w_up'][0]
        common[n + '_wd'] = a[n + '_w_down'][0]
        common[n + '_gc'] = cols(a[n + '_norm'][0], 8)
    chv = np.zeros((128, 32), f32)
    cw = a['conv_w'][0]
    for j in range(4):
        for k in range(4):
            chv[:, j * 4 + k] = cw[k, j * 128:(j + 1) * 128]
    chv[:, 16:20] = cols(a['conv_b'][0], 4)
    chv[:, 20:24] = cols(a['lru_b_a'][0], 4)
    chv[:, 24:28] = cols(a['lru_b_i'][0], 4)
    chv[:, 28:32] = cols(a['lru_lambda'][0], 4)
    common['chv'] = chv
    mo = np.zeros((8, 8, 8), f32)
    for kp in range(8):
        mo[kp, :, kp:] = 1.0
    common['mask_own'] = mo.reshape(8, 64)
    tri = (np.arange(128)[:, None] <= np.arange(128)[None, :]).astype(f32)
    maps = []
    for c in range(8):
        seq, j = c // 4, c % 4
        m = dict(common)
        m['xp'] = a['x_prompt'][seq]
        m['xs'] = a['x_sample'][c * NB:(c + 1) * NB].reshape(NS, D)
        m['conv0'] = a['state_conv'][0, c * NB:(c + 1) * NB]
        m['h0'] = a['state_lru_h'][0, c * NB:(c + 1) * NB]
        own = np.zeros((128, NSLOT), np.int32)
        for s in range(NSLOT):
            own[:, s] = (4 * s + j) * 128 + np.arange(128)
        m['own_idx'] = own
        mk = np.zeros((128, 4, 128), f32)
        for i in range(4):
            if i < j:
                mk[:, i, :] = 1.0
            elif i == j:
                mk[:, i, :] = tri
        m['masks'] = mk
        pt = a['page_table'][c * NB:(c + 1) * NB].astype(np.int32)
        m['pt_l'] = np.ascontiguousarray(pt.reshape(NB, NPG // 8, 8).transpose(2, 0, 1))
        maps.append({k: np.ascontiguousarray(v) for k, v in m.items()})
    return maps


def assemble(results, cfg, B=2):
    SEQ, NB = cfg['SEQ'], cfg['NB']
    NBLK = SEQ // 128
    NSLOT = NBLK // 4
    f32 = np.float32
    yp = np.zeros((B, SEQ, D), f32)
    ys = np.zeros((8 * NB, 8, D), f32)
    kvp = np.zeros((1, B, SEQ, KVL), f32); krp = np.zeros((1, B, SEQ, RP), f32)
    hp = np.zeros((1, B, DL), f32); cvp = np.zeros((1, B, 3, DL), f32)
    kvs = np.zeros((1, 8 * NB, 8, KVL), f32); krs = np.zeros((1, 8 * NB, 8, RP), f32)
    hs = np.zeros((1, 8 * NB, DL), f32); cvs = np.zeros((1, 8 * NB, 3, DL), f32)
    for c in range(8):
        r = results[c]
        seq, j = c // 4, c % 4
        yo = np.asarray(r['y_own'])
        for s in range(NSLOT):
            b = 4 * s + j
            yp[seq, b * 128:(b + 1) * 128] = yo[s * 128:(s + 1) * 128]
        if j == 0:
            kvp[0, seq] = r['kvp']; krp[0, seq] = r['krp']; hp[0, seq] = np.asarray(r['hp'])[0]; cvp[0, seq] = r['cvp']
        sl = slice(c * NB, (c + 1) * NB)
        ys[sl] = np.asarray(r['ys']).reshape(NB, 8, D)
        kvs[0, sl] = np.asarray(r['kvs']).reshape(NB, 8, KVL); krs[0, sl] = np.asarray(r['krs']).reshape(NB, 8, RP)
        hs[0, sl] = r['hs_']; cvs[0, sl] = r['cvs']
    return (yp, ys, kvp, krp, hp, cvp, kvs, krs, hs, cvs)


_NC_CACHE = {}


def kernel(**inputs):
    SEQ = int(inputs['x_prompt'].shape[1])
    NB = int(inputs['x_sample'].shape[0]) // 8
    NPG = int(inputs['page_table'].shape[1])
    NPHYS = int(inputs['cache_kv_latent'].shape[1])
    cfg = dict(SEQ=SEQ, NB=NB, NPG=NPG, NPHYS=NPHYS)
    key = tuple(sorted(cfg.items()))
    if key not in _NC_CACHE:
        _NC_CACHE[key] = build_program(cfg)
    nc = _NC_CACHE[key]
    maps = make_in_maps(inputs, cfg)
    res = run_bass_kernel_spmd(nc, maps, core_ids=list(range(8)))
    return assemble(res.results, cfg)
```

```python
import numpy as np
from contextlib import ExitStack
import concourse.bass as bass
import concourse.mybir as mybir
from concourse.bass_utils import run_bass_kernel_spmd

F32 = mybir.dt.float32
BF16 = mybir.dt.bfloat16
I32 = mybir.dt.int32
AF = mybir.ActivationFunctionType
ALU = mybir.AluOpType

D = 1024
DFF = 2816
NF = DFF // 128
DL = 512
QL = 256
KVL = 128
RP = 32
NH = 8
DIN = 1440
EPS = 1e-6
EPOCH = 2000
DGEN = 120
KDMA = 8


class Prog:
    def __init__(self, nc, es):
        self.nc = nc
        self.es = es
        self.engs = {'pe': nc.tensor, 'act': nc.scalar, 'dve': nc.vector, 'pool': nc.gpsimd, 'sp': nc.sync}
        self.cnt = {e: 0 for e in self.engs}
        self.sems = {e: [] for e in self.engs}
        self.dma_cnt = {'sp': 0, 'pool': 0}
        self.dma_sems = {q: [] for q in ('sp', 'pool')}
        self.waited = {}
        self.last_w = {}
        self.readers = {}
        self.semown = {}

    def _sem(self, e, ep):
        while len(self.sems[e]) <= ep:
            s = self.es.enter_context(self.nc.semaphore(f"s_{e}_{len(self.sems[e])}"))
            self.semown[id(s)] = e
            self.sems[e].append(s)
        return self.sems[e][ep]

    def _wait(self, e, tok):
        sem, val = tok
        k = (e, id(sem))
        if self.waited.get(k, 0) >= val:
            return
        self.waited[k] = val
        self.engs[e].wait_ge(sem, val)

    def _deps(self, e, r, w):
        toks = []
        for k in r:
            t = self.last_w.get(k)
            if t is not None:
                toks.append(t)
        for k in w:
            t = self.last_w.get(k)
            if t is not None:
                toks.append(t)
            for t in self.readers.get(k, ()):
                toks.append(t)
        out = []
        for t in toks:
            if e == 'pe' and self.semown.get(id(t[0])) == 'pe':
                continue
            out.append(t)
        return out

    def _record(self, tok, r, w):
        for k in w:
            self.last_w[k] = tok
            self.readers[k] = []
        for k in r:
            self.readers.setdefault(k, []).append(tok)

    def op(self, e, fn, r=(), w=(), inc=True):
        for t in self._deps(e, r, w):
            self._wait(e, t)
        i = self.cnt[e]
        sem = self._sem(e, i // EPOCH)
        inst = fn()
        tok = (sem, i % EPOCH + 1)
        if inc:
            inst.then_inc(sem, 1)
            self.cnt[e] += 1
        self._record(tok, r, w)
        return tok

    def _dsem(self, q, n):
        st = n // (KDMA * DGEN)
        while len(self.dma_sems[q]) <= st:
            k = len(self.dma_sems[q])
            self.dma_sems[q].append([self.es.enter_context(self.nc.semaphore(f"dq_{q}_{k}_{i}")) for i in range(KDMA)])
        m = n % (KDMA * DGEN)
        return self.dma_sems[q][st][m % KDMA], m // KDMA, st

    def dma(self, q, fn, r=(), w=()):
        for t in self._deps(q, r, w):
            self._wait(q, t)
        n = self.dma_cnt[q]
        sem, gen, st = self._dsem(q, n)
        if gen > 0:
            self._wait(q, (sem, 16 * gen))
        elif st > 0:
            ps, pg, _ = self._dsem(q, n - KDMA)
            self._wait(q, (ps, 16 * (pg + 1)))
        inst = fn()
        inst.then_inc(sem, 16)
        tok = (sem, 16 * (gen + 1))
        self.dma_cnt[q] += 1
        self._record(tok, r, w)
        return tok

    def barrier(self):
        toks = []
        for e in self.engs:
            i = self.cnt[e]
            if i > 0:
                toks.append((self.sems[e][(i - 1) // EPOCH], (i - 1) % EPOCH + 1))
        toks += self._dma_tail()
        for e in self.engs:
            for t in toks:
                if self.semown.get(id(t[0])) == e:
                    continue
                self._wait(e, t)

    def _dma_tail(self):
        toks = []
        for q in ('sp', 'pool'):
            n = self.dma_cnt[q]
            for m in range(max(0, n - KDMA), n):
                sem, gen, _ = self._dsem(q, m)
                toks.append((sem, 16 * (gen + 1)))
        return toks

    def finish(self):
        for t in self._dma_tail():
            for q in ('sp', 'pool'):
                self._wait(q, t)


def _tile(nc, es, name, shape, dt):
    return es.enter_context(nc.sbuf_tensor(name, list(shape), dt))


def _psum(nc, es, name, shape, dt):
    return es.enter_context(nc.psum_tensor(name, list(shape), dt))


def emit_ffn_weight_prep(P, nc, es0, wg, wu, wd, gain_cols, WGU_s, WD_s, tag):
    with ExitStack() as es:
        stg = [_tile(nc, es, f"wp_stg{tag}{i}", [128, 2, 8, 128], F32) for i in range(2)]
        stgd = [_tile(nc, es, f"wp_stgd{tag}{i}", [128, 1024], F32) for i in range(2)]
        ob = [_tile(nc, es, f"wp_ob{tag}{i}", [128, 2, 8, 128], BF16) for i in range(2)]
        obd = [_tile(nc, es, f"wp_obd{tag}{i}", [128, 1024], BF16) for i in range(2)]
        gb = _tile(nc, es, f"wp_gb{tag}", [128, 8, 128], F32)
        gc = _tile(nc, es, f"wp_gc{tag}", [128, 8], F32)
        ones = _tile(nc, es, f"wp_ones{tag}", [128, 128], F32)
        P.dma('sp', lambda: nc.sync.dma_start(out=gc[:, :], in_=gain_cols), w=[('wp_gc', tag)])
        P.op('dve', lambda: nc.vector.memset(ones[:, :], 1.0), w=[('wp_ones', tag)])
        for k in range(8):
            P.op('dve', lambda k=k: nc.vector.tensor_scalar(
                out=gb[:, k, :], in0=ones[:, :], scalar1=gc[:, k:k + 1], scalar2=None, op0=ALU.mult),
                r=[('wp_gc', tag), ('wp_ones', tag)], w=[('wp_gb', tag)])
        for f in range(NF):
            b = f % 2
            P.dma('sp', lambda: nc.sync.dma_start(
                out=stg[b][:, 0, :, :], in_=wg[:, f * 128:(f + 1) * 128].rearrange("(k p) c -> p k c", p=128)),
                w=[('wp_stg', tag, b, 0)])
            P.dma('sp', lambda: nc.sync.dma_start(
                out=stg[b][:, 1, :, :], in_=wu[:, f * 128:(f + 1) * 128].rearrange("(k p) c -> p k c", p=128)),
                w=[('wp_stg', tag, b, 1)])
            P.dma('sp', lambda: nc.sync.dma_start(out=stgd[b][:, :], in_=wd[f * 128:(f + 1) * 128, :]),
                  w=[('wp_stgd', tag, b)])
            for h in range(2):
                eng = 'dve' if h == 0 else 'pool'
                E = nc.vector if h == 0 else nc.gpsimd
                P.op(eng, lambda E=E, h=h: E.tensor_tensor(
                    out=ob[b][:, h, :, :], in0=stg[b][:, h, :, :], in1=gb[:, :, :], op=ALU.mult),
                    r=[('wp_stg', tag, b, h), ('wp_gb', tag)], w=[('wp_ob', tag, b, h)])
            P.op('act', lambda: nc.scalar.copy(out=obd[b][:, :], in_=stgd[b][:, :]),
                 r=[('wp_stgd', tag, b)], w=[('wp_obd', tag, b)])
            P.dma('pool', lambda: nc.gpsimd.dma_start(out=WGU_s[f], in_=ob[b][:, :, :, :]),
                  r=[('wp_ob', tag, b, 0), ('wp_ob', tag, b, 1)], w=[('WGU', tag, f)])
            P.dma('pool', lambda: nc.gpsimd.dma_start(out=WD_s[f], in_=obd[b][:, :]),
                  r=[('wp_obd', tag, b)], w=[('WD', tag, f)])
    P.barrier()


def emit_ffn_phase(P, nc, identb, tiles, WGU_s, WD_s, tag, final_gain=None):
    with ExitStack() as es:
        NW = 3
        xt = [_tile(nc, es, f"f_xt{tag}{i}", [128, 4, D], F32) for i in range(2)]
        xn = _tile(nc, es, f"f_xn{tag}", [128, D], BF16)
        xnT = _tile(nc, es, f"f_xnT{tag}", [128, 8, 512], BF16)
        hT = _tile(nc, es, f"f_hT{tag}", [128, NF, 512], BF16)
        sg = [_tile(nc, es, f"f_sg{tag}{i}", [128, 512], F32) for i in range(2)]
        wgu = [_tile(nc, es, f"f_wgu{tag}{i}", [128, 2, 8, 128], BF16) for i in range(NW)]
        wd_all = _tile(nc, es, f"f_wdall{tag}", [128, NF, 1024], BF16)
        ss = _tile(nc, es, f"f_ss{tag}", [128, 8], F32)
        rstd = _tile(nc, es, f"f_rstd{tag}", [128, 8], F32)
        junk = _tile(nc, es, f"f_junk{tag}", [128, D], F32)
        pg = [_psum(nc, es, f"f_pg{tag}{i}", [128, 512], F32) for i in range(2)]
        pu = [_psum(nc, es, f"f_pu{tag}{i}", [128, 512], F32) for i in range(2)]
        py = [_psum(nc, es, f"f_py{tag}{i}", [128, 512], F32) for i in range(2)]
        pt = _psum(nc, es, f"f_pt{tag}", [128, 8, 128], BF16)
        if final_gain is not None:
            fg = _tile(nc, es, f"f_fg{tag}", [128, D], F32)
            P.dma('sp', lambda: nc.sync.dma_start(out=fg[:, :], in_=final_gain.partition_broadcast(128)),
                  w=[('f_fg', tag)])
        wcount = 0
        ycount = 0
        P.dma('sp', lambda: nc.sync.dma_start(out=wd_all[:, :, :], in_=WD_s.rearrange("f p c -> p f c")),
              r=[('WD', tag, f) for f in range(NF)], w=[('f_wdall', tag)])
        for ti, (src, dst, T, skeys, dkeys) in enumerate(tiles):
            ns = T // 128
            xb = xt[ti % 2]
            kx = ('f_xt', tag, ti % 2)
            P.dma('sp', lambda: nc.sync.dma_start(
                out=xb[:, 0:ns, :], in_=src.rearrange("(s p) d -> p s d", p=128)), r=skeys, w=[kx])
            for s in range(ns):
                P.op('act', lambda s=s: nc.scalar.activation(
                    out=junk[:, :], in_=xb[:, s, :], func=AF.Square, accum_out=ss[:, s:s + 1]),
                    r=[kx], w=[('f_junk', tag), ('f_ss', tag, s)])
            P.op('act', lambda: nc.scalar.activation(
                out=rstd[:, 0:ns], in_=ss[:, 0:ns], func=AF.Sqrt, bias=EPS, scale=1.0 / D),
                r=[('f_ss', tag, s) for s in range(ns)], w=[('f_rstd', tag)])
            P.op('dve', lambda: nc.vector.reciprocal(out=rstd[:, 0:ns], in_=rstd[:, 0:ns]),
                 r=[('f_rstd', tag)], w=[('f_rstd', tag)])
            for s in range(ns):
                P.op('dve', lambda s=s: nc.vector.tensor_scalar(
                    out=xn[:, :], in0=xb[:, s, :], scalar1=rstd[:, s:s + 1], scalar2=None, op0=ALU.mult),
                    r=[kx, ('f_rstd', tag)], w=[('f_xn', tag)])
                for k in range(8):
                    P.op('pe', lambda k=k: nc.tensor.transpose(
                        out=pt[:, k, :], in_=xn[:, k * 128:(k + 1) * 128], identity=identb[:, :]),
                        r=[('f_xn', tag), 'identb'], w=[('f_pt', tag)], inc=(k == 7))
                P.op('act', lambda s=s: nc.scalar.copy(out=xnT[:, :, s * 128:(s + 1) * 128], in_=pt[:, :, :]),
                     r=[('f_pt', tag)], w=[('f_xnT', tag, s)])
            kxnT = [('f_xnT', tag, s) for s in range(ns)]
            for f in range(NF):
                wb = wcount % NW
                wcount += 1
                P.dma('sp', lambda: nc.sync.dma_start(out=wgu[wb][:, :, :, :], in_=WGU_s[f]),
                      r=[('WGU', tag, f)], w=[('f_wgu', tag, wb)])
                pb = f % 2
                for k in range(8):
                    P.op('pe', lambda k=k: nc.tensor.matmul(
                        pg[pb][:, 0:T], lhsT=wgu[wb][:, 0, k, :], rhs=xnT[:, k, 0:T], start=(k == 0), stop=(k == 7)),
                        r=[('f_wgu', tag, wb)] + kxnT, w=[('f_pg', tag, pb)], inc=(k == 7))
                for k in range(8):
                    P.op('pe', lambda k=k: nc.tensor.matmul(
                        pu[pb][:, 0:T], lhsT=wgu[wb][:, 1, k, :], rhs=xnT[:, k, 0:T], start=(k == 0), stop=(k == 7)),
                        r=[('f_wgu', tag, wb)] + kxnT, w=[('f_pu', tag, pb)], inc=(k == 7))
                P.op('act', lambda: nc.scalar.activation(out=sg[pb][:, 0:T], in_=pg[pb][:, 0:T], func=AF.Silu),
                     r=[('f_pg', tag, pb)], w=[('f_sg', tag, pb)])
                P.op('dve', lambda f=f: nc.vector.tensor_tensor(
                    out=hT[:, f, 0:T], in0=sg[pb][:, 0:T], in1=pu[pb][:, 0:T], op=ALU.mult),
                    r=[('f_sg', tag, pb), ('f_pu', tag, pb)], w=[('f_hT', tag, f)])
            for s in range(ns):
                for half in range(2):
                    yb = ycount % 2
                    ycount += 1
                    for f in range(NF):
                        P.op('pe', lambda: nc.tensor.matmul(
                            py[yb][:, :], lhsT=hT[:, f, s * 128:(s + 1) * 128],
                            rhs=wd_all[:, f, half * 512:(half + 1) * 512], start=(f == 0), stop=(f == NF - 1)),
                            r=[('f_hT', tag, f), ('f_wdall', tag)], w=[('f_py', tag, yb)], inc=(f == NF - 1))
                    P.op('dve', lambda: nc.vector.scalar_tensor_tensor(
                        out=xb[:, s, half * 512:(half + 1) * 512], in0=py[yb][:, :], scalar=0.5,
                        in1=xb[:, s, half * 512:(half + 1) * 512], op0=ALU.mult, op1=ALU.add),
                        r=[('f_py', tag, yb), kx], w=[kx])
            if final_gain is not None:
                for s in range(ns):
                    P.op('act', lambda s=s: nc.scalar.activation(
                        out=junk[:, :], in_=xb[:, s, :], func=AF.Square, accum_out=ss[:, s:s + 1]),
                        r=[kx], w=[('f_junk', tag), ('f_ss', tag, s)])
                P.op('act', lambda: nc.scalar.activation(
                    out=rstd[:, 0:ns], in_=ss[:, 0:ns], func=AF.Sqrt, bias=EPS, scale=1.0 / D),
                    r=[('f_ss', tag, s) for s in range(ns)], w=[('f_rstd', tag)])
                P.op('dve', lambda: nc.vector.reciprocal(out=rstd[:, 0:ns], in_=rstd[:, 0:ns]),
                     r=[('f_rstd', tag)], w=[('f_rstd', tag)])
                for s in range(ns):
                    P.op('dve', lambda s=s: nc.vector.scalar_tensor_tensor(
                        out=xb[:, s, :], in0=xb[:, s, :], scalar=rstd[:, s:s + 1], in1=fg[:, :],
                        op0=ALU.mult, op1=ALU.mult),
                        r=[kx, ('f_rstd', tag), ('f_fg', tag)], w=[kx])
            P.dma('pool', lambda: nc.gpsimd.dma_start(
                out=dst.rearrange("(s p) d -> p s d", p=128), in_=xb[:, 0:ns, :]), r=[kx], w=dkeys)
    P.barrier()


def build_program(cfg, stage="full"):
    SEQ, NB, NPG, NPHYS = cfg['SEQ'], cfg['NB'], cfg['NPG'], cfg['NPHYS']
    NSLOT = SEQ // 128 // 4
    NOWN = NSLOT * 128
    NS = NB * 8
    NG = NPG // 8
    nc = bass.Bass("TRN2", target_bir_lowering=False)

    def din(name, shape, dt=F32):
        return nc.dram_tensor(name, list(shape), dt, kind="ExternalInput").ap()

    def dout(name, shape, dt=F32):
        return nc.dram_tensor(name, list(shape), dt, kind="ExternalOutput").ap()

    def dscr(name, shape, dt):
        return nc.dram_tensor(name, list(shape), dt, kind="Internal").ap()

    I = {}
    I['xp'] = din('xp', [SEQ, D])
    I['xs'] = din('xs', [NS, D])
    for n in ('ffn1', 'ffn2'):
        I[n + '_wg'] = din(n + '_wg', [D, DFF])
        I[n + '_wu'] = din(n + '_wu', [D, DFF])
        I[n + '_wd'] = din(n + '_wd', [DFF, D])
        I[n + '_gc'] = din(n + '_gc', [128, 8])
    I['final_norm'] = din('final_norm', [D])
    I['identb'] = din('identb', [128, 128], BF16)
    I['identf'] = din('identf', [128, 128])
    I['w_in'] = din('w_in', [D, DIN]); I['mix_gc'] = din('mix_gc', [128, 8])
    I['w_q_up'] = din('w_q_up', [QL, 768]); I['q_gc'] = din('q_gc', [128, 2])
    I['w_out'] = din('w_out', [D, D]); I['out_gc'] = din('out_gc', [128, 8])
    I['w_uk'] = din('w_uk', [KVL, 8, 64]); I['w_uv'] = din('w_uv', [KVL, 8, 64])
    I['lru_w_a'] = din('lru_w_a', [8, 64, 64]); I['lru_w_i'] = din('lru_w_i', [8, 64, 64])
    I['chv'] = din('chv', [128, 32]); I['kv_norm'] = din('kv_norm', [KVL])
    I['cosp'] = din('cosp', [SEQ, 16]); I['sinp'] = din('sinp', [SEQ, 16])
    I['coss'] = din('coss', [NS, 16]); I['sins'] = din('sins', [NS, 16])
    I['conv0'] = din('conv0', [NB, 3, DL]); I['h0'] = din('h0', [NB, DL])
    I['own_idx'] = din('own_idx', [128, NSLOT], I32)
    I['masks'] = din('masks', [128, 4, 128])
    I['pt_l'] = din('pt_l', [8, NB, NG], I32)
    I['qcol'] = din('qcol', [128, 1]); I['mask_own'] = din('mask_own', [8, 64])
    I['ckv_cache'] = din('ckv_cache', [NPHYS * 128, KVL]); I['ckr_cache'] = din('ckr_cache', [NPHYS * 128, RP])
    O = {}
    O['y_own'] = dout('y_own', [NOWN, D]); O['ys'] = dout('ys', [NS, D])
    O['kvp'] = dout('kvp', [SEQ, KVL]); O['krp'] = dout('krp', [SEQ, RP])
    O['hp'] = dout('hp', [1, DL]); O['cvp'] = dout('cvp', [3, DL])
    O['kvs'] = dout('kvs', [NS, KVL]); O['krs'] = dout('krs', [NS, RP])
    O['hs_'] = dout('hs_', [NB, DL]); O['cvs'] = dout('cvs', [NB, 3, DL])
    if cfg.get('debug'):
        O['dbg_idx'] = dout('dbg_idx', [128, NB * NG], I32)
        O['dbg_G'] = dout('dbg_G', [128, 1024])
        O['dbg_GR'] = dout('dbg_GR', [128, 256])
        O['dbg_YM'] = dout('dbg_YM', [NS, DL])
        O['dbg_ps'] = dout('dbg_ps', [128, 512])
        O['dbg_ov'] = dout('dbg_ov', [64, 129])
    S = {}
    for n in ('ffn1', 'ffn2'):
        S[n + '_WGU'] = dscr(n + '_WGU', [NF, 128, 2, 8, 128], BF16)
        S[n + '_WD'] = dscr(n + '_WD', [NF, 128, 1024], BF16)
    S['X1'] = dscr('X1', [SEQ + NS, D], F32)
    S['X2'] = dscr('X2', [NOWN + NS, D], F32)
    S['YL'] = dscr('YL', [SEQ + NS, DL], F32)
    S['QS'] = dscr('QS', [SEQ + NS, 768], BF16)
    S['CKS'] = dscr('CKS', [NS, KVL], BF16)
    S['YM'] = dscr('YM', [NS, DL], F32)

    with ExitStack() as es:
        P = Prog(nc, es)
        W = {}
        W['identb'] = _tile(nc, es, "identb_sb", [128, 128], BF16)
        W['identf'] = _tile(nc, es, "identf_sb", [128, 128], F32)
        identb = W['identb']
        P.dma('sp', lambda: nc.sync.dma_start(out=identb[:, :], in_=I['identb']), w=['identb'])
        P.dma('sp', lambda: nc.sync.dma_start(out=W['identf'][:, :], in_=I['identf']), w=['identf'])
        for n in ('ffn1', 'ffn2'):
            emit_ffn_weight_prep(P, nc, es, I[n + '_wg'], I[n + '_wu'], I[n + '_wd'], I[n + '_gc'],
                                 S[n + '_WGU'], S[n + '_WD'], n)
        TT = 512
        tiles = []
        for t in range(SEQ // TT):
            tiles.append((I['xp'][t * TT:(t + 1) * TT, :], S['X1'][t * TT:(t + 1) * TT, :], TT, [],
                          [('X1', t * 4 + i) for i in range(4)]))
        tiles.append((I['xs'], S['X1'][SEQ:SEQ + NS, :], NS, [], [('X1', 's')]))
        emit_ffn_phase(P, nc, identb, tiles, S['ffn1_WGU'], S['ffn1_WD'], 'ffn1')
        with ExitStack() as es2:
            W['wq'] = _tile(nc, es2, "W_wq", [128, 2, 768], BF16)
            W['wout'] = _tile(nc, es2, "W_wout", [128, 8, D], BF16)
            W['wuk'] = _tile(nc, es2, "W_wuk", [128, 8, 64], BF16)
            W['wuv'] = _tile(nc, es2, "W_wuv", [128, 8, 64], BF16)
            W['wukT'] = _tile(nc, es2, "W_wukT", [64, 8, 128], BF16)
            W['bd'] = _tile(nc, es2, "W_bd", [128, 2, 4, 128], BF16)
            W['chv'] = _tile(nc, es2, "W_chv", [128, 32], F32)
            W['cch'] = _tile(nc, es2, "W_cch", [128, 4], F32)
            W['kvg'] = _tile(nc, es2, "W_kvg", [128, KVL], F32)
            St = {}
            St['ckvT'] = _tile(nc, es2, "St_ckvT", [128, SEQ], BF16)
            St['KhT'] = [_tile(nc, es2, f"St_KhT{i}", [96, SEQ], BF16) for i in range(2)]
            St['ckvT_s'] = _tile(nc, es2, "St_ckvTs", [128, 128], BF16)
            St['krT_s'] = _tile(nc, es2, "St_krTs", [96, 128], BF16)
            with ExitStack() as es3:
                W['win'] = _tile(nc, es3, "W_win", [128, 8, DIN], BF16)
                emit_small_prep(P, nc, es2, I, W)
                emit_mixer_prompt(P, nc, W, St, I, O, S, cfg)
                emit_mixer_in(P, nc, W, St, I, O, S, cfg, sample=True)
            emit_prompt_attn(P, nc, W, St, I, S, cfg)
            emit_sample_attn(P, nc, W, St, I, S, cfg, O)
        P.barrier()
        tiles = []
        t0 = 0
        while t0 < NOWN:
            tt = min(TT, NOWN - t0)
            tiles.append((S['X2'][t0:t0 + tt, :], O['y_own'][t0:t0 + tt, :], tt,
                          [('X2', t0 // 128 + i) for i in range(tt // 128)], [('y_own', t0)]))
            t0 += tt
        tiles.append((S['X2'][NOWN:NOWN + NS, :], O['ys'], NS, [('X2', 's')], [('ys',)]))
        emit_ffn_phase(P, nc, identb, tiles, S['ffn2_WGU'], S['ffn2_WD'], 'ffn2', final_gain=I['final_norm'])
        P.finish()
    return nc


def emit_small_prep(P, nc, es0, I, W):
    with ExitStack() as es:
        stg = [_tile(nc, es, f"sp_stg{i}", [128, 1440], F32) for i in range(2)]
        gc = _tile(nc, es, "sp_gc", [128, 18], F32)
        P.dma('sp', lambda: nc.sync.dma_start(out=gc[:, 0:8], in_=I['mix_gc']), w=['sp_gc'])
        P.dma('sp', lambda: nc.sync.dma_start(out=gc[:, 8:10], in_=I['q_gc']), w=['sp_gc'])
        P.dma('sp', lambda: nc.sync.dma_start(out=gc[:, 10:18], in_=I['out_gc']), w=['sp_gc'])
        n = 0

        def cast_rows(dst, src, ncols, gcol):
            nonlocal n
            b = n % 2
            n += 1
            P.dma('sp', lambda: nc.sync.dma_start(out=stg[b][:, 0:ncols], in_=src), w=[('sp_stg', b)])
            if gcol is None:
                P.op('dve', lambda: nc.vector.tensor_copy(out=dst, in_=stg[b][:, 0:ncols]),
                     r=[('sp_stg', b)], w=['Wsmall'])
            else:
                P.op('dve', lambda: nc.vector.tensor_scalar(
                    out=dst, in0=stg[b][:, 0:ncols], scalar1=gc[:, gcol:gcol + 1], scalar2=None, op0=ALU.mult),
                    r=[('sp_stg', b), 'sp_gc'], w=['Wsmall'])
        for k in range(8):
            cast_rows(W['win'][:, k, :], I['w_in'][k * 128:(k + 1) * 128, :], DIN, k)
        for k in range(2):
            cast_rows(W['wq'][:, k, :], I['w_q_up'][k * 128:(k + 1) * 128, :], 768, 8 + k)
        for k in range(8):
            cast_rows(W['wout'][:, k, :], I['w_out'][k * 128:(k + 1) * 128, :], D, 10 + k)
        cast_rows(W['wuk'][:, :, :], I['w_uk'].rearrange("c h n -> c (h n)"), 512, None)
        cast_rows(W['wuv'][:, :, :], I['w_uv'].rearrange("c h n -> c (h n)"), 512, None)
        bd = _tile(nc, es, "sp_bd", [128, 2, 4, 128], F32)
        P.op('dve', lambda: nc.vector.memset(bd[:, :, :, :], 0.0), w=['sp_bd'])
        for gi, nm in enumerate(('lru_w_a', 'lru_w_i')):
            for j in range(4):
                P.dma('sp', lambda: nc.sync.dma_start(out=bd[0:64, gi, j, 0:64], in_=I[nm][2 * j]), r=[], w=['sp_bd'])
                P.dma('sp', lambda: nc.sync.dma_start(out=bd[64:128, gi, j, 64:128], in_=I[nm][2 * j + 1]), r=[], w=['sp_bd'])
        P.op('dve', lambda: nc.vector.tensor_copy(out=W['bd'][:, :, :, :], in_=bd[:, :, :, :]), r=['sp_bd'], w=['Wsmall'])
        P.dma('sp', lambda: nc.sync.dma_start(out=W['chv'][:, :], in_=I['chv']), w=['Wsmall'])
        P.dma('sp', lambda: nc.sync.dma_start(out=W['kvg'][:, :], in_=I['kv_norm'].partition_broadcast(128)), w=['Wsmall'])
        P.op('act', lambda: nc.scalar.activation(out=W['cch'][:, :], in_=W['chv'][:, 28:32], func=AF.Exp, scale=-1.0),
             r=['Wsmall'], w=['cch'])
        P.op('act', lambda: nc.scalar.activation(out=W['cch'][:, :], in_=W['cch'][:, :], func=AF.Ln, bias=1.0, scale=1.0),
             r=['cch'], w=['cch'])
        P.op('dve', lambda: nc.vector.tensor_scalar(out=W['cch'][:, :], in0=W['cch'][:, :], scalar1=-8.0, scalar2=None, op0=ALU.mult),
             r=['cch'], w=['cch', 'Wsmall'])
        pT = _psum(nc, es, "sp_pT", [64, 8, 128], BF16)
        for h in range(8):
            P.op('pe', lambda: nc.tensor.transpose(out=pT[:, h, :], in_=W['wuk'][:, h, :], identity=W['identb'][:, :]),
                 r=['Wsmall', 'identb'], w=['sp_pT'], inc=(h == 7))
        P.op('act', lambda: nc.scalar.copy(out=W['wukT'][:, :, :], in_=pT[:, :, :]), r=['sp_pT'], w=['Wsmall'])
    P.barrier()


def emit_mixer_in(P, nc, W, St, I, O, S, cfg, sample):
    SEQ, NB = cfg['SEQ'], cfg['NB']
    NS = NB * 8
    nblk = 1 if sample else SEQ // 128
    tg = 's' if sample else 'p'
    with ExitStack() as es:
        x1 = _tile(nc, es, f"m_x1{tg}", [128, D], F32)
        junk = _tile(nc, es, f"m_junk{tg}", [128, 1536], F32)
        xn = _tile(nc, es, f"m_xn{tg}", [128, D], BF16)
        xnT = _tile(nc, es, f"m_xnT{tg}", [128, 8, 128], BF16)
        st = _tile(nc, es, f"m_st{tg}", [128, 8], F32)
        XW = 11 if sample else 131
        xlx = _tile(nc, es, f"m_xlx{tg}", [128, 4, (16 * 11) if sample else 131], F32)
        xc = _tile(nc, es, f"m_xc{tg}", [128, 4, 128], F32)
        xcb = _tile(nc, es, f"m_xcb{tg}", [128, 4, 128], BF16)
        rr = _tile(nc, es, f"m_rr{tg}", [128, 4, 128], F32)
        ii = _tile(nc, es, f"m_ii{tg}", [128, 4, 128], F32)
        aa = _tile(nc, es, f"m_aa{tg}", [128, 4, 128], F32)
        mm = _tile(nc, es, f"m_mm{tg}", [128, 4, 128], F32)
        uu = _tile(nc, es, f"m_uu{tg}", [128, 4, 128], F32)
        hs = _tile(nc, es, f"m_hs{tg}", [128, 4, 128], F32)
        hp = _tile(nc, es, f"m_hp{tg}", [128, 4, 16], F32)
        g1 = _tile(nc, es, f"m_g1{tg}", [128, 4, 128], F32)
        g2 = _tile(nc, es, f"m_g2{tg}", [128, 4, 128], F32)
        yl = _tile(nc, es, f"m_yl{tg}", [128, DL], F32)
        cs = _tile(nc, es, f"m_cs{tg}", [128, 2, 16], F32)
        cqn = _tile(nc, es, f"m_cqn{tg}", [128, QL], BF16)
        cqT = _tile(nc, es, f"m_cqT{tg}", [128, 2, 128], BF16)
        qf = _tile(nc, es, f"m_qf{tg}", [128, 8, 96], F32)
        qb = _tile(nc, es, f"m_qb{tg}", [128, 8, 96], BF16)
        rt = _tile(nc, es, f"m_rt{tg}", [128, 4, 8, 16], F32)
        ckv = _tile(nc, es, f"m_ckv{tg}", [128, KVL], F32)
        ckb = _tile(nc, es, f"m_ckb{tg}", [128, KVL], BF16)
        krf = _tile(nc, es, f"m_krf{tg}", [128, RP], F32)
        krt = _tile(nc, es, f"m_krt{tg}", [128, 4, 16], F32)
        kq = _tile(nc, es, f"m_kq{tg}", [128, 96], BF16)
        p_xl = _psum(nc, es, f"m_pxl{tg}", [128, 4, 128], F32)
        p_gl = _psum(nc, es, f"m_pgl{tg}", [128, 4, 128], F32)
        p_ga = _psum(nc, es, f"m_pga{tg}", [128, 4, 128], F32)
        p_gi = _psum(nc, es, f"m_pgi{tg}", [128, 4, 128], F32)
        p_zc = _psum(nc, es, f"m_pzc{tg}", [128, 512], F32)
        p_q = _psum(nc, es, f"m_pq{tg}", [128, 2, 512], F32)
        p_tb = _psum(nc, es, f"m_ptb{tg}", [128, 8, 128], BF16)
        p_tf = p_xl
        chv, cch = W['chv'], W['cch']
        identb, identf = W['identb'], W['identf']
        k_ = lambda n: (n, tg)
        P.op('dve', lambda: nc.vector.memset(kq[:, :], 0.0), w=[k_('kq')])
        if not sample:
            P.op('dve', lambda: nc.vector.memset(xlx[:, :, :], 0.0), w=[k_('xlx')])
            P.op('dve', lambda: nc.vector.memset(hp[:, :, :], 0.0), w=[k_('hp')])
        else:
            cst = _tile(nc, es, "m_cst", [48, DL], F32)
            h0t = _tile(nc, es, "m_h0t", [16, DL], F32)
            P.dma('sp', lambda: nc.sync.dma_start(out=cst[0:NB * 3, :], in_=I['conv0'].rearrange("b k d -> (b k) d")), w=['cst'])
            P.dma('sp', lambda: nc.sync.dma_start(out=h0t[0:NB, :], in_=I['h0']), w=['h0t'])
            for j in range(4):
                P.op('pe', lambda: nc.tensor.transpose(out=p_xl[:, j, 0:NB * 3], in_=cst[0:NB * 3, j * 128:(j + 1) * 128],
                                                       identity=identf[0:NB * 3, 0:NB * 3]),
                     r=['cst', 'identf'], w=[k_('pxl')], inc=(j == 3))
            P.op('act', lambda: nc.scalar.copy(
                out=xlx[:, :, :].rearrange("p j (b t) -> p j b t", t=11)[:, :, 0:NB, 0:3],
                in_=p_xl[:, :, 0:NB * 3].rearrange("p j (b k) -> p j b k", k=3)), r=[k_('pxl')], w=[k_('xlx')])
            for j in range(4):
                P.op('pe', lambda: nc.tensor.transpose(out=p_gl[:, j, 0:NB], in_=h0t[0:NB, j * 128:(j + 1) * 128],
                                                       identity=identf[0:NB, 0:NB]),
                     r=['h0t', 'identf'], w=[k_('pgl')], inc=(j == 3))
            P.op('act', lambda: nc.scalar.copy(out=hp[:, :, 0:NB], in_=p_gl[:, :, 0:NB]), r=[k_('pgl')], w=[k_('hp')])
        for blk in range(nblk):
            r0 = SEQ if sample else blk * 128
            nt = NS if sample else 128
            P.dma('sp', lambda: nc.sync.dma_start(out=x1[0:nt, :], in_=S['X1'][r0:r0 + nt, :]),
                  r=[('X1', 's' if sample else blk)], w=[k_('x1')])
            P.dma('sp', lambda: nc.sync.dma_start(out=cs[0:nt, 0, :], in_=(I['coss'] if sample else I['cosp'][r0:r0 + nt, :])), w=[k_('cs')])
            P.dma('sp', lambda: nc.sync.dma_start(out=cs[0:nt, 1, :], in_=(I['sins'] if sample else I['sinp'][r0:r0 + nt, :])), w=[k_('cs')])
            P.op('act', lambda: nc.scalar.activation(out=junk[:, 0:D], in_=x1[:, :], func=AF.Square, accum_out=st[:, 0:1]),
                 r=[k_('x1')], w=[k_('junk'), k_('st0')])
            P.op('act', lambda: nc.scalar.activation(out=st[:, 0:1], in_=st[:, 0:1], func=AF.Sqrt, bias=EPS, scale=1.0 / D),
                 r=[k_('st0')], w=[k_('st0')])
            P.op('dve', lambda: nc.vector.reciprocal(out=st[:, 0:1], in_=st[:, 0:1]), r=[k_('st0')], w=[k_('st0')])
            P.op('dve', lambda: nc.vector.tensor_scalar(out=xn[:, :], in0=x1[:, :], scalar1=st[:, 0:1], scalar2=None, op0=ALU.mult),
                 r=[k_('x1'), k_('st0')], w=[k_('xn')])
            for k in range(8):
                P.op('pe', lambda: nc.tensor.transpose(out=p_tb[:, k, :], in_=xn[:, k * 128:(k + 1) * 128], identity=identb[:, :]),
                     r=[k_('xn'), 'identb'], w=[k_('ptb')], inc=(k == 7))
            P.op('act', lambda: nc.scalar.copy(out=xnT[:, :, :], in_=p_tb[:, :, :]), r=[k_('ptb')], w=[k_('xnT')])
            for (pp, c0, key) in ((p_xl, 0, 'pxl'), (p_gl, 512, 'pgl')):
                for j in range(4):
                    for k in range(8):
                        P.op('pe', lambda: nc.tensor.matmul(pp[:, j, :], lhsT=W['win'][:, k, c0 + j * 128:c0 + (j + 1) * 128],
                                                            rhs=xnT[:, k, :], start=(k == 0), stop=(k == 7)),
                             r=['Wsmall', k_('xnT')], w=[k_(key)], inc=(j == 3 and k == 7))
            for k in range(8):
                P.op('pe', lambda: nc.tensor.matmul(p_zc[:, 0:416], lhsT=xnT[:, k, :], rhs=W['win'][:, k, 1024:1440],
                                                    start=(k == 0), stop=(k == 7)),
                     r=['Wsmall', k_('xnT')], w=[k_('pzc')], inc=(k == 7))
            if sample:
                xv = xlx[:, :, :].rearrange("p j (b t) -> p j b t", t=11)
                P.op('act', lambda: nc.scalar.copy(out=xv[:, :, 0:NB, 3:11], in_=p_xl[:, :, :].rearrange("p j (b t) -> p j b t", t=8)),
                     r=[k_('pxl')], w=[k_('xlx')])
                sh = lambda j, k: xv[:, j, 0:NB, k:k + 8]
                ov = lambda t, j: t[:, j, :].rearrange("p (b t) -> p b t", t=8)
            else:
                P.op('act', lambda: nc.scalar.copy(out=xlx[:, :, 3:131], in_=p_xl[:, :, :]), r=[k_('pxl')], w=[k_('xlx')])
                sh = lambda j, k: xlx[:, j, k:k + 128]
                ov = lambda t, j: t[:, j, :]
            for j in range(4):
                P.op('dve', lambda: nc.vector.tensor_scalar(out=ov(xc, j), in0=sh(j, 0), scalar1=chv[:, j * 4:j * 4 + 1],
                                                            scalar2=chv[:, 16 + j:17 + j], op0=ALU.mult, op1=ALU.add),
                     r=[k_('xlx'), 'Wsmall'], w=[k_('xc')])
                for k in range(1, 4):
                    P.op('dve', lambda: nc.vector.scalar_tensor_tensor(out=ov(xc, j), in0=sh(j, k), scalar=chv[:, j * 4 + k:j * 4 + k + 1],
                                                                       in1=ov(xc, j), op0=ALU.mult, op1=ALU.add),
                         r=[k_('xlx'), 'Wsmall', k_('xc')], w=[k_('xc')])
            if not sample:
                P.op('dve', lambda: nc.vector.tensor_copy(out=xlx[:, :, 0:3], in_=xlx[:, :, 128:131]), r=[k_('xlx')], w=[k_('xlx')])
            P.op('dve', lambda: nc.vector.tensor_copy(out=xcb[:, :, :], in_=xc[:, :, :]), r=[k_('xc')], w=[k_('xcb')])
            for (pp, gi, key) in ((p_ga, 0, 'pga'), (p_gi, 1, 'pgi')):
                for j in range(4):
                    P.op('pe', lambda: nc.tensor.matmul(pp[:, j, :], lhsT=W['bd'][:, gi, j, :], rhs=xcb[:, j, :], start=True, stop=True),
                         r=['Wsmall', k_('xcb')], w=[k_(key)], inc=(j == 3))
            for j in range(4):
                P.op('act', lambda: nc.scalar.activation(out=rr[:, j, :], in_=p_ga[:, j, :], func=AF.Sigmoid, bias=chv[:, 20 + j:21 + j], scale=1.0),
                     r=[k_('pga'), 'Wsmall'], w=[k_('rr')])
                P.op('act', lambda: nc.scalar.activation(out=ii[:, j, :], in_=p_gi[:, j, :], func=AF.Sigmoid, bias=chv[:, 24 + j:25 + j], scale=1.0),
                     r=[k_('pgi'), 'Wsmall'], w=[k_('ii')])
            for j in range(4):
                P.op('act', lambda: nc.scalar.activation(out=aa[:, j, :], in_=rr[:, j, :], func=AF.Exp, scale=cch[:, j:j + 1]),
                     r=[k_('rr'), 'Wsmall'], w=[k_('aa')])
            P.op('act', lambda: nc.scalar.activation(out=mm[:, :, :], in_=aa[:, :, :], func=AF.Square), r=[k_('aa')], w=[k_('mm')])
            P.op('act', lambda: nc.scalar.activation(out=mm[:, :, :], in_=mm[:, :, :], func=AF.Sqrt, bias=1.0, scale=-1.0),
                 r=[k_('mm')], w=[k_('mm')])
            P.op('dve', lambda: nc.vector.tensor_tensor(out=uu[:, :, :], in0=ii[:, :, :], in1=xc[:, :, :], op=ALU.mult),
                 r=[k_('ii'), k_('xc')], w=[k_('uu')])
            P.op('dve', lambda: nc.vector.tensor_tensor(out=uu[:, :, :], in0=uu[:, :, :], in1=mm[:, :, :], op=ALU.mult),
                 r=[k_('uu'), k_('mm')], w=[k_('uu')])
            if sample:
                av = aa[:, :, :].rearrange("p j (b t) -> p j b t", t=8)
                uv = uu[:, :, :].rearrange("p j (b t) -> p j b t", t=8)
                hv = hs[:, :, :].rearrange("p j (b t) -> p j b t", t=8)
                for t in range(8):
                    prev = hp[:, :, 0:NB] if t == 0 else hv[:, :, 0:NB, t - 1]
                    P.op('dve', lambda: nc.vector.tensor_tensor(out=hv[:, :, 0:NB, t], in0=av[:, :, 0:NB, t], in1=prev, op=ALU.mult),
                         r=[k_('aa'), k_('hp'), k_('hs')], w=[k_('hs')])
                    P.op('dve', lambda: nc.vector.tensor_tensor(out=hv[:, :, 0:NB, t], in0=hv[:, :, 0:NB, t], in1=uv[:, :, 0:NB, t], op=ALU.add),
                         r=[k_('uu'), k_('hs')], w=[k_('hs')])
            else:
                for j in range(4):
                    P.op('dve', lambda: nc.vector.tensor_tensor_scan(out=hs[:, j, :], data0=aa[:, j, :], data1=uu[:, j, :],
                                                                    initial=hp[:, j, 0:1], op0=ALU.mult, op1=ALU.add),
                         r=[k_('aa'), k_('uu'), k_('hp')], w=[k_('hs')])
                P.op('dve', lambda: nc.vector.tensor_copy(out=hp[:, :, 0:1], in_=hs[:, :, 127:128]), r=[k_('hs')], w=[k_('hp')])
            P.op('act', lambda: nc.scalar.activation(out=g1[:, :, :], in_=p_gl[:, :, :], func=AF.Square), r=[k_('pgl')], w=[k_('g1')])
            P.op('dve', lambda: nc.vector.tensor_scalar(out=g1[:, :, :], in0=g1[:, :, :], scalar1=0.044715, scalar2=1.0, op0=ALU.mult, op1=ALU.add),
                 r=[k_('g1')], w=[k_('g1')])
            P.op('dve', lambda: nc.vector.tensor_tensor(out=g1[:, :, :], in0=g1[:, :, :], in1=p_gl[:, :, :], op=ALU.mult),
                 r=[k_('g1'), k_('pgl')], w=[k_('g1')])
            P.op('act', lambda: nc.scalar.activation(out=g2[:, :, :], in_=g1[:, :, :], func=AF.Sigmoid, scale=1.5957691216057308),
                 r=[k_('g1')], w=[k_('g2')])
            P.op('dve', lambda: nc.vector.tensor_tensor(out=g2[:, :, :], in0=g2[:, :, :], in1=p_gl[:, :, :], op=ALU.mult),
                 r=[k_('g2'), k_('pgl')], w=[k_('g2')])
            P.op('dve', lambda: nc.vector.tensor_tensor(out=g2[:, :, :], in0=g2[:, :, :], in1=hs[:, :, :], op=ALU.mult),
                 r=[k_('g2'), k_('hs')], w=[k_('g2')])
            for j in range(4):
                P.op('pe', lambda: nc.tensor.transpose(out=p_tf[:, j, :], in_=g2[:, j, :], identity=identf[:, :]),
                     r=[k_('g2'), 'identf'], w=[k_('pxl')], inc=(j == 3))
            P.op('act', lambda: nc.scalar.copy(out=yl[:, :], in_=p_tf[:, :, :].rearrange("p j t -> p (j t)")), r=[k_('pxl')], w=[k_('yl')])
            P.dma('pool', lambda: nc.gpsimd.dma_start(out=S['YL'][r0:r0 + nt, :], in_=yl[0:nt, :]), r=[k_('yl')], w=[('YL', 's' if sample else blk)])
            P.op('act', lambda: nc.scalar.activation(out=junk[:, 0:QL], in_=p_zc[:, 0:QL], func=AF.Square, accum_out=st[:, 1:2]),
                 r=[k_('pzc')], w=[k_('junk'), k_('st1')])
            P.op('act', lambda: nc.scalar.activation(out=junk[:, 0:KVL], in_=p_zc[:, QL:QL + KVL], func=AF.Square, accum_out=st[:, 2:3]),
                 r=[k_('pzc')], w=[k_('junk'), k_('st1')])
            P.op('act', lambda: nc.scalar.activation(out=st[:, 1:2], in_=st[:, 1:2], func=AF.Sqrt, bias=EPS, scale=1.0 / QL), r=[k_('st1')], w=[k_('st1')])
            P.op('act', lambda: nc.scalar.activation(out=st[:, 2:3], in_=st[:, 2:3], func=AF.Sqrt, bias=EPS, scale=1.0 / KVL), r=[k_('st1')], w=[k_('st1')])
            P.op('dve', lambda: nc.vector.reciprocal(out=st[:, 1:3], in_=st[:, 1:3]), r=[k_('st1')], w=[k_('st1')])
            P.op('dve', lambda: nc.vector.tensor_scalar(out=cqn[:, :], in0=p_zc[:, 0:QL], scalar1=st[:, 1:2], scalar2=None, op0=ALU.mult),
                 r=[k_('pzc'), k_('st1')], w=[k_('cqn')])
            for k in range(2):
                P.op('pe', lambda: nc.tensor.transpose(out=p_tb[:, k, :], in_=cqn[:, k * 128:(k + 1) * 128], identity=identb[:, :]),
                     r=[k_('cqn'), 'identb'], w=[k_('ptb')], inc=(k == 1))
            P.op('act', lambda: nc.scalar.copy(out=cqT[:, :, :], in_=p_tb[:, 0:2, :]), r=[k_('ptb')], w=[k_('cqT')])
            for (c0, c1, hb) in ((0, 512, 0), (512, 768, 1)):
                for k in range(2):
                    P.op('pe', lambda: nc.tensor.matmul(p_q[:, hb, 0:c1 - c0], lhsT=cqT[:, k, :], rhs=W['wq'][:, k, c0:c1], start=(k == 0), stop=(k == 1)),
                         r=['Wsmall', k_('cqT')], w=[k_('pq')], inc=(k == 1 and hb == 1))
            qfl = qf[:, :, :].rearrange("p h e -> p (h e)")
            P.op('act', lambda: nc.scalar.copy(out=qfl[:, 0:512], in_=p_q[:, 0, :]), r=[k_('pq')], w=[k_('qf')])
            P.op('act', lambda: nc.scalar.copy(out=qfl[:, 512:768], in_=p_q[:, 1, 0:256]), r=[k_('pq')], w=[k_('qf')])
            P.op('dve', lambda: nc.vector.tensor_copy(out=qb[:, :, 0:64], in_=qf[:, :, 0:64]), r=[k_('qf')], w=[k_('qb')])
            cosb = cs[:, 0, :].unsqueeze(1).to_broadcast([128, 8, 16])
            sinb = cs[:, 1, :].unsqueeze(1).to_broadcast([128, 8, 16])
            q1, q2 = qf[:, :, 64:80], qf[:, :, 80:96]
            for (ti, a_, b_) in ((0, q1, cosb), (1, q2, sinb), (2, q2, cosb), (3, q1, sinb)):
                P.op('dve', lambda: nc.vector.tensor_tensor(out=rt[:, ti, :, :], in0=a_, in1=b_, op=ALU.mult),
                     r=[k_('qf'), k_('cs')], w=[k_('rt')])
            P.op('dve', lambda: nc.vector.tensor_tensor(out=qb[:, :, 64:80], in0=rt[:, 0, :, :], in1=rt[:, 1, :, :], op=ALU.subtract),
                 r=[k_('rt')], w=[k_('qb')])
            P.op('dve', lambda: nc.vector.tensor_tensor(out=qb[:, :, 80:96], in0=rt[:, 2, :, :], in1=rt[:, 3, :, :], op=ALU.add),
                 r=[k_('rt')], w=[k_('qb')])
            P.dma('pool', lambda: nc.gpsimd.dma_start(out=S['QS'][r0:r0 + nt, :], in_=qb[0:nt, :, :].rearrange("p h e -> p (h e)")),
                  r=[k_('qb')], w=[('QS', 's' if sample else blk)])
            P.op('dve', lambda: nc.vector.scalar_tensor_tensor(out=ckv[:, :], in0=p_zc[:, QL:QL + KVL], scalar=st[:, 2:3], in1=W['kvg'][:, :],
                                                               op0=ALU.mult, op1=ALU.mult),
                 r=[k_('pzc'), k_('st1'), 'Wsmall'], w=[k_('ckv')])
            P.dma('pool', lambda: nc.gpsimd.dma_start(out=(O['kvs'] if sample else O['kvp'][r0:r0 + nt, :]), in_=ckv[0:nt, :]), r=[k_('ckv')], w=[('okv', tg, blk)])
            P.op('dve', lambda: nc.vector.tensor_copy(out=ckb[:, :], in_=ckv[:, :]), r=[k_('ckv')], w=[k_('ckb')])
            k1, k2 = p_zc[:, 384:400], p_zc[:, 400:416]
            for (ti, a_, b_) in ((0, k1, cs[:, 0, :]), (1, k2, cs[:, 1, :]), (2, k2, cs[:, 0, :]), (3, k1, cs[:, 1, :])):
                P.op('dve', lambda: nc.vector.tensor_tensor(out=krt[:, ti, :], in0=a_, in1=b_, op=ALU.mult),
                     r=[k_('pzc'), k_('cs')], w=[k_('krt')])
            P.op('dve', lambda: nc.vector.tensor_tensor(out=krf[:, 0:16], in0=krt[:, 0, :], in1=krt[:, 1, :], op=ALU.subtract), r=[k_('krt')], w=[k_('krf')])
            P.op('dve', lambda: nc.vector.tensor_tensor(out=krf[:, 16:32], in0=krt[:, 2, :], in1=krt[:, 3, :], op=ALU.add), r=[k_('krt')], w=[k_('krf')])
            P.dma('pool', lambda: nc.gpsimd.dma_start(out=(O['krs'] if sample else O['krp'][r0:r0 + nt, :]), in_=krf[0:nt, :]), r=[k_('krf')], w=[('okr', tg, blk)])
            P.op('dve', lambda: nc.vector.tensor_copy(out=kq[:, 64:96], in_=krf[:, :]), r=[k_('krf'), k_('kq')], w=[k_('kq')])
            P.op('pe', lambda: nc.tensor.transpose(out=p_tb[:, 0, :], in_=ckb[:, :], identity=identb[:, :]), r=[k_('ckb'), 'identb'], w=[k_('ptb')], inc=False)
            P.op('pe', lambda: nc.tensor.transpose(out=p_tb[0:96, 1, :], in_=kq[:, :], identity=identb[:, :]), r=[k_('kq'), 'identb'], w=[k_('ptb')])
            if sample:
                P.op('act', lambda: nc.scalar.copy(out=St['ckvT_s'][:, :], in_=p_tb[:, 0, :]), r=[k_('ptb')], w=['ckvT_s'])
                P.op('act', lambda: nc.scalar.copy(out=St['krT_s'][64:96, :], in_=p_tb[64:96, 1, :]), r=[k_('ptb')], w=['krT_s'])
                P.dma('pool', lambda: nc.gpsimd.dma_start(out=S['CKS'], in_=ckb[0:nt, :]), r=[k_('ckb')], w=['CKS'])
            else:
                P.op('act', lambda: nc.scalar.copy(out=St['ckvT'][:, blk * 128:(blk + 1) * 128], in_=p_tb[:, 0, :]), r=[k_('ptb')], w=[('ckvT', blk)])
                for kb in range(2):
                    P.op('act', lambda: nc.scalar.copy(out=St['KhT'][kb][64:96, blk * 128:(blk + 1) * 128], in_=p_tb[64:96, 1, :]),
                         r=[k_('ptb')], w=[('KhT', kb, blk)])
        if sample:
            for j in range(4):
                P.op('pe', lambda: nc.tensor.transpose(out=p_ga[0:NB, j, :], in_=hs[:, j, :].rearrange("p (b t) -> p b t", t=8)[:, 0:NB, 7],
                                                       identity=identf[:, :]), r=[k_('hs'), 'identf'], w=[k_('pga')], inc=(j == 3))
            P.op('act', lambda: nc.scalar.copy(out=yl[0:NB, :], in_=p_ga[0:NB, :, :].rearrange("p j t -> p (j t)")), r=[k_('pga')], w=[k_('yl')])
            P.dma('pool', lambda: nc.gpsimd.dma_start(out=O['hs_'], in_=yl[0:NB, :]), r=[k_('yl')], w=['o_hs'])
            for k in range(3):
                for j in range(4):
                    P.op('pe', lambda: nc.tensor.transpose(out=p_gi[0:NB, j, :], in_=xlx[:, j, :].rearrange("p (b t) -> p b t", t=11)[:, 0:NB, 8 + k],
                                                           identity=identf[:, :]), r=[k_('xlx'), 'identf'], w=[k_('pgi')], inc=(j == 3))
                P.op('act', lambda: nc.scalar.copy(out=junk[0:NB, k * DL:(k + 1) * DL], in_=p_gi[0:NB, :, :].rearrange("p j t -> p (j t)")),
                     r=[k_('pgi')], w=[k_('junk')])
            P.dma('pool', lambda: nc.gpsimd.dma_start(out=O['cvs'].rearrange("b k d -> b (k d)"), in_=junk[0:NB, 0:3 * DL]), r=[k_('junk')], w=['o_cvs'])
        else:
            for j in range(4):
                P.op('pe', lambda: nc.tensor.transpose(out=p_ga[0:1, j, :], in_=hs[:, j, 127:128], identity=identf[:, :]),
                     r=[k_('hs'), 'identf'], w=[k_('pga')], inc=(j == 3))
            P.op('act', lambda: nc.scalar.copy(out=yl[0:1, :], in_=p_ga[0:1, :, :].rearrange("p j t -> p (j t)")), r=[k_('pga')], w=[k_('yl')])
            P.dma('pool', lambda: nc.gpsimd.dma_start(out=O['hp'], in_=yl[0:1, :]), r=[k_('yl')], w=['o_hp'])
            for j in range(4):
                P.op('pe', lambda: nc.tensor.transpose(out=p_gi[0:3, j, :], in_=xlx[:, j, 0:3], identity=identf[:, :]),
                     r=[k_('xlx'), 'identf'], w=[k_('pgi')], inc=(j == 3))
            P.op('act', lambda: nc.scalar.copy(out=junk[0:3, 0:DL], in_=p_gi[0:3, :, :].rearrange("p j t -> p (j t)")), r=[k_('pgi')], w=[k_('junk')])
            P.dma('pool', lambda: nc.gpsimd.dma_start(out=O['cvp'], in_=junk[0:3, 0:DL]), r=[k_('junk')], w=['o_cvp'])
    P.barrier()


def emit_outproj(P, nc, es, W, tg, yl_f, ym_f, x1g, x2_dst, x2_keys, r_keys):
    k_ = lambda n: (n, 'op' + tg)
    T = W['op_tiles']
    junk, st, yc, ycT, p_tb, p_o = T['junk'], T['st'], T['yc'], T['ycT'], T['p_tb'], T['p_o']
    for (i, src) in ((0, yl_f), (1, ym_f)):
        P.op('act', lambda: nc.scalar.activation(out=junk[:, :], in_=src, func=AF.Square, accum_out=st[:, i:i + 1]),
             r=r_keys, w=[k_('junk'), k_('st')])
    P.op('act', lambda: nc.scalar.activation(out=st[:, 0:2], in_=st[:, 0:2], func=AF.Sqrt, bias=EPS, scale=1.0 / DL), r=[k_('st')], w=[k_('st')])
    P.op('dve', lambda: nc.vector.reciprocal(out=st[:, 0:2], in_=st[:, 0:2]), r=[k_('st')], w=[k_('st')])
    for (i, src) in ((0, yl_f), (1, ym_f)):
        P.op('dve', lambda: nc.vector.tensor_scalar(out=yc[:, i * DL:(i + 1) * DL], in0=src, scalar1=st[:, i:i + 1], scalar2=None, op0=ALU.mult),
             r=r_keys + [k_('st')], w=[k_('yc')])
    for k in range(8):
        P.op('pe', lambda: nc.tensor.transpose(out=p_tb[:, k, :], in_=yc[:, k * 128:(k + 1) * 128], identity=W['identb'][:, :]),
             r=[k_('yc'), 'identb'], w=[k_('ptb')], inc=(k == 7))
    P.op('act', lambda: nc.scalar.copy(out=ycT[:, :, :], in_=p_tb[:, :, :]), r=[k_('ptb')], w=[k_('ycT')])
    for half in range(2):
        for k in range(8):
            P.op('pe', lambda: nc.tensor.matmul(p_o[:, half, :], lhsT=ycT[:, k, :], rhs=W['wout'][:, k, half * 512:(half + 1) * 512],
                                                start=(k == 0), stop=(k == 7)),
                 r=[k_('ycT'), 'Wsmall'], w=[k_('po')], inc=(half == 1 and k == 7))
    P.op('dve', lambda: nc.vector.tensor_tensor(out=x1g, in0=x1g, in1=p_o[:, :, :].rearrange("p a b -> p (a b)"), op=ALU.add),
         r=r_keys + [k_('po')], w=r_keys[-1:])
    P.dma('pool', lambda: nc.gpsimd.dma_start(out=x2_dst, in_=x1g), r=r_keys[-1:], w=x2_keys)


def emit_prompt_attn(P, nc, W, St, I, S, cfg):
    SEQ = cfg['SEQ']
    NBLK = SEQ // 128
    NSLOT = NBLK // 4
    scale = 96.0 ** -0.5
    with ExitStack() as es:
        oidx = _tile(nc, es, "a_oidx", [128, NSLOT], I32)
        mk = _tile(nc, es, "a_mk", [128, 4, 128], BF16)
        mkf = _tile(nc, es, "a_mkf", [128, 4, 128], F32)
        qg = _tile(nc, es, "a_qg", [128, NSLOT, 768], BF16)
        ym = _tile(nc, es, "a_ym", [128, NSLOT, DL], BF16)
        Vh = [_tile(nc, es, f"a_Vh{i}", [128, NBLK, 65], BF16) for i in range(2)]
        qT = [_tile(nc, es, f"a_qT{i}", [96, 128], BF16) for i in range(2)]
        pT = [_tile(nc, es, f"a_pT{i}", [128, 4, 128], BF16) for i in range(2)]
        rc = _tile(nc, es, "a_rc", [128, 2], F32)
        ylg = _tile(nc, es, "a_ylg", [128, DL], F32)
        x1g = _tile(nc, es, "a_x1g", [128, D], F32)
        T = dict(junk=_tile(nc, es, "a_junk", [128, DL], F32), st=_tile(nc, es, "a_st", [128, 2], F32),
                 yc=_tile(nc, es, "a_yc", [128, D], BF16), ycT=_tile(nc, es, "a_ycT", [128, 8, 128], BF16),
                 p_tb=_psum(nc, es, "a_ptb", [128, 8, 128], BF16), p_o=_psum(nc, es, "a_po", [128, 2, 512], F32))
        W['op_tiles'] = T
        p_s = [_psum(nc, es, f"a_ps{i}", [128, 4, 128], F32) for i in range(2)]
        p_k = _psum(nc, es, "a_pk", [128, 512], F32)
        p_ov = [_psum(nc, es, f"a_pov{i}", [128, 512], F32) for i in range(2)]
        P.dma('sp', lambda: nc.sync.dma_start(out=oidx[:, :], in_=I['own_idx']), w=['oidx'])
        P.dma('sp', lambda: nc.sync.dma_start(out=mkf[:, :, :], in_=I['masks']), w=['mkf'])
        P.op('dve', lambda: nc.vector.tensor_copy(out=mk[:, :, :], in_=mkf[:, :, :]), r=['mkf'], w=['mk'])
        oix = [_tile(nc, es, f"a_oix{i}", [128, 1], I32) for i in range(NSLOT)]
        for s in range(NSLOT):
            P.op('dve', lambda: nc.vector.tensor_copy(out=oix[s][:, :], in_=oidx[:, s:s + 1]), r=['oidx'], w=[('oix', s)])
            P.dma('pool', lambda: nc.gpsimd.indirect_dma_start(
                out=qg[:, s, :], out_offset=None, in_=S['QS'], in_offset=bass.IndirectOffsetOnAxis(ap=oix[s][:, 0:1], axis=0)),
                r=[('oix', s)] + [('QS', b) for b in range(NBLK)], w=[('qg', s)])
        for i in range(2):
            P.op('dve', lambda: nc.vector.memset(Vh[i][:, :, 64:65], 1.0), w=[('Vh', i)])
        cnt = 0
        for h in range(NH):
            hb = h % 2
            for c in range(0, SEQ, 512):
                wdt = min(512, SEQ - c)
                P.op('pe', lambda: nc.tensor.matmul(p_k[0:64, 0:wdt], lhsT=W['wuk'][:, h, :], rhs=St['ckvT'][:, c:c + wdt], start=True, stop=True),
                     r=['Wsmall'] + [('ckvT', b) for b in range(c // 128, (c + wdt) // 128)], w=['pk'])
                P.op('act', lambda: nc.scalar.copy(out=St['KhT'][hb][0:64, c:c + wdt], in_=p_k[0:64, 0:wdt]), r=['pk'],
                     w=[('KhT', hb, b) for b in range(c // 128, (c + wdt) // 128)])
            for b0 in range(0, NBLK, 8):
                nb = min(8, NBLK - b0)
                pv = p_ov[0]
                for i in range(nb):
                    P.op('pe', lambda: nc.tensor.matmul(pv[:, i * 64:(i + 1) * 64], lhsT=St['ckvT'][:, (b0 + i) * 128:(b0 + i + 1) * 128],
                                                        rhs=W['wuv'][:, h, :], start=True, stop=True),
                         r=['Wsmall', ('ckvT', b0 + i)], w=['pov0'], inc=(i == nb - 1))
                P.op('dve', lambda: nc.vector.tensor_copy(out=Vh[hb][:, b0:b0 + nb, 0:64], in_=pv[:, 0:nb * 64].rearrange("p (b v) -> p b v", v=64)),
                     r=['pov0'], w=[('Vh', hb)])
            for s in range(NSLOT):
                qb_ = cnt % 2
                cnt += 1
                P.op('pe', lambda: nc.tensor.transpose(out=T['p_tb'][0:96, 0, :], in_=qg[:, s, h * 96:(h + 1) * 96], identity=W['identb'][:, :]),
                     r=[('qg', s), 'identb'], w=[('ptb', 'op')])
                P.op('act', lambda: nc.scalar.copy(out=qT[qb_][:, :], in_=T['p_tb'][0:96, 0, :]), r=[('ptb', 'op')], w=[('qT', qb_)])
                nkb = 4 * s + 4
                po = p_ov[1]
                for g in range(nkb // 4):
                    sb = (cnt + g) % 2
                    for i in range(4):
                        kb = 4 * g + i
                        P.op('pe', lambda: nc.tensor.matmul(p_s[sb][:, i, :], lhsT=St['KhT'][hb][0:96, kb * 128:(kb + 1) * 128], rhs=qT[qb_][:, :],
                                                            start=True, stop=True),
                             r=[('KhT', hb, kb), ('qT', qb_)], w=[('ps', sb)], inc=(i == 3))
                    P.op('act', lambda: nc.scalar.activation(out=pT[sb][:, :, :], in_=p_s[sb][:, :, :], func=AF.Exp, scale=scale),
                         r=[('ps', sb)], w=[('pT', sb)])
                    if g == nkb // 4 - 1:
                        P.op('dve', lambda: nc.vector.tensor_tensor(out=pT[sb][:, :, :], in0=pT[sb][:, :, :], in1=mk[:, :, :], op=ALU.mult),
                             r=[('pT', sb), 'mk'], w=[('pT', sb)])
                    for i in range(4):
                        kb = 4 * g + i
                        P.op('pe', lambda: nc.tensor.matmul(po[:, 0:65], lhsT=pT[sb][:, i, :], rhs=Vh[hb][:, kb, :], start=(kb == 0), stop=(kb == nkb - 1)),
                             r=[('pT', sb), ('Vh', hb)], w=['pov1'], inc=(kb == nkb - 1))
                P.op('dve', lambda: nc.vector.reciprocal(out=rc[:, 0:1], in_=po[:, 64:65]), r=['pov1'], w=['rc'])
                P.op('dve', lambda: nc.vector.tensor_scalar(out=ym[:, s, h * 64:(h + 1) * 64], in0=po[:, 0:64], scalar1=rc[:, 0:1], scalar2=None, op0=ALU.mult),
                     r=['pov1', 'rc'], w=[('ym', s)])
        for s in range(NSLOT):
            P.dma('pool', lambda: nc.gpsimd.indirect_dma_start(
                out=ylg[:, :], out_offset=None, in_=S['YL'], in_offset=bass.IndirectOffsetOnAxis(ap=oix[s][:, 0:1], axis=0)),
                r=[('oix', s)] + [('YL', b) for b in range(NBLK)], w=['ylg'])
            P.dma('pool', lambda: nc.gpsimd.indirect_dma_start(
                out=x1g[:, :], out_offset=None, in_=S['X1'], in_offset=bass.IndirectOffsetOnAxis(ap=oix[s][:, 0:1], axis=0)),
                r=[('oix', s)] + [('X1', b) for b in range(NBLK)], w=['x1g'])
            emit_outproj(P, nc, es, W, 'p', ylg[:, :], ym[:, s, :], x1g[:, :], S['X2'][s * 128:(s + 1) * 128, :], [('X2', s)],
                         [('ym', s), 'ylg', 'x1g'])
    P.barrier()


def emit_sample_attn(P, nc, W, St, I, S, cfg, O=None):
    SEQ, NB, NPG = cfg['SEQ'], cfg['NB'], cfg['NPG']
    NS = NB * 8
    NG = NPG // 8
    scale = 96.0 ** -0.5
    with ExitStack() as es:
        qs = _tile(nc, es, "s_qs", [128, 768], BF16)
        qn = _tile(nc, es, "s_qn", [64, 8, 128], BF16)
        QLT = _tile(nc, es, "s_QLT", [128, NB, 8, 8], BF16)
        QRT = _tile(nc, es, "s_QRT", [96, NB, 8, 8], BF16)
        ptr = _tile(nc, es, "s_ptr", [128, NB, NG], I32)
        ptf = _tile(nc, es, "s_ptf", [128, NB, NG], F32)
        qcol = _tile(nc, es, "s_qcol", [128, 1], F32)
        idx = _tile(nc, es, "s_idx", [128, NB, NG], I32)
        G = [_tile(nc, es, f"s_G{i}", [128, 8, 129], BF16) for i in range(2)]
        GR = [_tile(nc, es, f"s_GR{i}", [128, 8, 96], BF16) for i in range(2)]
        Gf = [_tile(nc, es, f"s_Gf{i}", [128, 8, 128], F32) for i in range(2)]
        six = [_tile(nc, es, f"s_six{i}", [128, 1], I32) for i in range(2)]
        GRf = [_tile(nc, es, f"s_GRf{i}", [128, 8, 32], F32) for i in range(2)]
        cT = [_tile(nc, es, f"s_cT{i}", [128, 8, 128], BF16) for i in range(2)]
        rT = [_tile(nc, es, f"s_rT{i}", [96, 8, 128], BF16) for i in range(2)]
        pT = [_tile(nc, es, f"s_pT{i}", [128, 8, 64], BF16) for i in range(2)]
        pTo = _tile(nc, es, "s_pTo", [8, 64], BF16)
        cks = _tile(nc, es, "s_cks", [8, NB, 129], BF16)
        mo = _tile(nc, es, "s_mo", [8, 64], F32)
        on = _tile(nc, es, "s_on", [64, 128], BF16)
        onT = _tile(nc, es, "s_onT", [128, 64], BF16)
        rc = _tile(nc, es, "s_rc", [64, 1], F32)
        ymb = _tile(nc, es, "s_ymb", [8, DL], F32)
        ymg = _tile(nc, es, "s_ymg", [128, DL], F32)
        ylg = _tile(nc, es, "s_ylg", [128, DL], F32)
        x1g = _tile(nc, es, "s_x1g", [128, D], F32)
        T = dict(junk=_tile(nc, es, "s_junk", [128, DL], F32), st=_tile(nc, es, "s_st", [128, 2], F32),
                 yc=_tile(nc, es, "s_yc", [128, D], BF16), ycT=_tile(nc, es, "s_ycT", [128, 8, 128], BF16),
                 p_tb=_psum(nc, es, "s_ptb", [128, 8, 128], BF16), p_o=_psum(nc, es, "s_po", [128, 2, 512], F32))
        W['op_tiles'] = T
        p_tb = T['p_tb']
        p_tr = _psum(nc, es, "s_ptr2", [128, 8, 128], BF16)
        p_s = [_psum(nc, es, f"s_ps{i}", [128, 8, 64], F32) for i in range(2)]
        p_ov = _psum(nc, es, "s_pov", [128, 512], F32)
        p_m = _psum(nc, es, "s_pm", [128, 512], F32)
        identb = W['identb']
        for pl in range(8):
            P.dma('sp', lambda: nc.sync.dma_start(
                out=ptr[pl * 16:(pl + 1) * 16, :, :],
                in_=I['pt_l'][pl:pl + 1, :, :].to_broadcast([16, NB, NG])), w=['s_ptr'])
        P.dma('sp', lambda: nc.sync.dma_start(out=qcol[:, :], in_=I['qcol']), w=['s_qcol'])
        P.dma('sp', lambda: nc.sync.dma_start(out=mo[:, :], in_=I['mask_own']), w=['s_mo'])
        P.op('dve', lambda: nc.vector.tensor_copy(out=ptf[:, :, :], in_=ptr[:, :, :]), r=['s_ptr'], w=['s_ptf'])
        P.op('dve', lambda: nc.vector.tensor_scalar(out=ptf[:, :, :], in0=ptf[:, :, :], scalar1=16.0, scalar2=qcol[:, 0:1], op0=ALU.mult, op1=ALU.add),
             r=['s_ptf', 's_qcol'], w=['s_ptf'])
        P.op('dve', lambda: nc.vector.tensor_copy(out=idx[:, :, :], in_=ptf[:, :, :]), r=['s_ptf'], w=['s_idx'])
        dbg = cfg.get('debug')
        if dbg:
            P.dma('sp', lambda: nc.sync.dma_start(out=O['dbg_idx'], in_=idx[:, :, :].rearrange("p b g -> p (b g)")), r=['s_idx'], w=['dbg_idx'])
            dps = _tile(nc, es, "s_dps", [128, 512], F32)
            dov = _tile(nc, es, "s_dov", [64, 129], F32)
        P.dma('sp', lambda: nc.sync.dma_start(out=qs[:, :], in_=S['QS'][SEQ:SEQ + NS, :]), r=[('QS', 's')], w=['s_qs'])
        P.dma('sp', lambda: nc.sync.dma_start(out=cks[:, :, 0:128], in_=S['CKS'].rearrange("(b t) c -> t b c", t=8)), r=['CKS'], w=['s_cks'])
        P.op('dve', lambda: nc.vector.memset(cks[:, :, 128:129], 1.0), w=['s_cks'])
        for i in range(2):
            P.op('dve', lambda: nc.vector.memset(G[i][:, :, 128:129], 1.0), w=[('s_G', i)])
            P.op('dve', lambda: nc.vector.memset(GR[i][:, :, :], 0.0), w=[('s_GR', i)])
        for h in range(8):
            P.op('pe', lambda: nc.tensor.transpose(out=p_tb[0:96, h, :], in_=qs[:, h * 96:(h + 1) * 96], identity=identb[:, :]),
                 r=['s_qs', 'identb'], w=[('ptb', 'ops')], inc=(h == 7))
        P.op('act', lambda: nc.scalar.copy(out=qn[:, :, :], in_=p_tb[0:64, :, :]), r=[('ptb', 'ops')], w=['s_qn'])
        P.op('act', lambda: nc.scalar.copy(out=QRT[64:96, :, :, :].rearrange("p b h t -> p h b t"),
                                           in_=p_tb[64:96, :, :].rearrange("p h (b t) -> p h b t", t=8)), r=[('ptb', 'ops')], w=['s_QRT'])
        for h in range(8):
            P.op('pe', lambda: nc.tensor.matmul(p_m[:, 0:128], lhsT=W['wukT'][:, h, :], rhs=qn[:, h, :], start=True, stop=True),
                 r=['Wsmall', 's_qn'], w=['s_pm'])
            P.op('act', lambda: nc.scalar.copy(out=QLT[:, :, h, :], in_=p_m[:, 0:128].rearrange("p (b t) -> p b t", t=8)), r=['s_pm'], w=['s_QLT'])
        n = 0
        for b in range(NB):
            qlt = QLT[:, b, :, :].rearrange("p h t -> p (h t)")
            qrt = QRT[64:96, b, :, :].rearrange("p h t -> p (h t)")
            first = True
            for g in range(NG):
                gb = n % 2
                n += 1
                P.op('dve', lambda: nc.vector.tensor_copy(out=six[gb][:, :], in_=idx[:, b, g:g + 1]), r=['s_idx'], w=[('s_six', gb)])
                P.dma('pool', lambda: nc.gpsimd.indirect_dma_start(
                    out=Gf[gb][:, :, :].rearrange("p a c -> p (a c)"), out_offset=None, in_=I['ckv_cache'].rearrange("(r g) c -> r (g c)", g=8),
                    in_offset=bass.IndirectOffsetOnAxis(ap=six[gb][:, 0:1], axis=0)), r=[('s_six', gb)], w=[('s_Gf', gb)])
                P.dma('pool', lambda: nc.gpsimd.indirect_dma_start(
                    out=GRf[gb][:, :, :].rearrange("p a c -> p (a c)"), out_offset=None, in_=I['ckr_cache'].rearrange("(r g) c -> r (g c)", g=8),
                    in_offset=bass.IndirectOffsetOnAxis(ap=six[gb][:, 0:1], axis=0)), r=[('s_six', gb)], w=[('s_GRf', gb)])
                if dbg and b == 0 and g == 0:
                    P.dma('sp', lambda: nc.sync.dma_start(out=O['dbg_G'], in_=Gf[gb][:, :, :].rearrange("p a c -> p (a c)")), r=[('s_Gf', gb)], w=['dbg_G'])
                    P.dma('sp', lambda: nc.sync.dma_start(out=O['dbg_GR'], in_=GRf[gb][:, :, :].rearrange("p a c -> p (a c)")), r=[('s_GRf', gb)], w=['dbg_GR'])
                P.op('dve', lambda: nc.vector.tensor_copy(out=G[gb][:, :, 0:128], in_=Gf[gb][:, :, :]), r=[('s_Gf', gb)], w=[('s_G', gb)])
                P.op('dve', lambda: nc.vector.tensor_copy(out=GR[gb][:, :, 64:96], in_=GRf[gb][:, :, :]), r=[('s_GRf', gb)], w=[('s_GR', gb)])
                for i in range(8):
                    P.op('pe', lambda: nc.tensor.transpose(out=p_tb[:, i, :], in_=G[gb][:, i, 0:128], identity=identb[:, :]),
                         r=[('s_G', gb), 'identb'], w=[('ptb', 'ops')], inc=(i == 7))
                P.op('act', lambda: nc.scalar.copy(out=cT[gb][:, :, :], in_=p_tb[:, :, :]), r=[('ptb', 'ops')], w=[('s_cT', gb)])
                for i in range(8):
                    P.op('pe', lambda: nc.tensor.transpose(out=p_tr[0:96, i, :], in_=GR[gb][:, i, :], identity=identb[:, :]),
                         r=[('s_GR', gb), 'identb'], w=['s_ptr2'], inc=(i == 7))
                P.op('dve', lambda: nc.vector.tensor_copy(out=rT[gb][64:96, :, :], in_=p_tr[64:96, :, :]), r=['s_ptr2'], w=[('s_rT', gb)])
                for i in range(8):
                    P.op('pe', lambda: nc.tensor.matmul(p_s[gb][:, i, :], lhsT=cT[gb][:, i, :], rhs=qlt, start=True, stop=False),
                         r=[('s_cT', gb), 's_QLT'], w=[('s_ps', gb)], inc=False)
                    P.op('pe', lambda: nc.tensor.matmul(p_s[gb][:, i, :], lhsT=rT[gb][64:96, i, :], rhs=qrt, start=False, stop=True),
                         r=[('s_rT', gb), 's_QRT'], w=[('s_ps', gb)], inc=(i == 7))
                if dbg and b == 0 and g == 0:
                    P.op('dve', lambda: nc.vector.tensor_copy(out=dps[:, :], in_=p_s[gb][:, :, :].rearrange("p a c -> p (a c)")), r=[('s_ps', gb)], w=['s_dps'])
                    P.dma('sp', lambda: nc.sync.dma_start(out=O['dbg_ps'], in_=dps[:, :]), r=['s_dps'], w=['dbg_ps'])
                P.op('act', lambda: nc.scalar.activation(out=pT[gb][:, :, :], in_=p_s[gb][:, :, :], func=AF.Exp, scale=scale),
                     r=[('s_ps', gb)], w=[('s_pT', gb)])
                for i in range(8):
                    P.op('pe', lambda: nc.tensor.matmul(p_ov[0:64, 0:129], lhsT=pT[gb][:, i, :], rhs=G[gb][:, i, :], start=first, stop=False),
                         r=[('s_pT', gb), ('s_G', gb)], w=['s_pov'], inc=False)
                    first = False
            P.op('pe', lambda: nc.tensor.matmul(p_m[0:8, 0:64], lhsT=St['ckvT_s'][:, b * 8:(b + 1) * 8], rhs=qlt, start=True, stop=False),
                 r=['ckvT_s', 's_QLT'], w=['s_pm'], inc=False)
            P.op('pe', lambda: nc.tensor.matmul(p_m[0:8, 0:64], lhsT=St['krT_s'][64:96, b * 8:(b + 1) * 8], rhs=qrt, start=False, stop=True),
                 r=['krT_s', 's_QRT'], w=['s_pm'])
            P.op('act', lambda: nc.scalar.activation(out=pTo[:, :], in_=p_m[0:8, 0:64], func=AF.Exp, scale=scale), r=['s_pm'], w=['s_pTo'])
            P.op('dve', lambda: nc.vector.tensor_tensor(out=pTo[:, :], in0=pTo[:, :], in1=mo[:, :], op=ALU.mult), r=['s_pTo', 's_mo'], w=['s_pTo'])
            P.op('pe', lambda: nc.tensor.matmul(p_ov[0:64, 0:129], lhsT=pTo[:, :], rhs=cks[:, b, :], start=False, stop=True),
                 r=['s_pTo', 's_cks'], w=['s_pov'])
            if dbg and b == 0:
                P.op('dve', lambda: nc.vector.tensor_copy(out=dov[:, :], in_=p_ov[0:64, 0:129]), r=['s_pov'], w=['s_dov'])
                P.dma('sp', lambda: nc.sync.dma_start(out=O['dbg_ov'], in_=dov[:, :]), r=['s_dov'], w=['dbg_ov'])
            P.op('dve', lambda: nc.vector.reciprocal(out=rc[:, :], in_=p_ov[0:64, 128:129]), r=['s_pov'], w=['s_rc'])
            P.op('dve', lambda: nc.vector.tensor_scalar(out=on[:, :], in0=p_ov[0:64, 0:128], scalar1=rc[:, 0:1], scalar2=None, op0=ALU.mult),
                 r=['s_pov', 's_rc'], w=['s_on'])
            P.op('pe', lambda: nc.tensor.transpose(out=p_tr[:, 0, 0:64], in_=on[:, :], identity=identb[0:64, 0:64]), r=['s_on', 'identb'], w=['s_ptr2'])
            P.op('act', lambda: nc.scalar.copy(out=onT[:, :], in_=p_tr[:, 0, 0:64]), r=['s_ptr2'], w=['s_onT'])
            for h in range(8):
                P.op('pe', lambda: nc.tensor.matmul(p_o_s(p_m, h), lhsT=onT[:, h * 8:(h + 1) * 8], rhs=W['wuv'][:, h, :], start=True, stop=True),
                     r=['s_onT', 'Wsmall'], w=['s_pm'], inc=(h == 7))
            P.op('act', lambda: nc.scalar.copy(out=ymb[:, :], in_=p_m[0:8, 0:512]), r=['s_pm'], w=['s_ymb'])
            P.dma('pool', lambda: nc.gpsimd.dma_start(out=S['YM'][b * 8:(b + 1) * 8, :], in_=ymb[:, :]), r=['s_ymb'], w=['YM'])
        P.dma('sp', lambda: nc.sync.dma_start(out=ymg[:, :], in_=S['YM']), r=['YM'], w=['s_ymg'])
        if dbg:
            P.dma('sp', lambda: nc.sync.dma_start(out=O['dbg_YM'], in_=ymg[:, :]), r=['s_ymg'], w=['dbg_YM'])
        P.dma('sp', lambda: nc.sync.dma_start(out=ylg[:, :], in_=S['YL'][SEQ:SEQ + NS, :]), r=[('YL', 's')], w=['s_ylg'])
        P.dma('sp', lambda: nc.sync.dma_start(out=x1g[:, :], in_=S['X1'][SEQ:SEQ + NS, :]), r=[('X1', 's')], w=['s_x1g'])
        NOWN = SEQ // 4
        emit_outproj(P, nc, es, W, 's', ylg[:, :], ymg[:, :], x1g[:, :], S['X2'][NOWN:NOWN + NS, :], [('X2', 's')], ['s_ymg', 's_ylg', 's_x1g'])
    P.barrier()


def p_o_s(p_m, h):
    return p_m[0:8, h * 64:(h + 1) * 64]


def emit_mixer_prompt(P, nc, W, St, I, O, S, cfg):
    SEQ = cfg['SEQ']
    nblk = SEQ // 128
    tg = 'p'
    with ExitStack() as es:
        def two(name, shape, dt):
            return [_tile(nc, es, f"mp_{name}{i}", shape, dt) for i in range(2)]
        x1 = two('x1', [128, D], F32)
        junk = _tile(nc, es, "mp_junk", [128, D], F32)
        xn = two('xn', [128, D], BF16)
        xnT = two('xnT', [128, 8, 128], BF16)
        st = two('st', [128, 4], F32)
        xlx = two('xlx', [128, 4, 131], F32)
        gl = two('gl', [128, 4, 128], F32)
        zc = two('zc', [128, 416], F32)
        xc = two('xc', [128, 4, 128], F32)
        xcb = two('xcb', [128, 4, 128], BF16)
        rr = two('rr', [128, 4, 128], F32)
        ii = two('ii', [128, 4, 128], F32)
        aa = two('aa', [128, 4, 128], F32)
        uu = two('uu', [128, 4, 128], F32)
        hs = two('hs', [128, 4, 128], F32)
        hp = _tile(nc, es, "mp_hp", [128, 4, 1], F32)
        halo = _tile(nc, es, "mp_halo", [128, 4, 3], F32)
        g1 = two('g1', [128, 4, 128], F32)
        yl = two('yl', [128, DL], F32)
        cs = two('cs', [128, 2, 16], F32)
        cqn = two('cqn', [128, QL], BF16)
        cqT = two('cqT', [128, 2, 128], BF16)
        qf = two('qf', [128, 8, 96], F32)
        qb = two('qb', [128, 8, 96], BF16)
        rt = two('rt', [128, 4, 8, 16], F32)
        ckv = two('ckv', [128, KVL], F32)
        ckb = two('ckb', [128, KVL], BF16)
        krf = two('krf', [128, RP], F32)
        krt = two('krt', [128, 4, 16], F32)
        kq = two('kq', [128, 96], BF16)
        p_xl = _psum(nc, es, "mp_pxl", [128, 4, 128], F32)
        p_gl = _psum(nc, es, "mp_pgl", [128, 4, 128], F32)
        p_ga = _psum(nc, es, "mp_pga", [128, 4, 128], F32)
        p_gi = _psum(nc, es, "mp_pgi", [128, 4, 128], F32)
        p_zc = _psum(nc, es, "mp_pzc", [128, 512], F32)
        p_q = _psum(nc, es, "mp_pq", [128, 2, 512], F32)
        p_tb = _psum(nc, es, "mp_ptb", [128, 8, 128], BF16)
        chv, cch = W['chv'], W['cch']
        identb, identf = W['identb'], W['identf']
        for i in range(2):
            P.op('dve', lambda: nc.vector.memset(kq[i][:, :], 0.0), w=[('kq', i)])
            P.op('dve', lambda: nc.vector.memset(xlx[i][:, :, :], 0.0), w=[('xlx', i)])
        P.op('dve', lambda: nc.vector.memset(hp[:, :, :], 0.0), w=['hp'])

        def stageA(blk):
            b = blk % 2
            k_ = lambda n: (n, b)
            r0 = blk * 128
            P.dma('sp', lambda: nc.sync.dma_start(out=x1[b][:, :], in_=S['X1'][r0:r0 + 128, :]), r=[('X1', blk)], w=[k_('x1')])
            P.dma('sp', lambda: nc.sync.dma_start(out=cs[b][:, 0, :], in_=I['cosp'][r0:r0 + 128, :]), w=[k_('cs')])
            P.dma('sp', lambda: nc.sync.dma_start(out=cs[b][:, 1, :], in_=I['sinp'][r0:r0 + 128, :]), w=[k_('cs')])
            P.op('act', lambda: nc.scalar.activation(out=junk[:, :], in_=x1[b][:, :], func=AF.Square, accum_out=st[b][:, 0:1]),
                 r=[k_('x1')], w=['junk', k_('st0')])
            P.op('act', lambda: nc.scalar.activation(out=st[b][:, 0:1], in_=st[b][:, 0:1], func=AF.Sqrt, bias=EPS, scale=1.0 / D),
                 r=[k_('st0')], w=[k_('st0')])
            P.op('dve', lambda: nc.vector.reciprocal(out=st[b][:, 0:1], in_=st[b][:, 0:1]), r=[k_('st0')], w=[k_('st0')])
            P.op('dve', lambda: nc.vector.tensor_scalar(out=xn[b][:, :], in0=x1[b][:, :], scalar1=st[b][:, 0:1], scalar2=None, op0=ALU.mult),
                 r=[k_('x1'), k_('st0')], w=[k_('xn')])
            for k in range(8):
                P.op('pe', lambda: nc.tensor.transpose(out=p_tb[:, k, :], in_=xn[b][:, k * 128:(k + 1) * 128], identity=identb[:, :]),
                     r=[k_('xn'), 'identb'], w=['ptb'], inc=(k == 7))
            P.op('act', lambda: nc.scalar.copy(out=xnT[b][:, :, :], in_=p_tb[:, :, :]), r=['ptb'], w=[k_('xnT')])
            for (pp, c0, key) in ((p_xl, 0, 'pxl'), (p_gl, 512, 'pgl')):
                for j in range(4):
                    for k in range(8):
                        P.op('pe', lambda: nc.tensor.matmul(pp[:, j, :], lhsT=W['win'][:, k, c0 + j * 128:c0 + (j + 1) * 128],
                                                            rhs=xnT[b][:, k, :], start=(k == 0), stop=(k == 7)),
                             r=['Wsmall', k_('xnT')], w=[key], inc=(j == 3 and k == 7))
            for k in range(8):
                P.op('pe', lambda: nc.tensor.matmul(p_zc[:, 0:416], lhsT=xnT[b][:, k, :], rhs=W['win'][:, k, 1024:1440],
                                                    start=(k == 0), stop=(k == 7)),
                     r=['Wsmall', k_('xnT')], w=['pzc'], inc=(k == 7))
            P.op('act', lambda: nc.scalar.copy(out=xlx[b][:, :, 3:131], in_=p_xl[:, :, :]), r=['pxl'], w=[k_('xlx')])
            P.op('dve', lambda: nc.vector.tensor_copy(out=gl[b][:, :, :], in_=p_gl[:, :, :]), r=['pgl'], w=[k_('gl')])
            P.op('act', lambda: nc.scalar.copy(out=zc[b][:, :], in_=p_zc[:, 0:416]), r=['pzc'], w=[k_('zc')])

        def stageB1(blk):
            b = blk % 2
            k_ = lambda n: (n, b)
            r0 = blk * 128
            if blk > 0:
                P.op('dve', lambda: nc.vector.tensor_copy(out=xlx[b][:, :, 0:3], in_=halo[:, :, :]),
                     r=['halo', k_('xlx')], w=[k_('xlx')])
                yield
            P.op('dve', lambda: nc.vector.tensor_copy(out=halo[:, :, :], in_=xlx[b][:, :, 128:131]), r=[k_('xlx')], w=['halo'])
            yield
            for j in range(4):
                P.op('dve', lambda: nc.vector.tensor_scalar(out=xc[b][:, j, :], in0=xlx[b][:, j, 0:128], scalar1=chv[:, j * 4:j * 4 + 1],
                                                            scalar2=chv[:, 16 + j:17 + j], op0=ALU.mult, op1=ALU.add),
                     r=[k_('xlx'), 'Wsmall'], w=[k_('xc')])
                yield
                for k in range(1, 4):
                    P.op('dve', lambda: nc.vector.scalar_tensor_tensor(out=xc[b][:, j, :], in0=xlx[b][:, j, k:k + 128], scalar=chv[:, j * 4 + k:j * 4 + k + 1],
                                                                       in1=xc[b][:, j, :], op0=ALU.mult, op1=ALU.add),
                         r=[k_('xlx'), 'Wsmall', k_('xc')], w=[k_('xc')])
                    yield
            P.op('dve', lambda: nc.vector.tensor_copy(out=xcb[b][:, :, :], in_=xc[b][:, :, :]), r=[k_('xc')], w=[k_('xcb')])
            yield
            for (pp, gi, key) in ((p_ga, 0, 'pga'), (p_gi, 1, 'pgi')):
                for j in range(4):
                    P.op('pe', lambda: nc.tensor.matmul(pp[:, j, :], lhsT=W['bd'][:, gi, j, :], rhs=xcb[b][:, j, :], start=True, stop=True),
                         r=['Wsmall', k_('xcb')], w=[key], inc=(j == 3))
                    yield
            for j in range(4):
                P.op('act', lambda: nc.scalar.activation(out=rr[b][:, j, :], in_=p_ga[:, j, :], func=AF.Sigmoid, bias=chv[:, 20 + j:21 + j], scale=1.0),
                     r=['pga', 'Wsmall'], w=[k_('rr')])
                yield
                P.op('act', lambda: nc.scalar.activation(out=ii[b][:, j, :], in_=p_gi[:, j, :], func=AF.Sigmoid, bias=chv[:, 24 + j:25 + j], scale=1.0),
                     r=['pgi', 'Wsmall'], w=[k_('ii')])
                yield
            for j in range(4):
                P.op('act', lambda: nc.scalar.activation(out=aa[b][:, j, :], in_=rr[b][:, j, :], func=AF.Exp, scale=cch[:, j:j + 1]),
                     r=[k_('rr'), 'Wsmall'], w=[k_('aa')])
                yield
            P.op('act', lambda: nc.scalar.activation(out=rr[b][:, :, :], in_=aa[b][:, :, :], func=AF.Square), r=[k_('aa')], w=[k_('rr')])
            yield
            P.op('act', lambda: nc.scalar.activation(out=rr[b][:, :, :], in_=rr[b][:, :, :], func=AF.Sqrt, bias=1.0, scale=-1.0),
                 r=[k_('rr')], w=[k_('rr')])
            yield
            P.op('dve', lambda: nc.vector.tensor_tensor(out=uu[b][:, :, :], in0=ii[b][:, :, :], in1=xc[b][:, :, :], op=ALU.mult),
                 r=[k_('ii'), k_('xc')], w=[k_('uu')])
            yield
            P.op('dve', lambda: nc.vector.tensor_tensor(out=uu[b][:, :, :], in0=uu[b][:, :, :], in1=rr[b][:, :, :], op=ALU.mult),
                 r=[k_('uu'), k_('rr')], w=[k_('uu')])
            yield
            for j in range(4):
                P.op('dve', lambda: nc.vector.tensor_tensor_scan(out=hs[b][:, j, :], data0=aa[b][:, j, :], data1=uu[b][:, j, :],
                                                                initial=hp[:, j, 0:1], op0=ALU.mult, op1=ALU.add),
                     r=[k_('aa'), k_('uu'), 'hp'], w=[k_('hs')])
                yield
            P.op('dve', lambda: nc.vector.tensor_copy(out=hp[:, :, 0:1], in_=hs[b][:, :, 127:128]), r=[k_('hs')], w=['hp'])
            yield
            P.op('act', lambda: nc.scalar.activation(out=g1[b][:, :, :], in_=gl[b][:, :, :], func=AF.Square), r=[k_('gl')], w=[k_('g1')])
            yield
            P.op('dve', lambda: nc.vector.tensor_scalar(out=g1[b][:, :, :], in0=g1[b][:, :, :], scalar1=0.044715, scalar2=1.0, op0=ALU.mult, op1=ALU.add),
                 r=[k_('g1')], w=[k_('g1')])
            yield
            P.op('dve', lambda: nc.vector.tensor_tensor(out=g1[b][:, :, :], in0=g1[b][:, :, :], in1=gl[b][:, :, :], op=ALU.mult),
                 r=[k_('g1'), k_('gl')], w=[k_('g1')])
            yield
            P.op('act', lambda: nc.scalar.activation(out=g1[b][:, :, :], in_=g1[b][:, :, :], func=AF.Sigmoid, scale=1.5957691216057308),
                 r=[k_('g1')], w=[k_('g1')])
            yield
            P.op('dve', lambda: nc.vector.tensor_tensor(out=g1[b][:, :, :], in0=g1[b][:, :, :], in1=gl[b][:, :, :], op=ALU.mult),
                 r=[k_('g1'), k_('gl')], w=[k_('g1')])
            yield
            P.op('dve', lambda: nc.vector.tensor_tensor(out=g1[b][:, :, :], in0=g1[b][:, :, :], in1=hs[b][:, :, :], op=ALU.mult),
                 r=[k_('g1'), k_('hs')], w=[k_('g1')])
            yield
            for j in range(4):
                P.op('pe', lambda: nc.tensor.transpose(out=p_ga[:, j, :], in_=g1[b][:, j, :], identity=identf[:, :]),
                     r=[k_('g1'), 'identf'], w=['pga'], inc=(j == 3))
                yield
            P.op('act', lambda: nc.scalar.copy(out=yl[b][:, :], in_=p_ga[:, :, :].rearrange("p j t -> p (j t)")), r=['pga'], w=[k_('yl')])
            yield
            P.dma('pool', lambda: nc.gpsimd.dma_start(out=S['YL'][r0:r0 + 128, :], in_=yl[b][:, :]), r=[k_('yl')], w=[('YL', blk)])
            yield

        def stageB2(blk):
            b = blk % 2
            k_ = lambda n: (n, b)
            r0 = blk * 128
            P.op('act', lambda: nc.scalar.activation(out=junk[:, 0:QL], in_=zc[b][:, 0:QL], func=AF.Square, accum_out=st[b][:, 1:2]),
                 r=[k_('zc')], w=['junk', k_('st1')])
            yield
            P.op('act', lambda: nc.scalar.activation(out=junk[:, 0:KVL], in_=zc[b][:, QL:QL + KVL], func=AF.Square, accum_out=st[b][:, 2:3]),
                 r=[k_('zc')], w=['junk', k_('st1')])
            yield
            P.op('act', lambda: nc.scalar.activation(out=st[b][:, 1:2], in_=st[b][:, 1:2], func=AF.Sqrt, bias=EPS, scale=1.0 / QL), r=[k_('st1')], w=[k_('st1')])
            yield
            P.op('act', lambda: nc.scalar.activation(out=st[b][:, 2:3], in_=st[b][:, 2:3], func=AF.Sqrt, bias=EPS, scale=1.0 / KVL), r=[k_('st1')], w=[k_('st1')])
            yield
            P.op('dve', lambda: nc.vector.reciprocal(out=st[b][:, 1:3], in_=st[b][:, 1:3]), r=[k_('st1')], w=[k_('st1')])
            yield
            P.op('dve', lambda: nc.vector.tensor_scalar(out=cqn[b][:, :], in0=zc[b][:, 0:QL], scalar1=st[b][:, 1:2], scalar2=None, op0=ALU.mult),
                 r=[k_('zc'), k_('st1')], w=[k_('cqn')])
            yield
            for k in range(2):
                P.op('pe', lambda: nc.tensor.transpose(out=p_tb[:, k, :], in_=cqn[b][:, k * 128:(k + 1) * 128], identity=identb[:, :]),
                     r=[k_('cqn'), 'identb'], w=['ptb'], inc=(k == 1))
                yield
            P.op('act', lambda: nc.scalar.copy(out=cqT[b][:, :, :], in_=p_tb[:, 0:2, :]), r=['ptb'], w=[k_('cqT')])
            yield
            for (c0, c1, hb) in ((0, 512, 0), (512, 768, 1)):
                for k in range(2):
                    P.op('pe', lambda: nc.tensor.matmul(p_q[:, hb, 0:c1 - c0], lhsT=cqT[b][:, k, :], rhs=W['wq'][:, k, c0:c1], start=(k == 0), stop=(k == 1)),
                         r=['Wsmall', k_('cqT')], w=['pq'], inc=(k == 1 and hb == 1))
                    yield
            qfl = qf[b][:, :, :].rearrange("p h e -> p (h e)")
            P.op('act', lambda: nc.scalar.copy(out=qfl[:, 0:512], in_=p_q[:, 0, :]), r=['pq'], w=[k_('qf')])
            yield
            P.op('act', lambda: nc.scalar.copy(out=qfl[:, 512:768], in_=p_q[:, 1, 0:256]), r=['pq'], w=[k_('qf')])
            yield
            P.op('dve', lambda: nc.vector.tensor_copy(out=qb[b][:, :, 0:64], in_=qf[b][:, :, 0:64]), r=[k_('qf')], w=[k_('qb')])
            yield
            cosb = cs[b][:, 0, :].unsqueeze(1).to_broadcast([128, 8, 16])
            sinb = cs[b][:, 1, :].unsqueeze(1).to_broadcast([128, 8, 16])
            q1, q2 = qf[b][:, :, 64:80], qf[b][:, :, 80:96]
            for (ti, a_, b_) in ((0, q1, cosb), (1, q2, sinb), (2, q2, cosb), (3, q1, sinb)):
                P.op('dve', lambda: nc.vector.tensor_tensor(out=rt[b][:, ti, :, :], in0=a_, in1=b_, op=ALU.mult),
                     r=[k_('qf'), k_('cs')], w=[k_('rt')])
                yield
            P.op('dve', lambda: nc.vector.tensor_tensor(out=qb[b][:, :, 64:80], in0=rt[b][:, 0, :, :], in1=rt[b][:, 1, :, :], op=ALU.subtract),
                 r=[k_('rt')], w=[k_('qb')])
            yield
            P.op('dve', lambda: nc.vector.tensor_tensor(out=qb[b][:, :, 80:96], in0=rt[b][:, 2, :, :], in1=rt[b][:, 3, :, :], op=ALU.add),
                 r=[k_('rt')], w=[k_('qb')])
            yield
            P.dma('pool', lambda: nc.gpsimd.dma_start(out=S['QS'][r0:r0 + 128, :], in_=qb[b][:, :, :].rearrange("p h e -> p (h e)")),
                  r=[k_('qb')], w=[('QS', blk)])
            yield
            P.op('dve', lambda: nc.vector.scalar_tensor_tensor(out=ckv[b][:, :], in0=zc[b][:, QL:QL + KVL], scalar=st[b][:, 2:3], in1=W['kvg'][:, :],
                                                               op0=ALU.mult, op1=ALU.mult),
                 r=[k_('zc'), k_('st1'), 'Wsmall'], w=[k_('ckv')])
            yield
            P.dma('pool', lambda: nc.gpsimd.dma_start(out=O['kvp'][r0:r0 + 128, :], in_=ckv[b][:, :]), r=[k_('ckv')], w=[('okv', tg, blk)])
            yield
            P.op('dve', lambda: nc.vector.tensor_copy(out=ckb[b][:, :], in_=ckv[b][:, :]), r=[k_('ckv')], w=[k_('ckb')])
            yield
            k1, k2 = zc[b][:, 384:400], zc[b][:, 400:416]
            for (ti, a_, b_) in ((0, k1, cs[b][:, 0, :]), (1, k2, cs[b][:, 1, :]), (2, k2, cs[b][:, 0, :]), (3, k1, cs[b][:, 1, :])):
                P.op('dve', lambda: nc.vector.tensor_tensor(out=krt[b][:, ti, :], in0=a_, in1=b_, op=ALU.mult),
                     r=[k_('zc'), k_('cs')], w=[k_('krt')])
                yield
            P.op('dve', lambda: nc.vector.tensor_tensor(out=krf[b][:, 0:16], in0=krt[b][:, 0, :], in1=krt[b][:, 1, :], op=ALU.subtract), r=[k_('krt')], w=[k_('krf')])
            yield
            P.op('dve', lambda: nc.vector.tensor_tensor(out=krf[b][:, 16:32], in0=krt[b][:, 2, :], in1=krt[b][:, 3, :], op=ALU.add), r=[k_('krt')], w=[k_('krf')])
            yield
            P.dma('pool', lambda: nc.gpsimd.dma_start(out=O['krp'][r0:r0 + 128, :], in_=krf[b][:, :]), r=[k_('krf')], w=[('okr', tg, blk)])
            yield
            P.op('dve', lambda: nc.vector.tensor_copy(out=kq[b][:, 64:96], in_=krf[b][:, :]), r=[k_('krf'), k_('kq')], w=[k_('kq')])
            yield
            P.op('pe', lambda: nc.tensor.transpose(out=p_tb[:, 0, :], in_=ckb[b][:, :], identity=identb[:, :]), r=[k_('ckb'), 'identb'], w=['ptb'], inc=False)
            yield
            P.op('pe', lambda: nc.tensor.transpose(out=p_tb[0:96, 1, :], in_=kq[b][:, :], identity=identb[:, :]), r=[k_('kq'), 'identb'], w=['ptb'])
            yield
            P.op('act', lambda: nc.scalar.copy(out=St['ckvT'][:, blk * 128:(blk + 1) * 128], in_=p_tb[:, 0, :]), r=['ptb'], w=[('ckvT', blk)])
            yield
            for kb in range(2):
                P.op('act', lambda: nc.scalar.copy(out=St['KhT'][kb][64:96, blk * 128:(blk + 1) * 128], in_=p_tb[64:96, 1, :]),
                     r=['ptb'], w=[('KhT', kb, blk)])
                yield


        def stageB(blk):
            g1_, g2_ = stageB1(blk), stageB2(blk)
            alive = [g1_, g2_]
            while alive:
                for g in list(alive):
                    try:
                        next(g)
                    except StopIteration:
                        alive.remove(g)

        stageA(0)
        for blk in range(nblk):
            if blk + 1 < nblk:
                stageA(blk + 1)
            stageB(blk)
        lb = (nblk - 1) % 2
        for j in range(4):
            P.op('pe', lambda: nc.tensor.transpose(out=p_ga[0:1, j, :], in_=hs[lb][:, j, 127:128], identity=identf[:, :]),
                 r=[('hs', lb), 'identf'], w=['pga'], inc=(j == 3))
        P.op('act', lambda: nc.scalar.copy(out=yl[0][0:1, :], in_=p_ga[0:1, :, :].rearrange("p j t -> p (j t)")), r=['pga'], w=[('yl', 0)])
        P.dma('pool', lambda: nc.gpsimd.dma_start(out=O['hp'], in_=yl[0][0:1, :]), r=[('yl', 0)], w=['o_hp'])
        for j in range(4):
            P.op('pe', lambda: nc.tensor.transpose(out=p_gi[0:3, j, :], in_=xlx[lb][:, j, 128:131], identity=identf[:, :]),
                 r=[('xlx', lb), 'identf'], w=['pgi'], inc=(j == 3))
        P.op('act', lambda: nc.scalar.copy(out=junk[0:3, 0:DL], in_=p_gi[0:3, :, :].rearrange("p j t -> p (j t)")), r=['pgi'], w=['junk'])
        P.dma('pool', lambda: nc.gpsimd.dma_start(out=O['cvp'], in_=junk[0:3, 0:DL]), r=['junk'], w=['o_cvp'])
    P.barrier()


def _bf16():
    import ml_dtypes
    return ml_dtypes.bfloat16


def make_in_maps(inp, cfg):
    SEQ, NB, NPG, NPHYS = cfg['SEQ'], cfg['NB'], cfg['NPG'], cfg['NPHYS']
    NBLK = SEQ // 128
    NSLOT = NBLK // 4
    NS = NB * 8
    f32 = np.float32
    a = {k: np.asarray(v) for k, v in inp.items()}
    past_len = NPG * 128

    def cols(g, n):
        return np.ascontiguousarray(np.asarray(g, f32).reshape(n, 128).T)
    half = 16
    freqs = (10000.0 ** (-np.arange(half, dtype=f32) / f32(half))).astype(f32)
    pos_p = np.arange(SEQ, dtype=f32)
    pos_s = (past_len + np.arange(8)).astype(f32)
    angp = (pos_p[:, None] * freqs[None, :]).astype(f32)
    angs = np.tile((pos_s[:, None] * freqs[None, :]).astype(f32), (NB, 1))
    common = {
        'final_norm': a['final_norm'].astype(f32),
        'identb': np.eye(128, dtype=f32).astype(_bf16()), 'identf': np.eye(128, dtype=f32),
        'w_in': a['w_in'][0], 'mix_gc': cols(a['mix_norm'][0], 8),
        'w_q_up': a['w_q_up'][0], 'q_gc': cols(a['q_norm'][0], 2),
        'w_out': a['w_out'][0], 'out_gc': cols(np.concatenate([a['out_norm_lru'][0], a['out_norm_mla'][0]]), 8),
        'w_uk': a['w_uk'][0], 'w_uv': a['w_uv'][0], 'lru_w_a': a['lru_w_a'][0], 'lru_w_i': a['lru_w_i'][0],
        'kv_norm': a['kv_norm'][0],
        'cosp': np.cos(angp).astype(f32), 'sinp': np.sin(angp).astype(f32),
        'coss': np.cos(angs).astype(f32), 'sins': np.sin(angs).astype(f32),
        'qcol': (np.arange(128) % 16).astype(f32).reshape(128, 1),
        'ckv_cache': a['cache_kv_latent'][0].reshape(NPHYS * 128, KVL),
        'ckr_cache': a['cache_k_rope'][0].reshape(NPHYS * 128, RP),
    }
    for n in ('ffn1', 'ffn2'):
        common[n + '_wg'] = a[n + '_w_gate'][0]
        common[n + '_wu'] = a[n + '_w_up'][0]
        common[n + '_wd'] = a[n + '_w_down'][0]
        common[n + '_gc'] = cols(a[n + '_norm'][0], 8)
    chv = np.zeros((128, 32), f32)
    cw = a['conv_w'][0]
    for j in range(4):
        for k in range(4):
            chv[:, j * 4 + k] = cw[k, j * 128:(j + 1) * 128]
    chv[:, 16:20] = cols(a['conv_b'][0], 4)
    chv[:, 20:24] = cols(a['lru_b_a'][0], 4)
    chv[:, 24:28] = cols(a['lru_b_i'][0], 4)
    chv[:, 28:32] = cols(a['lru_lambda'][0], 4)
    common['chv'] = chv
    mo = np.zeros((8, 8, 8), f32)
    for kp in range(8):
        mo[kp, :, kp:] = 1.0
    common['mask_own'] = mo.reshape(8, 64)
    tri = (np.arange(128)[:, None] <= np.arange(128)[None, :]).astype(f32)
    maps = []
    for c in range(8):
        seq, j = c // 4, c % 4
        m = dict(common)
        m['xp'] = a['x_prompt'][seq]
        m['xs'] = a['x_sample'][c * NB:(c + 1) * NB].reshape(NS, D)
        m['conv0'] = a['state_conv'][0, c * NB:(c + 1) * NB]
        m['h0'] = a['state_lru_h'][0, c * NB:(c + 1) * NB]
        own = np.zeros((128, NSLOT), np.int32)
        for s in range(NSLOT):
            own[:, s] = (4 * s + j) * 128 + np.arange(128)
        m['own_idx'] = own
        mk = np.zeros((128, 4, 128), f32)
        for i in range(4):
            if i < j:
                mk[:, i, :] = 1.0
            elif i == j:
                mk[:, i, :] = tri
        m['masks'] = mk
        pt = a['page_table'][c * NB:(c + 1) * NB].astype(np.int32)
        m['pt_l'] = np.ascontiguousarray(pt.reshape(NB, NPG // 8, 8).transpose(2, 0, 1))
        maps.append({k: np.ascontiguousarray(v) for k, v in m.items()})
    return maps


def assemble(results, cfg, B=2):
    SEQ, NB = cfg['SEQ'], cfg['NB']
    NBLK = SEQ // 128
    NSLOT = NBLK // 4
    f32 = np.float32
    yp = np.zeros((B, SEQ, D), f32)
    ys = np.zeros((8 * NB, 8, D), f32)
    kvp = np.zeros((1, B, SEQ, KVL), f32); krp = np.zeros((1, B, SEQ, RP), f32)
    hp = np.zeros((1, B, DL), f32); cvp = np.zeros((1, B, 3, DL), f32)
    kvs = np.zeros((1, 8 * NB, 8, KVL), f32); krs = np.zeros((1, 8 * NB, 8, RP), f32)
    hs = np.zeros((1, 8 * NB, DL), f32); cvs = np.zeros((1, 8 * NB, 3, DL), f32)
    for c in range(8):
        r = results[c]
        seq, j = c // 4, c % 4
        yo = np.asarray(r['y_own'])
        for s in range(NSLOT):
            b = 4 * s + j
            yp[seq, b * 128:(b + 1) * 128] = yo[s * 128:(s + 1) * 128]
        if j == 0:
            kvp[0, seq] = r['kvp']; krp[0, seq] = r['krp']; hp[0, seq] = np.asarray(r['hp'])[0]; cvp[0, seq] = r['cvp']
        sl = slice(c * NB, (c + 1) * NB)
        ys[sl] = np.asarray(r['ys']).reshape(NB, 8, D)
        kvs[0, sl] = np.asarray(r['kvs']).reshape(NB, 8, KVL); krs[0, sl] = np.asarray(r['krs']).reshape(NB, 8, RP)
        hs[0, sl] = r['hs_']; cvs[0, sl] = r['cvs']
    return (yp, ys, kvp, krp, hp, cvp, kvs, krs, hs, cvs)


_NC_CACHE = {}


def kernel(**inputs):
    SEQ = int(inputs['x_prompt'].shape[1])
    NB = int(inputs['x_sample'].shape[0]) // 8
    NPG = int(inputs['page_table'].shape[1])
    NPHYS = int(inputs['cache_kv_latent'].shape[1])
    cfg = dict(SEQ=SEQ, NB=NB, NPG=NPG, NPHYS=NPHYS)
    key = tuple(sorted(cfg.items()))
    if key not in _NC_CACHE:
        _NC_CACHE[key] = build_program(cfg)
    nc = _NC_CACHE[key]
    maps = make_in_maps(inputs, cfg)
    res = run_bass_kernel_spmd(nc, maps, core_ids=list(range(8)))
    return assemble(res.results, cfg)
```

```python
import numpy as np
from contextlib import ExitStack
import concourse.bass as bass
import concourse.mybir as mybir
from concourse.bass_utils import run_bass_kernel_spmd

F32 = mybir.dt.float32
BF16 = mybir.dt.bfloat16
I32 = mybir.dt.int32
AF = mybir.ActivationFunctionType
ALU = mybir.AluOpType

D = 1024
DFF = 2816
NF = DFF // 128
DL = 512
QL = 256
KVL = 128
RP = 32
NH = 8
DIN = 1440
EPS = 1e-6
EPOCH = 2000
DGEN = 120
KDMA = 8


class Prog:
    def __init__(self, nc, es):
        self.nc = nc
        self.es = es
        self.engs = {'pe': nc.tensor, 'act': nc.scalar, 'dve': nc.vector, 'pool': nc.gpsimd, 'sp': nc.sync}
        self.cnt = {e: 0 for e in self.engs}
        self.sems = {e: [] for e in self.engs}
        self.dma_cnt = {'sp': 0, 'pool': 0}
        self.dma_sems = {q: [] for q in ('sp', 'pool')}
        self.waited = {}
        self.last_w = {}
        self.readers = {}
        self.semown = {}

    def _sem(self, e, ep):
        while len(self.sems[e]) <= ep:
            s = self.es.enter_context(self.nc.semaphore(f"s_{e}_{len(self.sems[e])}"))
            self.semown[id(s)] = e
            self.sems[e].append(s)
        return self.sems[e][ep]

    def _wait(self, e, tok):
        sem, val = tok
        k = (e, id(sem))
        if self.waited.get(k, 0) >= val:
            return
        self.waited[k] = val
        self.engs[e].wait_ge(sem, val)

    def _deps(self, e, r, w):
        toks = []
        for k in r:
            t = self.last_w.get(k)
            if t is not None:
                toks.append(t)
        for k in w:
            t = self.last_w.get(k)
            if t is not None:
                toks.append(t)
            for t in self.readers.get(k, ()):
                toks.append(t)
        out = []
        for t in toks:
            if e == 'pe' and self.semown.get(id(t[0])) == 'pe':
                continue
            out.append(t)
        return out

    def _record(self, tok, r, w):
        for k in w:
            self.last_w[k] = tok
            self.readers[k] = []
        for k in r:
            self.readers.setdefault(k, []).append(tok)

    def op(self, e, fn, r=(), w=(), inc=True):
        for t in self._deps(e, r, w):
            self._wait(e, t)
        i = self.cnt[e]
        sem = self._sem(e, i // EPOCH)
        inst = fn()
        tok = (sem, i % EPOCH + 1)
        if inc:
            inst.then_inc(sem, 1)
            self.cnt[e] += 1
        self._record(tok, r, w)
        return tok

    def _dsem(self, q, n):
        st = n // (KDMA * DGEN)
        while len(self.dma_sems[q]) <= st:
            k = len(self.dma_sems[q])
            self.dma_sems[q].append([self.es.enter_context(self.nc.semaphore(f"dq_{q}_{k}_{i}")) for i in range(KDMA)])
        m = n % (KDMA * DGEN)
        return self.dma_sems[q][st][m % KDMA], m // KDMA, st

    def dma(self, q, fn, r=(), w=()):
        for t in self._deps(q, r, w):
            self._wait(q, t)
        n = self.dma_cnt[q]
        sem, gen, st = self._dsem(q, n)
        if gen > 0:
            self._wait(q, (sem, 16 * gen))
        elif st > 0:
            ps, pg, _ = self._dsem(q, n - KDMA)
            self._wait(q, (ps, 16 * (pg + 1)))
        inst = fn()
        inst.then_inc(sem, 16)
        tok = (sem, 16 * (gen + 1))
        self.dma_cnt[q] += 1
        self._record(tok, r, w)
        return tok

    def barrier(self):
        toks = []
        for e in self.engs:
            i = self.cnt[e]
            if i > 0:
                toks.append((self.sems[e][(i - 1) // EPOCH], (i - 1) % EPOCH + 1))
        toks += self._dma_tail()
        for e in self.engs:
            for t in toks:
                if self.semown.get(id(t[0])) == e:
                    continue
                self._wait(e, t)

    def _dma_tail(self):
        toks = []
        for q in ('sp', 'pool'):
            n = self.dma_cnt[q]
            for m in range(max(0, n - KDMA), n):
                sem, gen, _ = self._dsem(q, m)
                toks.append((sem, 16 * (gen + 1)))
        return toks

    def finish(self):
        for t in self._dma_tail():
            for q in ('sp', 'pool'):
                self._wait(q, t)


def _tile(nc, es, name, shape, dt):
    return es.enter_context(nc.sbuf_tensor(name, list(shape), dt))


def _psum(nc, es, name, shape, dt):
    return es.enter_context(nc.psum_tensor(name, list(shape), dt))


def emit_ffn_weight_prep(P, nc, es0, wg, wu, wd, gain_cols, WGU_s, WD_s, tag):
    with ExitStack() as es:
        stg = [_tile(nc, es, f"wp_stg{tag}{i}", [128, 2, 8, 128], F32) for i in range(2)]
        stgd = [_tile(nc, es, f"wp_stgd{tag}{i}", [128, 1024], F32) for i in range(2)]
        ob = [_tile(nc, es, f"wp_ob{tag}{i}", [128, 2, 8, 128], BF16) for i in range(2)]
        obd = [_tile(nc, es, f"wp_obd{tag}{i}", [128, 1024], BF16) for i in range(2)]
        gb = _tile(nc, es, f"wp_gb{tag}", [128, 8, 128], F32)
        gc = _tile(nc, es, f"wp_gc{tag}", [128, 8], F32)
        ones = _tile(nc, es, f"wp_ones{tag}", [128, 128], F32)
        P.dma('sp', lambda: nc.sync.dma_start(out=gc[:, :], in_=gain_cols), w=[('wp_gc', tag)])
        P.op('dve', lambda: nc.vector.memset(ones[:, :], 1.0), w=[('wp_ones', tag)])
        for k in range(8):
            P.op('dve', lambda k=k: nc.vector.tensor_scalar(
                out=gb[:, k, :], in0=ones[:, :], scalar1=gc[:, k:k + 1], scalar2=None, op0=ALU.mult),
                r=[('wp_gc', tag), ('wp_ones', tag)], w=[('wp_gb', tag)])
        for f in range(NF):
            b = f % 2
            P.dma('sp', lambda: nc.sync.dma_start(
                out=stg[b][:, 0, :, :], in_=wg[:, f * 128:(f + 1) * 128].rearrange("(k p) c -> p k c", p=128)),
                w=[('wp_stg', tag, b, 0)])
            P.dma('sp', lambda: nc.sync.dma_start(
                out=stg[b][:, 1, :, :], in_=wu[:, f * 128:(f + 1) * 128].rearrange("(k p) c -> p k c", p=128)),
                w=[('wp_stg', tag, b, 1)])
            P.dma('sp', lambda: nc.sync.dma_start(out=stgd[b][:, :], in_=wd[f * 128:(f + 1) * 128, :]),
                  w=[('wp_stgd', tag, b)])
            for h in range(2):
                eng = 'dve' if h == 0 else 'pool'
                E = nc.vector if h == 0 else nc.gpsimd
                P.op(eng, lambda E=E, h=h: E.tensor_tensor(
                    out=ob[b][:, h, :, :], in0=stg[b][:, h, :, :], in1=gb[:, :, :], op=ALU.mult),
                    r=[('wp_stg', tag, b, h), ('wp_gb', tag)], w=[('wp_ob', tag, b, h)])
            P.op('act', lambda: nc.scalar.copy(out=obd[b][:, :], in_=stgd[b][:, :]),
                 r=[('wp_stgd', tag, b)], w=[('wp_obd', tag, b)])
            P.dma('pool', lambda: nc.gpsimd.dma_start(out=WGU_s[f], in_=ob[b][:, :, :, :]),
                  r=[('wp_ob', tag, b, 0), ('wp_ob', tag, b, 1)], w=[('WGU', tag, f)])
            P.dma('pool', lambda: nc.gpsimd.dma_start(out=WD_s[f], in_=obd[b][:, :]),
                  r=[('wp_obd', tag, b)], w=[('WD', tag, f)])
    P.barrier()


def emit_ffn_phase(P, nc, identb, tiles, WGU_s, WD_s, tag, final_gain=None):
    with ExitStack() as es:
        NW = 3
        xt = [_tile(nc, es, f"f_xt{tag}{i}", [128, 4, D], F32) for i in range(2)]
        xn = _tile(nc, es, f"f_xn{tag}", [128, D], BF16)
        xnT = _tile(nc, es, f"f_xnT{tag}", [128, 8, 512], BF16)
        hT = _tile(nc, es, f"f_hT{tag}", [128, NF, 512], BF16)
        sg = [_tile(nc, es, f"f_sg{tag}{i}", [128, 512], F32) for i in range(2)]
        wgu = [_tile(nc, es, f"f_wgu{tag}{i}", [128, 2, 8, 128], BF16) for i in range(NW)]
        wd_all = _tile(nc, es, f"f_wdall{tag}", [128, NF, 1024], BF16)
        ss = _tile(nc, es, f"f_ss{tag}", [128, 8], F32)
        rstd = _tile(nc, es, f"f_rstd{tag}", [128, 8], F32)
        junk = _tile(nc, es, f"f_junk{tag}", [128, D], F32)
        pg = [_psum(nc, es, f"f_pg{tag}{i}", [128, 512], F32) for i in range(2)]
        pu = [_psum(nc, es, f"f_pu{tag}{i}", [128, 512], F32) for i in range(2)]
        py = [_psum(nc, es, f"f_py{tag}{i}", [128, 512], F32) for i in range(2)]
        pt = _psum(nc, es, f"f_pt{tag}", [128, 8, 128], BF16)
        if final_gain is not None:
            fg = _tile(nc, es, f"f_fg{tag}", [128, D], F32)
            P.dma('sp', lambda: nc.sync.dma_start(out=fg[:, :], in_=final_gain.partition_broadcast(128)),
                  w=[('f_fg', tag)])
        wcount = 0
        ycount = 0
        P.dma('sp', lambda: nc.sync.dma_start(out=wd_all[:, :, :], in_=WD_s.rearrange("f p c -> p f c")),
              r=[('WD', tag, f) for f in range(NF)], w=[('f_wdall', tag)])
        for ti, (src, dst, T, skeys, dkeys) in enumerate(tiles):
            ns = T // 128
            xb = xt[ti % 2]
            kx = ('f_xt', tag, ti % 2)
            P.dma('sp', lambda: nc.sync.dma_start(
                out=xb[:, 0:ns, :], in_=src.rearrange("(s p) d -> p s d", p=128)), r=skeys, w=[kx])
            for s in range(ns):
                P.op('act', lambda s=s: nc.scalar.activation(
                    out=junk[:, :], in_=xb[:, s, :], func=AF.Square, accum_out=ss[:, s:s + 1]),
                    r=[kx], w=[('f_junk', tag), ('f_ss', tag, s)])
            P.op('act', lambda: nc.scalar.activation(
                out=rstd[:, 0:ns], in_=ss[:, 0:ns], func=AF.Sqrt, bias=EPS, scale=1.0 / D),
                r=[('f_ss', tag, s) for s in range(ns)], w=[('f_rstd', tag)])
            P.op('dve', lambda: nc.vector.reciprocal(out=rstd[:, 0:ns], in_=rstd[:, 0:ns]),
                 r=[('f_rstd', tag)], w=[('f_rstd', tag)])
            for s in range(ns):
                P.op('dve', lambda s=s: nc.vector.tensor_scalar(
                    out=xn[:, :], in0=xb[:, s, :], scalar1=rstd[:, s:s + 1], scalar2=None, op0=ALU.mult),
                    r=[kx, ('f_rstd', tag)], w=[('f_xn', tag)])
                for k in range(8):
                    P.op('pe', lambda k=k: nc.tensor.transpose(
                        out=pt[:, k, :], in_=xn[:, k * 128:(k + 1) * 128], identity=identb[:, :]),
                        r=[('f_xn', tag), 'identb'], w=[('f_pt', tag)], inc=(k == 7))
                P.op('act', lambda s=s: nc.scalar.copy(out=xnT[:, :, s * 128:(s + 1) * 128], in_=pt[:, :, :]),
                     r=[('f_pt', tag)], w=[('f_xnT', tag, s)])
            kxnT = [('f_xnT', tag, s) for s in range(ns)]
            for f in range(NF):
                wb = wcount % NW
                wcount += 1
                P.dma('sp', lambda: nc.sync.dma_start(out=wgu[wb][:, :, :, :], in_=WGU_s[f]),
                      r=[('WGU', tag, f)], w=[('f_wgu', tag, wb)])
                pb = f % 2
                for k in range(8):
                    P.op('pe', lambda k=k: nc.tensor.matmul(
                        pg[pb][:, 0:T], lhsT=wgu[wb][:, 0, k, :], rhs=xnT[:, k, 0:T], start=(k == 0), stop=(k == 7)),
                        r=[('f_wgu', tag, wb)] + kxnT, w=[('f_pg', tag, pb)], inc=(k == 7))
                for k in range(8):
                    P.op('pe', lambda k=k: nc.tensor.matmul(
                        pu[pb][:, 0:T], lhsT=wgu[wb][:, 1, k, :], rhs=xnT[:, k, 0:T], start=(k == 0), stop=(k == 7)),
                        r=[('f_wgu', tag, wb)] + kxnT, w=[('f_pu', tag, pb)], inc=(k == 7))
                P.op('act', lambda: nc.scalar.activation(out=sg[pb][:, 0:T], in_=pg[pb][:, 0:T], func=AF.Silu),
                     r=[('f_pg', tag, pb)], w=[('f_sg', tag, pb)])
                P.op('dve', lambda f=f: nc.vector.tensor_tensor(
                    out=hT[:, f, 0:T], in0=sg[pb][:, 0:T], in1=pu[pb][:, 0:T], op=ALU.mult),
                    r=[('f_sg', tag, pb), ('f_pu', tag, pb)], w=[('f_hT', tag, f)])
            for s in range(ns):
                for half in range(2):
                    yb = ycount % 2
                    ycount += 1
                    for f in range(NF):
                        P.op('pe', lambda: nc.tensor.matmul(
                            py[yb][:, :], lhsT=hT[:, f, s * 128:(s + 1) * 128],
                            rhs=wd_all[:, f, half * 512:(half + 1) * 512], start=(f == 0), stop=(f == NF - 1)),
                            r=[('f_hT', tag, f), ('f_wdall', tag)], w=[('f_py', tag, yb)], inc=(f == NF - 1))
                    P.op('dve', lambda: nc.vector.scalar_tensor_tensor(
                        out=xb[:, s, half * 512:(half + 1) * 512], in0=py[yb][:, :], scalar=0.5,
                        in1=xb[:, s, half * 512:(half + 1) * 512], op0=ALU.mult, op1=ALU.add),
                        r=[('f_py', tag, yb), kx], w=[kx])
            if final_gain is not None:
                for s in range(ns):
                    P.op('act', lambda s=s: nc.scalar.activation(
                        out=junk[:, :], in_=xb[:, s, :], func=AF.Square, accum_out=ss[:, s:s + 1]),
                        r=[kx], w=[('f_junk', tag), ('f_ss', tag, s)])
                P.op('act', lambda: nc.scalar.activation(
                    out=rstd[:, 0:ns], in_=ss[:, 0:ns], func=AF.Sqrt, bias=EPS, scale=1.0 / D),
                    r=[('f_ss', tag, s) for s in range(ns)], w=[('f_rstd', tag)])
                P.op('dve', lambda: nc.vector.reciprocal(out=rstd[:, 0:ns], in_=rstd[:, 0:ns]),
                     r=[('f_rstd', tag)], w=[('f_rstd', tag)])
                for s in range(ns):
                    P.op('dve', lambda s=s: nc.vector.scalar_tensor_tensor(
                        out=xb[:, s, :], in0=xb[:, s, :], scalar=rstd[:, s:s + 1], in1=fg[:, :],
                        op0=ALU.mult, op1=ALU.mult),
                        r=[kx, ('f_rstd', tag), ('f_fg', tag)], w=[kx])
            P.dma('pool', lambda: nc.gpsimd.dma_start(
                out=dst.rearrange("(s p) d -> p s d", p=128), in_=xb[:, 0:ns, :]), r=[kx], w=dkeys)
    P.barrier()


def build_program(cfg, stage="full"):
    SEQ, NB, NPG, NPHYS = cfg['SEQ'], cfg['NB'], cfg['NPG'], cfg['NPHYS']
    NSLOT = SEQ // 128 // 4
    NOWN = NSLOT * 128
    NS = NB * 8
    NG = NPG // 8
    nc = bass.Bass("TRN2", target_bir_lowering=False)

    def din(name, shape, dt=F32):
        return nc.dram_tensor(name, list(shape), dt, kind="ExternalInput").ap()

    def dout(name, shape, dt=F32):
        return nc.dram_tensor(name, list(shape), dt, kind="ExternalOutput").ap()

    def dscr(name, shape, dt):
        return nc.dram_tensor(name, list(shape), dt, kind="Internal").ap()

    I = {}
    I['xp'] = din('xp', [SEQ, D])
    I['xs'] = din('xs', [NS, D])
    for n in ('ffn1', 'ffn2'):
        I[n + '_wg'] = din(n + '_wg', [D, DFF])
        I[n + '_wu'] = din(n + '_wu', [D, DFF])
        I[n + '_wd'] = din(n + '_wd', [DFF, D])
        I[n + '_gc'] = din(n + '_gc', [128, 8])
    I['final_norm'] = din('final_norm', [D])
    I['identb'] = din('identb', [128, 128], BF16)
    I['identf'] = din('identf', [128, 128])
    I['w_in'] = din('w_in', [D, DIN]); I['mix_gc'] = din('mix_gc', [128, 8])
    I['w_q_up'] = din('w_q_up', [QL, 768]); I['q_gc'] = din('q_gc', [128, 2])
    I['w_out'] = din('w_out', [D, D]); I['out_gc'] = din('out_gc', [128, 8])
    I['w_uk'] = din('w_uk', [KVL, 8, 64]); I['w_uv'] = din('w_uv', [KVL, 8, 64])
    I['lru_w_a'] = din('lru_w_a', [8, 64, 64]); I['lru_w_i'] = din('lru_w_i', [8, 64, 64])
    I['chv'] = din('chv', [128, 32]); I['kv_norm'] = din('kv_norm', [KVL])
    I['cosp'] = din('cosp', [SEQ, 16]); I['sinp'] = din('sinp', [SEQ, 16])
    I['coss'] = din('coss', [NS, 16]); I['sins'] = din('sins', [NS, 16])
    I['conv0'] = din('conv0', [NB, 3, DL]); I['h0'] = din('h0', [NB, DL])
    I['own_idx'] = din('own_idx', [128, NSLOT], I32)
    I['masks'] = din('masks', [128, 4, 128])
    I['pt_l'] = din('pt_l', [8, NB, NG], I32)
    I['qcol'] = din('qcol', [128, 1]); I['mask_own'] = din('mask_own', [8, 64])
    I['ckv_cache'] = din('ckv_cache', [NPHYS * 128, KVL]); I['ckr_cache'] = din('ckr_cache', [NPHYS * 128, RP])
    O = {}
    O['y_own'] = dout('y_own', [NOWN, D]); O['ys'] = dout('ys', [NS, D])
    O['kvp'] = dout('kvp', [SEQ, KVL]); O['krp'] = dout('krp', [SEQ, RP])
    O['hp'] = dout('hp', [1, DL]); O['cvp'] = dout('cvp', [3, DL])
    O['kvs'] = dout('kvs', [NS, KVL]); O['krs'] = dout('krs', [NS, RP])
    O['hs_'] = dout('hs_', [NB, DL]); O['cvs'] = dout('cvs', [NB, 3, DL])
    if cfg.get('debug'):
        O['dbg_idx'] = dout('dbg_idx', [128, NB * NG], I32)
        O['dbg_G'] = dout('dbg_G', [128, 1024])
        O['dbg_GR'] = dout('dbg_GR', [128, 256])
        O['dbg_YM'] = dout('dbg_YM', [NS, DL])
        O['dbg_ps'] = dout('dbg_ps', [128, 512])
        O['dbg_ov'] = dout('dbg_ov', [64, 129])
    S = {}
    for n in ('ffn1', 'ffn2'):
        S[n + '_WGU'] = dscr(n + '_WGU', [NF, 128, 2, 8, 128], BF16)
        S[n + '_WD'] = dscr(n + '_WD', [NF, 128, 1024], BF16)
    S['X1'] = dscr('X1', [SEQ + NS, D], F32)
    S['X2'] = dscr('X2', [NOWN + NS, D], F32)
    S['YL'] = dscr('YL', [SEQ + NS, DL], F32)
    S['QS'] = dscr('QS', [SEQ + NS, 768], BF16)
    S['CKS'] = dscr('CKS', [NS, KVL], BF16)
    S['YM'] = dscr('YM', [NS, DL], F32)

    with ExitStack() as es:
        P = Prog(nc, es)
        W = {}
        W['identb'] = _tile(nc, es, "identb_sb", [128, 128], BF16)
        W['identf'] = _tile(nc, es, "identf_sb", [128, 128], F32)
        identb = W['identb']
        P.dma('sp', lambda: nc.sync.dma_start(out=identb[:, :], in_=I['identb']), w=['identb'])
        P.dma('sp', lambda: nc.sync.dma_start(out=W['identf'][:, :], in_=I['identf']), w=['identf'])
        for n in ('ffn1', 'ffn2'):
            emit_ffn_weight_prep(P, nc, es, I[n + '_wg'], I[n + '_wu'], I[n + '_wd'], I[n + '_gc'],
                                 S[n + '_WGU'], S[n + '_WD'], n)
        TT = 512
        tiles = []
        for t in range(SEQ // TT):
            tiles.append((I['xp'][t * TT:(t + 1) * TT, :], S['X1'][t * TT:(t + 1) * TT, :], TT, [],
                          [('X1', t * 4 + i) for i in range(4)]))
        tiles.append((I['xs'], S['X1'][SEQ:SEQ + NS, :], NS, [], [('X1', 's')]))
        emit_ffn_phase(P, nc, identb, tiles, S['ffn1_WGU'], S['ffn1_WD'], 'ffn1')
        with ExitStack() as es2:
            W['wq'] = _tile(nc, es2, "W_wq", [128, 2, 768], BF16)
            W['wout'] = _tile(nc, es2, "W_wout", [128, 8, D], BF16)
            W['wuk'] = _tile(nc, es2, "W_wuk", [128, 8, 64], BF16)
            W['wuv'] = _tile(nc, es2, "W_wuv", [128, 8, 64], BF16)
            W['wukT'] = _tile(nc, es2, "W_wukT", [64, 8, 128], BF16)
            W['bd'] = _tile(nc, es2, "W_bd", [128, 2, 4, 128], BF16)
            W['chv'] = _tile(nc, es2, "W_chv", [128, 32], F32)
            W['cch'] = _tile(nc, es2, "W_cch", [128, 4], F32)
            W['kvg'] = _tile(nc, es2, "W_kvg", [128, KVL], F32)
            St = {}
            St['ckvT'] = _tile(nc, es2, "St_ckvT", [128, SEQ], BF16)
            St['KhT'] = [_tile(nc, es2, f"St_KhT{i}", [96, SEQ], BF16) for i in range(2)]
            St['ckvT_s'] = _tile(nc, es2, "St_ckvTs", [128, 128], BF16)
            St['krT_s'] = _tile(nc, es2, "St_krTs", [96, 128], BF16)
            with ExitStack() as es3:
                W['win'] = _tile(nc, es3, "W_win", [128, 8, DIN], BF16)
                emit_small_prep(P, nc, es2, I, W)
                emit_mixer_prompt(P, nc, W, St, I, O, S, cfg)
                emit_mixer_in(P, nc, W, St, I, O, S, cfg, sample=True)
            emit_prompt_attn(P, nc, W, St, I, S, cfg)
            emit_sample_attn(P, nc, W, St, I, S, cfg, O)
        P.barrier()
        tiles = []
        t0 = 0
        while t0 < NOWN:
            tt = min(TT, NOWN - t0)
            tiles.append((S['X2'][t0:t0 + tt, :], O['y_own'][t0:t0 + tt, :], tt,
                          [('X2', t0 // 128 + i) for i in range(tt // 128)], [('y_own', t0)]))
            t0 += tt
        tiles.append((S['X2'][NOWN:NOWN + NS, :], O['ys'], NS, [('X2', 's')], [('ys',)]))
        emit_ffn_phase(P, nc, identb, tiles, S['ffn2_WGU'], S['ffn2_WD'], 'ffn2', final_gain=I['final_norm'])
        P.finish()
    return nc


def emit_small_prep(P, nc, es0, I, W):
    with ExitStack() as es:
        stg = [_tile(nc, es, f"sp_stg{i}", [128, 1440], F32) for i in range(2)]
        gc = _tile(nc, es, "sp_gc", [128, 18], F32)
        P.dma('sp', lambda: nc.sync.dma_start(out=gc[:, 0:8], in_=I['mix_gc']), w=['sp_gc'])
        P.dma('sp', lambda: nc.sync.dma_start(out=gc[:, 8:10], in_=I['q_gc']), w=['sp_gc'])
        P.dma('sp', lambda: nc.sync.dma_start(out=gc[:, 10:18], in_=I['out_gc']), w=['sp_gc'])
        n = 0

        def cast_rows(dst, src, ncols, gcol):
            nonlocal n
            b = n % 2
            n += 1
            P.dma('sp', lambda: nc.sync.dma_start(out=stg[b][:, 0:ncols], in_=src), w=[('sp_stg', b)])
            if gcol is None:
                P.op('dve', lambda: nc.vector.tensor_copy(out=dst, in_=stg[b][:, 0:ncols]),
                     r=[('sp_stg', b)], w=['Wsmall'])
            else:
                P.op('dve', lambda: nc.vector.tensor_scalar(
                    out=dst, in0=stg[b][:, 0:ncols], scalar1=gc[:, gcol:gcol + 1], scalar2=None, op0=ALU.mult),
                    r=[('sp_stg', b), 'sp_gc'], w=['Wsmall'])
        for k in range(8):
            cast_rows(W['win'][:, k, :], I['w_in'][k * 128:(k + 1) * 128, :], DIN, k)
        for k in range(2):
            cast_rows(W['wq'][:, k, :], I['w_q_up'][k * 128:(k + 1) * 128, :], 768, 8 + k)
        for k in range(8):
            cast_rows(W['wout'][:, k, :], I['w_out'][k * 128:(k + 1) * 128, :], D, 10 + k)
        cast_rows(W['wuk'][:, :, :], I['w_uk'].rearrange("c h n -> c (h n)"), 512, None)
        cast_rows(W['wuv'][:, :, :], I['w_uv'].rearrange("c h n -> c (h n)"), 512, None)
        bd = _tile(nc, es, "sp_bd", [128, 2, 4, 128], F32)
        P.op('dve', lambda: nc.vector.memset(bd[:, :, :, :], 0.0), w=['sp_bd'])
        for gi, nm in enumerate(('lru_w_a', 'lru_w_i')):
            for j in range(4):
                P.dma('sp', lambda: nc.sync.dma_start(out=bd[0:64, gi, j, 0:64], in_=I[nm][2 * j]), r=[], w=['sp_bd'])
                P.dma('sp', lambda: nc.sync.dma_start(out=bd[64:128, gi, j, 64:128], in_=I[nm][2 * j + 1]), r=[], w=['sp_bd'])
        P.op('dve', lambda: nc.vector.tensor_copy(out=W['bd'][:, :, :, :], in_=bd[:, :, :, :]), r=['sp_bd'], w=['Wsmall'])
        P.dma('sp', lambda: nc.sync.dma_start(out=W['chv'][:, :], in_=I['chv']), w=['Wsmall'])
        P.dma('sp', lambda: nc.sync.dma_start(out=W['kvg'][:, :], in_=I['kv_norm'].partition_broadcast(128)), w=['Wsmall'])
        P.op('act', lambda: nc.scalar.activation(out=W['cch'][:, :], in_=W['chv'][:, 28:32], func=AF.Exp, scale=-1.0),
             r=['Wsmall'], w=['cch'])
        P.op('act', lambda: nc.scalar.activation(out=W['cch'][:, :], in_=W['cch'][:, :], func=AF.Ln, bias=1.0, scale=1.0),
             r=['cch'], w=['cch'])
        P.op('dve', lambda: nc.vector.tensor_scalar(out=W['cch'][:, :], in0=W['cch'][:, :], scalar1=-8.0, scalar2=None, op0=ALU.mult),
             r=['cch'], w=['cch', 'Wsmall'])
        pT = _psum(nc, es, "sp_pT", [64, 8, 128], BF16)
        for h in range(8):
            P.op('pe', lambda: nc.tensor.transpose(out=pT[:, h, :], in_=W['wuk'][:, h, :], identity=W['identb'][:, :]),
                 r=['Wsmall', 'identb'], w=['sp_pT'], inc=(h == 7))
        P.op('act', lambda: nc.scalar.copy(out=W['wukT'][:, :, :], in_=pT[:, :, :]), r=['sp_pT'], w=['Wsmall'])
    P.barrier()


def emit_mixer_in(P, nc, W, St, I, O, S, cfg, sample):
    SEQ, NB = cfg['SEQ'], cfg['NB']
    NS = NB * 8
    nblk = 1 if sample else SEQ // 128
    tg = 's' if sample else 'p'
    with ExitStack() as es:
        x1 = _tile(nc, es, f"m_x1{tg}", [128, D], F32)
        junk = _tile(nc, es, f"m_junk{tg}", [128, 1536], F32)
        xn = _tile(nc, es, f"m_xn{tg}", [128, D], BF16)
        xnT = _tile(nc, es, f"m_xnT{tg}", [128, 8, 128], BF16)
        st = _tile(nc, es, f"m_st{tg}", [128, 8], F32)
        XW = 11 if sample else 131
        xlx = _tile(nc, es, f"m_xlx{tg}", [128, 4, (16 * 11) if sample else 131], F32)
        xc = _tile(nc, es, f"m_xc{tg}", [128, 4, 128], F32)
        xcb = _tile(nc, es, f"m_xcb{tg}", [128, 4, 128], BF16)
        rr = _tile(nc, es, f"m_rr{tg}", [128, 4, 128], F32)
        ii = _tile(nc, es, f"m_ii{tg}", [128, 4, 128], F32)
        aa = _tile(nc, es, f"m_aa{tg}", [128, 4, 128], F32)
        mm = _tile(nc, es, f"m_mm{tg}", [128, 4, 128], F32)
        uu = _tile(nc, es, f"m_uu{tg}", [128, 4, 128], F32)
        hs = _tile(nc, es, f"m_hs{tg}", [128, 4, 128], F32)
        hp = _tile(nc, es, f"m_hp{tg}", [128, 4, 16], F32)
        g1 = _tile(nc, es, f"m_g1{tg}", [128, 4, 128], F32)
        g2 = _tile(nc, es, f"m_g2{tg}", [128, 4, 128], F32)
        yl = _tile(nc, es, f"m_yl{tg}", [128, DL], F32)
        cs = _tile(nc, es, f"m_cs{tg}", [128, 2, 16], F32)
        cqn = _tile(nc, es, f"m_cqn{tg}", [128, QL], BF16)
        cqT = _tile(nc, es, f"m_cqT{tg}", [128, 2, 128], BF16)
        qf = _tile(nc, es, f"m_qf{tg}", [128, 8, 96], F32)
        qb = _tile(nc, es, f"m_qb{tg}", [128, 8, 96], BF16)
        rt = _tile(nc, es, f"m_rt{tg}", [128, 4, 8, 16], F32)
        ckv = _tile(nc, es, f"m_ckv{tg}", [128, KVL], F32)
        ckb = _tile(nc, es, f"m_ckb{tg}", [128, KVL], BF16)
        krf = _tile(nc, es, f"m_krf{tg}", [128, RP], F32)
        krt = _tile(nc, es, f"m_krt{tg}", [128, 4, 16], F32)
        kq = _tile(nc, es, f"m_kq{tg}", [128, 96], BF16)
        p_xl = _psum(nc, es, f"m_pxl{tg}", [128, 4, 128], F32)
        p_gl = _psum(nc, es, f"m_pgl{tg}", [128, 4, 128], F32)
        p_ga = _psum(nc, es, f"m_pga{tg}", [128, 4, 128], F32)
        p_gi = _psum(nc, es, f"m_pgi{tg}", [128, 4, 128], F32)
        p_zc = _psum(nc, es, f"m_pzc{tg}", [128, 512], F32)
        p_q = _psum(nc, es, f"m_pq{tg}", [128, 2, 512], F32)
        p_tb = _psum(nc, es, f"m_ptb{tg}", [128, 8, 128], BF16)
        p_tf = p_xl
        chv, cch = W['chv'], W['cch']
        identb, identf = W['identb'], W['identf']
        k_ = lambda n: (n, tg)
        P.op('dve', lambda: nc.vector.memset(kq[:, :], 0.0), w=[k_('kq')])
        if not sample:
            P.op('dve', lambda: nc.vector.memset(xlx[:, :, :], 0.0), w=[k_('xlx')])
            P.op('dve', lambda: nc.vector.memset(hp[:, :, :], 0.0), w=[k_('hp')])
        else:
            cst = _tile(nc, es, "m_cst", [48, DL], F32)
            h0t = _tile(nc, es, "m_h0t", [16, DL], F32)
            P.dma('sp', lambda: nc.sync.dma_start(out=cst[0:NB * 3, :], in_=I['conv0'].rearrange("b k d -> (b k) d")), w=['cst'])
            P.dma('sp', lambda: nc.sync.dma_start(out=h0t[0:NB, :], in_=I['h0']), w=['h0t'])
            for j in range(4):
                P.op('pe', lambda: nc.tensor.transpose(out=p_xl[:, j, 0:NB * 3], in_=cst[0:NB * 3, j * 128:(j + 1) * 128],
                                                       identity=identf[0:NB * 3, 0:NB * 3]),
                     r=['cst', 'identf'], w=[k_('pxl')], inc=(j == 3))
            P.op('act', lambda: nc.scalar.copy(
                out=xlx[:, :, :].rearrange("p j (b t) -> p j b t", t=11)[:, :, 0:NB, 0:3],
                in_=p_xl[:, :, 0:NB * 3].rearrange("p j (b k) -> p j b k", k=3)), r=[k_('pxl')], w=[k_('xlx')])
            for j in range(4):
                P.op('pe', lambda: nc.tensor.transpose(out=p_gl[:, j, 0:NB], in_=h0t[0:NB, j * 128:(j + 1) * 128],
                                                       identity=identf[0:NB, 0:NB]),
                     r=['h0t', 'identf'], w=[k_('pgl')], inc=(j == 3))
            P.op('act', lambda: nc.scalar.copy(out=hp[:, :, 0:NB], in_=p_gl[:, :, 0:NB]), r=[k_('pgl')], w=[k_('hp')])
        for blk in range(nblk):
            r0 = SEQ if sample else blk * 128
            nt = NS if sample else 128
            P.dma('sp', lambda: nc.sync.dma_start(out=x1[0:nt, :], in_=S['X1'][r0:r0 + nt, :]),
                  r=[('X1', 's' if sample else blk)], w=[k_('x1')])
            P.dma('sp', lambda: nc.sync.dma_start(out=cs[0:nt, 0, :], in_=(I['coss'] if sample else I['cosp'][r0:r0 + nt, :])), w=[k_('cs')])
            P.dma('sp', lambda: nc.sync.dma_start(out=cs[0:nt, 1, :], in_=(I['sins'] if sample else I['sinp'][r0:r0 + nt, :])), w=[k_('cs')])
            P.op('act', lambda: nc.scalar.activation(out=junk[:, 0:D], in_=x1[:, :], func=AF.Square, accum_out=st[:, 0:1]),
                 r=[k_('x1')], w=[k_('junk'), k_('st0')])
            P.op('act', lambda: nc.scalar.activation(out=st[:, 0:1], in_=st[:, 0:1], func=AF.Sqrt, bias=EPS, scale=1.0 / D),
                 r=[k_('st0')], w=[k_('st0')])
            P.op('dve', lambda: nc.vector.reciprocal(out=st[:, 0:1], in_=st[:, 0:1]), r=[k_('st0')], w=[k_('st0')])
            P.op('dve', lambda: nc.vector.tensor_scalar(out=xn[:, :], in0=x1[:, :], scalar1=st[:, 0:1], scalar2=None, op0=ALU.mult),
                 r=[k_('x1'), k_('st0')], w=[k_('xn')])
            for k in range(8):
                P.op('pe', lambda: nc.tensor.transpose(out=p_tb[:, k, :], in_=xn[:, k * 128:(k + 1) * 128], identity=identb[:, :]),
                     r=[k_('xn'), 'identb'], w=[k_('ptb')], inc=(k == 7))
            P.op('act', lambda: nc.scalar.copy(out=xnT[:, :, :], in_=p_tb[:, :, :]), r=[k_('ptb')], w=[k_('xnT')])
            for (pp, c0, key) in ((p_xl, 0, 'pxl'), (p_gl, 512, 'pgl')):
                for j in range(4):
                    for k in range(8):
                        P.op('pe', lambda: nc.tensor.matmul(pp[:, j, :], lhsT=W['win'][:, k, c0 + j * 128:c0 + (j + 1) * 128],
                                                            rhs=xnT[:, k, :], start=(k == 0), stop=(k == 7)),
                             r=['Wsmall', k_('xnT')], w=[k_(key)], inc=(j == 3 and k == 7))
            for k in range(8):
                P.op('pe', lambda: nc.tensor.matmul(p_zc[:, 0:416], lhsT=xnT[:, k, :], rhs=W['win'][:, k, 1024:1440],
                                                    start=(k == 0), stop=(k == 7)),
                     r=['Wsmall', k_('xnT')], w=[k_('pzc')], inc=(k == 7))
            if sample:
                xv = xlx[:, :, :].rearrange("p j (b t) -> p j b t", t=11)
                P.op('act', lambda: nc.scalar.copy(out=xv[:, :, 0:NB, 3:11], in_=p_xl[:, :, :].rearrange("p j (b t) -> p j b t", t=8)),
                     r=[k_('pxl')], w=[k_('xlx')])
                sh = lambda j, k: xv[:, j, 0:NB, k:k + 8]
                ov = lambda t, j: t[:, j, :].rearrange("p (b t) -> p b t", t=8)
            else:
                P.op('act', lambda: nc.scalar.copy(out=xlx[:, :, 3:131], in_=p_xl[:, :, :]), r=[k_('pxl')], w=[k_('xlx')])
                sh = lambda j, k: xlx[:, j, k:k + 128]
                ov = lambda t, j: t[:, j, :]
            for j in range(4):
                P.op('dve', lambda: nc.vector.tensor_scalar(out=ov(xc, j), in0=sh(j, 0), scalar1=chv[:, j * 4:j * 4 + 1],
                                                            scalar2=chv[:, 16 + j:17 + j], op0=ALU.mult, op1=ALU.add),
                     r=[k_('xlx'), 'Wsmall'], w=[k_('xc')])
                for k in range(1, 4):
                    P.op('dve', lambda: nc.vector.scalar_tensor_tensor(out=ov(xc, j), in0=sh(j, k), scalar=chv[:, j * 4 + k:j * 4 + k + 1],
                                                                       in1=ov(xc, j), op0=ALU.mult, op1=ALU.add),
                         r=[k_('xlx'), 'Wsmall', k_('xc')], w=[k_('xc')])
            if not sample:
                P.op('dve', lambda: nc.vector.tensor_copy(out=xlx[:, :, 0:3], in_=xlx[:, :, 128:131]), r=[k_('xlx')], w=[k_('xlx')])
            P.op('dve', lambda: nc.vector.tensor_copy(out=xcb[:, :, :], in_=xc[:, :, :]), r=[k_('xc')], w=[k_('xcb')])
            for (pp, gi, key) in ((p_ga, 0, 'pga'), (p_gi, 1, 'pgi')):
                for j in range(4):
                    P.op('pe', lambda: nc.tensor.matmul(pp[:, j, :], lhsT=W['bd'][:, gi, j, :], rhs=xcb[:, j, :], start=True, stop=True),
                         r=['Wsmall', k_('xcb')], w=[k_(key)], inc=(j == 3))
            for j in range(4):
                P.op('act', lambda: nc.scalar.activation(out=rr[:, j, :], in_=p_ga[:, j, :], func=AF.Sigmoid, bias=chv[:, 20 + j:21 + j], scale=1.0),
                     r=[k_('pga'), 'Wsmall'], w=[k_('rr')])
                P.op('act', lambda: nc.scalar.activation(out=ii[:, j, :], in_=p_gi[:, j, :], func=AF.Sigmoid, bias=chv[:, 24 + j:25 + j], scale=1.0),
                     r=[k_('pgi'), 'Wsmall'], w=[k_('ii')])
            for j in range(4):
                P.op('act', lambda: nc.scalar.activation(out=aa[:, j, :], in_=rr[:, j, :], func=AF.Exp, scale=cch[:, j:j + 1]),
                     r=[k_('rr'), 'Wsmall'], w=[k_('aa')])
            P.op('act', lambda: nc.scalar.activation(out=mm[:, :, :], in_=aa[:, :, :], func=AF.Square), r=[k_('aa')], w=[k_('mm')])
            P.op('act', lambda: nc.scalar.activation(out=mm[:, :, :], in_=mm[:, :, :], func=AF.Sqrt, bias=1.0, scale=-1.0),
                 r=[k_('mm')], w=[k_('mm')])
            P.op('dve', lambda: nc.vector.tensor_tensor(out=uu[:, :, :], in0=ii[:, :, :], in1=xc[:, :, :], op=ALU.mult),
                 r=[k_('ii'), k_('xc')], w=[k_('uu')])
            P.op('dve', lambda: nc.vector.tensor_tensor(out=uu[:, :, :], in0=uu[:, :, :], in1=mm[:, :, :], op=ALU.mult),
                 r=[k_('uu'), k_('mm')], w=[k_('uu')])
            if sample:
                av = aa[:, :, :].rearrange("p j (b t) -> p j b t", t=8)
                uv = uu[:, :, :].rearrange("p j (b t) -> p j b t", t=8)
                hv = hs[:, :, :].rearrange("p j (b t) -> p j b t", t=8)
                for t in range(8):
                    prev = hp[:, :, 0:NB] if t == 0 else hv[:, :, 0:NB, t - 1]
                    P.op('dve', lambda: nc.vector.tensor_tensor(out=hv[:, :, 0:NB, t], in0=av[:, :, 0:NB, t], in1=prev, op=ALU.mult),
                         r=[k_('aa'), k_('hp'), k_('hs')], w=[k_('hs')])
                    P.op('dve', lambda: nc.vector.tensor_tensor(out=hv[:, :, 0:NB, t], in0=hv[:, :, 0:NB, t], in1=uv[:, :, 0:NB, t], op=ALU.add),
                         r=[k_('uu'), k_('hs')], w=[k_('hs')])
            else:
                for j in range(4):
                    P.op('dve', lambda: nc.vector.tensor_tensor_scan(out=hs[:, j, :], data0=aa[:, j, :], data1=uu[:, j, :],
                                                                    initial=hp[:, j, 0:1], op0=ALU.mult, op1=ALU.add),
                         r=[k_('aa'), k_('uu'), k_('hp')], w=[k_('hs')])
                P.op('dve', lambda: nc.vector.tensor_copy(out=hp[:, :, 0:1], in_=hs[:, :, 127:128]), r=[k_('hs')], w=[k_('hp')])
            P.op('act', lambda: nc.scalar.activation(out=g1[:, :, :], in_=p_gl[:, :, :], func=AF.Square), r=[k_('pgl')], w=[k_('g1')])
            P.op('dve', lambda: nc.vector.tensor_scalar(out=g1[:, :, :], in0=g1[:, :, :], scalar1=0.044715, scalar2=1.0, op0=ALU.mult, op1=ALU.add),
                 r=[k_('g1')], w=[k_('g1')])
            P.op('dve', lambda: nc.vector.tensor_tensor(out=g1[:, :, :], in0=g1[:, :, :], in1=p_gl[:, :, :], op=ALU.mult),
                 r=[k_('g1'), k_('pgl')], w=[k_('g1')])
            P.op('act', lambda: nc.scalar.activation(out=g2[:, :, :], in_=g1[:, :, :], func=AF.Sigmoid, scale=1.5957691216057308),
                 r=[k_('g1')], w=[k_('g2')])
            P.op('dve', lambda: nc.vector.tensor_tensor(out=g2[:, :, :], in0=g2[:, :, :], in1=p_gl[:, :, :], op=ALU.mult),
                 r=[k_('g2'), k_('pgl')], w=[k_('g2')])
            P.op('dve', lambda: nc.vector.tensor_tensor(out=g2[:, :, :], in0=g2[:, :, :], in1=hs[:, :, :], op=ALU.mult),
                 r=[k_('g2'), k_('hs')], w=[k_('g2')])
            for j in range(4):
                P.op('pe', lambda: nc.tensor.transpose(out=p_tf[:, j, :], in_=g2[:, j, :], identity=identf[:, :]),
                     r=[k_('g2'), 'identf'], w=[k_('pxl')], inc=(j == 3))
            P.op('act', lambda: nc.scalar.copy(out=yl[:, :], in_=p_tf[:, :, :].rearrange("p j t -> p (j t)")), r=[k_('pxl')], w=[k_('yl')])
            P.dma('pool', lambda: nc.gpsimd.dma_start(out=S['YL'][r0:r0 + nt, :], in_=yl[0:nt, :]), r=[k_('yl')], w=[('YL', 's' if sample else blk)])
            P.op('act', lambda: nc.scalar.activation(out=junk[:, 0:QL], in_=p_zc[:, 0:QL], func=AF.Square, accum_out=st[:, 1:2]),
                 r=[k_('pzc')], w=[k_('junk'), k_('st1')])
            P.op('act', lambda: nc.scalar.activation(out=junk[:, 0:KVL], in_=p_zc[:, QL:QL + KVL], func=AF.Square, accum_out=st[:, 2:3]),
                 r=[k_('pzc')], w=[k_('junk'), k_('st1')])
            P.op('act', lambda: nc.scalar.activation(out=st[:, 1:2], in_=st[:, 1:2], func=AF.Sqrt, bias=EPS, scale=1.0 / QL), r=[k_('st1')], w=[k_('st1')])
            P.op('act', lambda: nc.scalar.activation(out=st[:, 2:3], in_=st[:, 2:3], func=AF.Sqrt, bias=EPS, scale=1.0 / KVL), r=[k_('st1')], w=[k_('st1')])
            P.op('dve', lambda: nc.vector.reciprocal(out=st[:, 1:3], in_=st[:, 1:3]), r=[k_('st1')], w=[k_('st1')])
            P.op('dve', lambda: nc.vector.tensor_scalar(out=cqn[:, :], in0=p_zc[:, 0:QL], scalar1=st[:, 1:2], scalar2=None, op0=ALU.mult),
                 r=[k_('pzc'), k_('st1')], w=[k_('cqn')])
            for k in range(2):
                P.op('pe', lambda: nc.tensor.transpose(out=p_tb[:, k, :], in_=cqn[:, k * 128:(k + 1) * 128], identity=identb[:, :]),
                     r=[k_('cqn'), 'identb'], w=[k_('ptb')], inc=(k == 1))
            P.op('act', lambda: nc.scalar.copy(out=cqT[:, :, :], in_=p_tb[:, 0:2, :]), r=[k_('ptb')], w=[k_('cqT')])
            for (c0, c1, hb) in ((0, 512, 0), (512, 768, 1)):
                for k in range(2):
                    P.op('pe', lambda: nc.tensor.matmul(p_q[:, hb, 0:c1 - c0], lhsT=cqT[:, k, :], rhs=W['wq'][:, k, c0:c1], start=(k == 0), stop=(k == 1)),
                         r=['Wsmall', k_('cqT')], w=[k_('pq')], inc=(k == 1 and hb == 1))
            qfl = qf[:, :, :].rearrange("p h e -> p (h e)")
            P.op('act', lambda: nc.scalar.copy(out=qfl[:, 0:512], in_=p_q[:, 0, :]), r=[k_('pq')], w=[k_('qf')])
            P.op('act', lambda: nc.scalar.copy(out=qfl[:, 512:768], in_=p_q[:, 1, 0:256]), r=[k_('pq')], w=[k_('qf')])
            P.op('dve', lambda: nc.vector.tensor_copy(out=qb[:, :, 0:64], in_=qf[:, :, 0:64]), r=[k_('qf')], w=[k_('qb')])
            cosb = cs[:, 0, :].unsqueeze(1).to_broadcast([128, 8, 16])
            sinb = cs[:, 1, :].unsqueeze(1).to_broadcast([128, 8, 16])
            q1, q2 = qf[:, :, 64:80], qf[:, :, 80:96]
            for (ti, a_, b_) in ((0, q1, cosb), (1, q2, sinb), (2, q2, cosb), (3, q1, sinb)):
                P.op('dve', lambda: nc.vector.tensor_tensor(out=rt[:, ti, :, :], in0=a_, in1=b_, op=ALU.mult),
                     r=[k_('qf'), k_('cs')], w=[k_('rt')])
            P.op('dve', lambda: nc.vector.tensor_tensor(out=qb[:, :, 64:80], in0=rt[:, 0, :, :], in1=rt[:, 1, :, :], op=ALU.subtract),
                 r=[k_('rt')], w=[k_('qb')])
            P.op('dve', lambda: nc.vector.tensor_tensor(out=qb[:, :, 80:96], in0=rt[:, 2, :, :], in1=rt[:, 3, :, :], op=ALU.add),
                 r=[k_('rt')], w=[k_('qb')])
            P.dma('pool', lambda: nc.gpsimd.dma_start(out=S['QS'][r0:r0 + nt, :], in_=qb[0:nt, :, :].rearrange("p h e -> p (h e)")),
                  r=[k_('qb')], w=[('QS', 's' if sample else blk)])
            P.op('dve', lambda: nc.vector.scalar_tensor_tensor(out=ckv[:, :], in0=p_zc[:, QL:QL + KVL], scalar=st[:, 2:3], in1=W['kvg'][:, :],
                                                               op0=ALU.mult, op1=ALU.mult),
                 r=[k_('pzc'), k_('st1'), 'Wsmall'], w=[k_('ckv')])
            P.dma('pool', lambda: nc.gpsimd.dma_start(out=(O['kvs'] if sample else O['kvp'][r0:r0 + nt, :]), in_=ckv[0:nt, :]), r=[k_('ckv')], w=[('okv', tg, blk)])
            P.op('dve', lambda: nc.vector.tensor_copy(out=ckb[:, :], in_=ckv[:, :]), r=[k_('ckv')], w=[k_('ckb')])
            k1, k2 = p_zc[:, 384:400], p_zc[:, 400:416]
            for (ti, a_, b_) in ((0, k1, cs[:, 0, :]), (1, k2, cs[:, 1, :]), (2, k2, cs[:, 0, :]), (3, k1, cs[:, 1, :])):
                P.op('dve', lambda: nc.vector.tensor_tensor(out=krt[:, ti, :], in0=a_, in1=b_, op=ALU.mult),
                     r=[k_('pzc'), k_('cs')], w=[k_('krt')])
            P.op('dve', lambda: nc.vector.tensor_tensor(out=krf[:, 0:16], in0=krt[:, 0, :], in1=krt[:, 1, :], op=ALU.subtract), r=[k_('krt')], w=[k_('krf')])
            P.op('dve', lambda: nc.vector.tensor_tensor(out=krf[:, 16:32], in0=krt[:, 2, :], in1=krt[:, 3, :], op=ALU.add), r=[k_('krt')], w=[k_('krf')])
            P.dma('pool', lambda: nc.gpsimd.dma_start(out=(O['krs'] if sample else O['krp'][r0:r0 + nt, :]), in_=krf[0:nt, :]), r=[k_('krf')], w=[('okr', tg, blk)])
            P.op('dve', lambda: nc.vector.tensor_copy(out=kq[:, 64:96], in_=krf[:, :]), r=[k_('krf'), k_('kq')], w=[k_('kq')])
            P.op('pe', lambda: nc.tensor.transpose(out=p_tb[:, 0, :], in_=ckb[:, :], identity=identb[:, :]), r=[k_('ckb'), 'identb'], w=[k_('ptb')], inc=False)
            P.op('pe', lambda: nc.tensor.transpose(out=p_tb[0:96, 1, :], in_=kq[:, :], identity=identb[:, :]), r=[k_('kq'), 'identb'], w=[k_('ptb')])
            if sample:
                P.op('act', lambda: nc.scalar.copy(out=St['ckvT_s'][:, :], in_=p_tb[:, 0, :]), r=[k_('ptb')], w=['ckvT_s'])
                P.op('act', lambda: nc.scalar.copy(out=St['krT_s'][64:96, :], in_=p_tb[64:96, 1, :]), r=[k_('ptb')], w=['krT_s'])
                P.dma('pool', lambda: nc.gpsimd.dma_start(out=S['CKS'], in_=ckb[0:nt, :]), r=[k_('ckb')], w=['CKS'])
            else:
                P.op('act', lambda: nc.scalar.copy(out=St['ckvT'][:, blk * 128:(blk + 1) * 128], in_=p_tb[:, 0, :]), r=[k_('ptb')], w=[('ckvT', blk)])
                for kb in range(2):
                    P.op('act', lambda: nc.scalar.copy(out=St['KhT'][kb][64:96, blk * 128:(blk + 1) * 128], in_=p_tb[64:96, 1, :]),
                         r=[k_('ptb')], w=[('KhT', kb, blk)])
        if sample:
            for j in range(4):
                P.op('pe', lambda: nc.tensor.transpose(out=p_ga[0:NB, j, :], in_=hs[:, j, :].rearrange("p (b t) -> p b t", t=8)[:, 0:NB, 7],
                                                       identity=identf[:, :]), r=[k_('hs'), 'identf'], w=[k_('pga')], inc=(j == 3))
            P.op('act', lambda: nc.scalar.copy(out=yl[0:NB, :], in_=p_ga[0:NB, :, :].rearrange("p j t -> p (j t)")), r=[k_('pga')], w=[k_('yl')])
            P.dma('pool', lambda: nc.gpsimd.dma_start(out=O['hs_'], in_=yl[0:NB, :]), r=[k_('yl')], w=['o_hs'])
            for k in range(3):
                for j in range(4):
                    P.op('pe', lambda: nc.tensor.transpose(out=p_gi[0:NB, j, :], in_=xlx[:, j, :].rearrange("p (b t) -> p b t", t=11)[:, 0:NB, 8 + k],
                                                           identity=identf[:, :]), r=[k_('xlx'), 'identf'], w=[k_('pgi')], inc=(j == 3))
                P.op('act', lambda: nc.scalar.copy(out=junk[0:NB, k * DL:(k + 1) * DL], in_=p_gi[0:NB, :, :].rearrange("p j t -> p (j t)")),
                     r=[k_('pgi')], w=[k_('junk')])
            P.dma('pool', lambda: nc.gpsimd.dma_start(out=O['cvs'].rearrange("b k d -> b (k d)"), in_=junk[0:NB, 0:3 * DL]), r=[k_('junk')], w=['o_cvs'])
        else:
            for j in range(4):
                P.op('pe', lambda: nc.tensor.transpose(out=p_ga[0:1, j, :], in_=hs[:, j, 127:128], identity=identf[:, :]),
                     r=[k_('hs'), 'identf'], w=[k_('pga')], inc=(j == 3))
            P.op('act', lambda: nc.scalar.copy(out=yl[0:1, :], in_=p_ga[0:1, :, :].rearrange("p j t -> p (j t)")), r=[k_('pga')], w=[k_('yl')])
            P.dma('pool', lambda: nc.gpsimd.dma_start(out=O['hp'], in_=yl[0:1, :]), r=[k_('yl')], w=['o_hp'])
            for j in range(4):
                P.op('pe', lambda: nc.tensor.transpose(out=p_gi[0:3, j, :], in_=xlx[:, j, 0:3], identity=identf[:, :]),
                     r=[k_('xlx'), 'identf'], w=[k_('pgi')], inc=(j == 3))
            P.op('act', lambda: nc.scalar.copy(out=junk[0:3, 0:DL], in_=p_gi[0:3, :, :].rearrange("p j t -> p (j t)")), r=[k_('pgi')], w=[k_('junk')])
            P.dma('pool', lambda: nc.gpsimd.dma_start(out=O['cvp'], in_=junk[0:3, 0:DL]), r=[k_('junk')], w=['o_cvp'])
    P.barrier()


def emit_outproj(P, nc, es, W, tg, yl_f, ym_f, x1g, x2_dst, x2_keys, r_keys):
    k_ = lambda n: (n, 'op' + tg)
    T = W['op_tiles']
    junk, st, yc, ycT, p_tb, p_o = T['junk'], T['st'], T['yc'], T['ycT'], T['p_tb'], T['p_o']
    for (i, src) in ((0, yl_f), (1, ym_f)):
        P.op('act', lambda: nc.scalar.activation(out=junk[:, :], in_=src, func=AF.Square, accum_out=st[:, i:i + 1]),
             r=r_keys, w=[k_('junk'), k_('st')])
    P.op('act', lambda: nc.scalar.activation(out=st[:, 0:2], in_=st[:, 0:2], func=AF.Sqrt, bias=EPS, scale=1.0 / DL), r=[k_('st')], w=[k_('st')])
    P.op('dve', lambda: nc.vector.reciprocal(out=st[:, 0:2], in_=st[:, 0:2]), r=[k_('st')], w=[k_('st')])
    for (i, src) in ((0, yl_f), (1, ym_f)):
        P.op('dve', lambda: nc.vector.tensor_scalar(out=yc[:, i * DL:(i + 1) * DL], in0=src, scalar1=st[:, i:i + 1], scalar2=None, op0=ALU.mult),
             r=r_keys + [k_('st')], w=[k_('yc')])
    for k in range(8):
        P.op('pe', lambda: nc.tensor.transpose(out=p_tb[:, k, :], in_=yc[:, k * 128:(k + 1) * 128], identity=W['identb'][:, :]),
             r=[k_('yc'), 'identb'], w=[k_('ptb')], inc=(k == 7))
    P.op('act', lambda: nc.scalar.copy(out=ycT[:, :, :], in_=p_tb[:, :, :]), r=[k_('ptb')], w=[k_('ycT')])
    for half in range(2):
        for k in range(8):
            P.op('pe', lambda: nc.tensor.matmul(p_o[:, half, :], lhsT=ycT[:, k, :], rhs=W['wout'][:, k, half * 512:(half + 1) * 512],
                                                start=(k == 0), stop=(k == 7)),
                 r=[k_('ycT'), 'Wsmall'], w=[k_('po')], inc=(half == 1 and k == 7))
    P.op('dve', lambda: nc.vector.tensor_tensor(out=x1g, in0=x1g, in1=p_o[:, :, :].rearrange("p a b -> p (a b)"), op=ALU.add),
         r=r_keys + [k_('po')], w=r_keys[-1:])
    P.dma('pool', lambda: nc.gpsimd.dma_start(out=x2_dst, in_=x1g), r=r_keys[-1:], w=x2_keys)


def emit_prompt_attn(P, nc, W, St, I, S, cfg):
    SEQ = cfg['SEQ']
    NBLK = SEQ // 128
    NSLOT = NBLK // 4
    scale = 96.0 ** -0.5
    with ExitStack() as es:
        oidx = _tile(nc, es, "a_oidx", [128, NSLOT], I32)
        mk = _tile(nc, es, "a_mk", [128, 4, 128], BF16)
        mkf = _tile(nc, es, "a_mkf", [128, 4, 128], F32)
        qg = _tile(nc, es, "a_qg", [128, NSLOT, 768], BF16)
        ym = _tile(nc, es, "a_ym", [128, NSLOT, DL], BF16)
        Vh = [_tile(nc, es, f"a_Vh{i}", [128, NBLK, 65], BF16) for i in range(2)]
        qT = [_tile(nc, es, f"a_qT{i}", [96, 128], BF16) for i in range(2)]
        pT = [_tile(nc, es, f"a_pT{i}", [128, 4, 128], BF16) for i in range(2)]
        rc = _tile(nc, es, "a_rc", [128, 2], F32)
        ylg = _tile(nc, es, "a_ylg", [128, DL], F32)
        x1g = _tile(nc, es, "a_x1g", [128, D], F32)
        T = dict(junk=_tile(nc, es, "a_junk", [128, DL], F32), st=_tile(nc, es, "a_st", [128, 2], F32),
                 yc=_tile(nc, es, "a_yc", [128, D], BF16), ycT=_tile(nc, es, "a_ycT", [128, 8, 128], BF16),
                 p_tb=_psum(nc, es, "a_ptb", [128, 8, 128], BF16), p_o=_psum(nc, es, "a_po", [128, 2, 512], F32))
        W['op_tiles'] = T
        p_s = [_psum(nc, es, f"a_ps{i}", [128, 4, 128], F32) for i in range(2)]
        p_k = _psum(nc, es, "a_pk", [128, 512], F32)
        p_ov = [_psum(nc, es, f"a_pov{i}", [128, 512], F32) for i in range(2)]
        P.dma('sp', lambda: nc.sync.dma_start(out=oidx[:, :], in_=I['own_idx']), w=['oidx'])
        P.dma('sp', lambda: nc.sync.dma_start(out=mkf[:, :, :], in_=I['masks']), w=['mkf'])
        P.op('dve', lambda: nc.vector.tensor_copy(out=mk[:, :, :], in_=mkf[:, :, :]), r=['mkf'], w=['mk'])
        oix = [_tile(nc, es, f"a_oix{i}", [128, 1], I32) for i in range(NSLOT)]
        for s in range(NSLOT):
            P.op('dve', lambda: nc.vector.tensor_copy(out=oix[s][:, :], in_=oidx[:, s:s + 1]), r=['oidx'], w=[('oix', s)])
            P.dma('pool', lambda: nc.gpsimd.indirect_dma_start(
                out=qg[:, s, :], out_offset=None, in_=S['QS'], in_offset=bass.IndirectOffsetOnAxis(ap=oix[s][:, 0:1], axis=0)),
                r=[('oix', s)] + [('QS', b) for b in range(NBLK)], w=[('qg', s)])
        for i in range(2):
            P.op('dve', lambda: nc.vector.memset(Vh[i][:, :, 64:65], 1.0), w=[('Vh', i)])
        cnt = 0
        for h in range(NH):
            hb = h % 2
            for c in range(0, SEQ, 512):
                wdt = min(512, SEQ - c)
                P.op('pe', lambda: nc.tensor.matmul(p_k[0:64, 0:wdt], lhsT=W['wuk'][:, h, :], rhs=St['ckvT'][:, c:c + wdt], start=True, stop=True),
                     r=['Wsmall'] + [('ckvT', b) for b in range(c // 128, (c + wdt) // 128)], w=['pk'])
                P.op('act', lambda: nc.scalar.copy(out=St['KhT'][hb][0:64, c:c + wdt], in_=p_k[0:64, 0:wdt]), r=['pk'],
                     w=[('KhT', hb, b) for b in range(c // 128, (c + wdt) // 128)])
            for b0 in range(0, NBLK, 8):
                nb = min(8, NBLK - b0)
                pv = p_ov[0]
                for i in range(nb):
                    P.op('pe', lambda: nc.tensor.matmul(pv[:, i * 64:(i + 1) * 64], lhsT=St['ckvT'][:, (b0 + i) * 128:(b0 + i + 1) * 128],
                                                        rhs=W['wuv'][:, h, :], start=True, stop=True),
                         r=['Wsmall', ('ckvT', b0 + i)], w=['pov0'], inc=(i == nb - 1))
                P.op('dve', lambda: nc.vector.tensor_copy(out=Vh[hb][:, b0:b0 + nb, 0:64], in_=pv[:, 0:nb * 64].rearrange("p (b v) -> p b v", v=64)),
                     r=['pov0'], w=[('Vh', hb)])
            for s in range(NSLOT):
                qb_ = cnt % 2
                cnt += 1
                P.op('pe', lambda: nc.tensor.transpose(out=T['p_tb'][0:96, 0, :], in_=qg[:, s, h * 96:(h + 1) * 96], identity=W['identb'][:, :]),
                     r=[('qg', s), 'identb'], w=[('ptb', 'op')])
                P.op('act', lambda: nc.scalar.copy(out=qT[qb_][:, :], in_=T['p_tb'][0:96, 0, :]), r=[('ptb', 'op')], w=[('qT', qb_)])
                nkb = 4 * s + 4
                po = p_ov[1]
                for g in range(nkb // 4):
                    sb = (cnt + g) % 2
                    for i in range(4):
                        kb = 4 * g + i
                        P.op('pe', lambda: nc.tensor.matmul(p_s[sb][:, i, :], lhsT=St['KhT'][hb][0:96, kb * 128:(kb + 1) * 128], rhs=qT[qb_][:, :],
                                                            start=True, stop=True),
                             r=[('KhT', hb, kb), ('qT', qb_)], w=[('ps', sb)], inc=(i == 3))
                    P.op('act', lambda: nc.scalar.activation(out=pT[sb][:, :, :], in_=p_s[sb][:, :, :], func=AF.Exp, scale=scale),
                         r=[('ps', sb)], w=[('pT', sb)])
                    if g == nkb // 4 - 1:
                        P.op('dve', lambda: nc.vector.tensor_tensor(out=pT[sb][:, :, :], in0=pT[sb][:, :, :], in1=mk[:, :, :], op=ALU.mult),
                             r=[('pT', sb), 'mk'], w=[('pT', sb)])
                    for i in range(4):
                        kb = 4 * g + i
                        P.op('pe', lambda: nc.tensor.matmul(po[:, 0:65], lhsT=pT[sb][:, i, :], rhs=Vh[hb][:, kb, :], start=(kb == 0), stop=(kb == nkb - 1)),
                             r=[('pT', sb), ('Vh', hb)], w=['pov1'], inc=(kb == nkb - 1))
                P.op('dve', lambda: nc.vector.reciprocal(out=rc[:, 0:1], in_=po[:, 64:65]), r=['pov1'], w=['rc'])
                P.op('dve', lambda: nc.vector.tensor_scalar(out=ym[:, s, h * 64:(h + 1) * 64], in0=po[:, 0:64], scalar1=rc[:, 0:1], scalar2=None, op0=ALU.mult),
                     r=['pov1', 'rc'], w=[('ym', s)])
        for s in range(NSLOT):
            P.dma('pool', lambda: nc.gpsimd.indirect_dma_start(
                out=ylg[:, :], out_offset=None, in_=S['YL'], in_offset=bass.IndirectOffsetOnAxis(ap=oix[s][:, 0:1], axis=0)),
                r=[('oix', s)] + [('YL', b) for b in range(NBLK)], w=['ylg'])
            P.dma('pool', lambda: nc.gpsimd.indirect_dma_start(
                out=x1g[:, :], out_offset=None, in_=S['X1'], in_offset=bass.IndirectOffsetOnAxis(ap=oix[s][:, 0:1], axis=0)),
                r=[('oix', s)] + [('X1', b) for b in range(NBLK)], w=['x1g'])
            emit_outproj(P, nc, es, W, 'p', ylg[:, :], ym[:, s, :], x1g[:, :], S['X2'][s * 128:(s + 1) * 128, :], [('X2', s)],
                         [('ym', s), 'ylg', 'x1g'])
    P.barrier()


def emit_sample_attn(P, nc, W, St, I, S, cfg, O=None):
    SEQ, NB, NPG = cfg['SEQ'], cfg['NB'], cfg['NPG']
    NS = NB * 8
    NG = NPG // 8
    scale = 96.0 ** -0.5
    with ExitStack() as es:
        qs = _tile(nc, es, "s_qs", [128, 768], BF16)
        qn = _tile(nc, es, "s_qn", [64, 8, 128], BF16)
        QLT = _tile(nc, es, "s_QLT", [128, NB, 8, 8], BF16)
        QRT = _tile(nc, es, "s_QRT", [96, NB, 8, 8], BF16)
        ptr = _tile(nc, es, "s_ptr", [128, NB, NG], I32)
        ptf = _tile(nc, es, "s_ptf", [128, NB, NG], F32)
        qcol = _tile(nc, es, "s_qcol", [128, 1], F32)
        idx = _tile(nc, es, "s_idx", [128, NB, NG], I32)
        G = [_tile(nc, es, f"s_G{i}", [128, 8, 129], BF16) for i in range(2)]
        GR = [_tile(nc, es, f"s_GR{i}", [128, 8, 96], BF16) for i in range(2)]
        Gf = [_tile(nc, es, f"s_Gf{i}", [128, 8, 128], F32) for i in range(3)]
        six = [_tile(nc, es, f"s_six{i}", [128, 1], I32) for i in range(3)]
        GRf = [_tile(nc, es, f"s_GRf{i}", [128, 8, 32], F32) for i in range(3)]
        cT = [_tile(nc, es, f"s_cT{i}", [128, 8, 128], BF16) for i in range(2)]
        rT = [_tile(nc, es, f"s_rT{i}", [96, 8, 128], BF16) for i in range(2)]
        pT = [_tile(nc, es, f"s_pT{i}", [128, 8, 64], BF16) for i in range(2)]
        pTo = _tile(nc, es, "s_pTo", [8, 64], BF16)
        cks = _tile(nc, es, "s_cks", [8, NB, 129], BF16)
        mo = _tile(nc, es, "s_mo", [8, 64], F32)
        on = _tile(nc, es, "s_on", [64, 128], BF16)
        onT = _tile(nc, es, "s_onT", [128, 64], BF16)
        rc = _tile(nc, es, "s_rc", [64, 1], F32)
        ymb = _tile(nc, es, "s_ymb", [8, DL], F32)
        ymg = _tile(nc, es, "s_ymg", [128, DL], F32)
        ylg = _tile(nc, es, "s_ylg", [128, DL], F32)
        x1g = _tile(nc, es, "s_x1g", [128, D], F32)
        T = dict(junk=_tile(nc, es, "s_junk", [128, DL], F32), st=_tile(nc, es, "s_st", [128, 2], F32),
                 yc=_tile(nc, es, "s_yc", [128, D], BF16), ycT=_tile(nc, es, "s_ycT", [128, 8, 128], BF16),
                 p_tb=_psum(nc, es, "s_ptb", [128, 8, 128], BF16), p_o=_psum(nc, es, "s_po", [128, 2, 512], F32))
        W['op_tiles'] = T
        p_tb = T['p_tb']
        p_tr = _psum(nc, es, "s_ptr2", [128, 8, 128], BF16)
        p_s = [_psum(nc, es, f"s_ps{i}", [128, 8, 64], F32) for i in range(2)]
        p_ov = _psum(nc, es, "s_pov", [128, 512], F32)
        p_m = _psum(nc, es, "s_pm", [128, 512], F32)
        identb = W['identb']
        for pl in range(8):
            P.dma('sp', lambda: nc.sync.dma_start(
                out=ptr[pl * 16:(pl + 1) * 16, :, :],
                in_=I['pt_l'][pl:pl + 1, :, :].to_broadcast([16, NB, NG])), w=['s_ptr'])
        P.dma('sp', lambda: nc.sync.dma_start(out=qcol[:, :], in_=I['qcol']), w=['s_qcol'])
        P.dma('sp', lambda: nc.sync.dma_start(out=mo[:, :], in_=I['mask_own']), w=['s_mo'])
        P.op('dve', lambda: nc.vector.tensor_copy(out=ptf[:, :, :], in_=ptr[:, :, :]), r=['s_ptr'], w=['s_ptf'])
        P.op('dve', lambda: nc.vector.tensor_scalar(out=ptf[:, :, :], in0=ptf[:, :, :], scalar1=16.0, scalar2=qcol[:, 0:1], op0=ALU.mult, op1=ALU.add),
             r=['s_ptf', 's_qcol'], w=['s_ptf'])
        P.op('dve', lambda: nc.vector.tensor_copy(out=idx[:, :, :], in_=ptf[:, :, :]), r=['s_ptf'], w=['s_idx'])
        dbg = cfg.get('debug')
        if dbg:
            P.dma('sp', lambda: nc.sync.dma_start(out=O['dbg_idx'], in_=idx[:, :, :].rearrange("p b g -> p (b g)")), r=['s_idx'], w=['dbg_idx'])
            dps = _tile(nc, es, "s_dps", [128, 512], F32)
            dov = _tile(nc, es, "s_dov", [64, 129], F32)
        P.dma('sp', lambda: nc.sync.dma_start(out=qs[:, :], in_=S['QS'][SEQ:SEQ + NS, :]), r=[('QS', 's')], w=['s_qs'])
        P.dma('sp', lambda: nc.sync.dma_start(out=cks[:, :, 0:128], in_=S['CKS'].rearrange("(b t) c -> t b c", t=8)), r=['CKS'], w=['s_cks'])
        P.op('dve', lambda: nc.vector.memset(cks[:, :, 128:129], 1.0), w=['s_cks'])
        for i in range(2):
            P.op('dve', lambda: nc.vector.memset(G[i][:, :, 128:129], 1.0), w=[('s_G', i)])
            P.op('dve', lambda: nc.vector.memset(GR[i][:, :, :], 0.0), w=[('s_GR', i)])
        for h in range(8):
            P.op('pe', lambda: nc.tensor.transpose(out=p_tb[0:96, h, :], in_=qs[:, h * 96:(h + 1) * 96], identity=identb[:, :]),
                 r=['s_qs', 'identb'], w=[('ptb', 'ops')], inc=(h == 7))
        P.op('act', lambda: nc.scalar.copy(out=qn[:, :, :], in_=p_tb[0:64, :, :]), r=[('ptb', 'ops')], w=['s_qn'])
        P.op('act', lambda: nc.scalar.copy(out=QRT[64:96, :, :, :].rearrange("p b h t -> p h b t"),
                                           in_=p_tb[64:96, :, :].rearrange("p h (b t) -> p h b t", t=8)), r=[('ptb', 'ops')], w=['s_QRT'])
        for h in range(8):
            P.op('pe', lambda: nc.tensor.matmul(p_m[:, 0:128], lhsT=W['wukT'][:, h, :], rhs=qn[:, h, :], start=True, stop=True),
                 r=['Wsmall', 's_qn'], w=['s_pm'])
            P.op('act', lambda: nc.scalar.copy(out=QLT[:, :, h, :], in_=p_m[:, 0:128].rearrange("p (b t) -> p b t", t=8)), r=['s_pm'], w=['s_QLT'])
        state = {'first': True}

        def S0(b, g, fb):
            P.op('dve', lambda: nc.vector.tensor_copy(out=six[fb][:, :], in_=idx[:, b, g:g + 1]), r=['s_idx'], w=[('s_six', fb)])
            yield
            P.dma('pool', lambda: nc.gpsimd.indirect_dma_start(
                out=Gf[fb][:, :, :].rearrange("p a c -> p (a c)"), out_offset=None, in_=I['ckv_cache'].rearrange("(r g) c -> r (g c)", g=8),
                in_offset=bass.IndirectOffsetOnAxis(ap=six[fb][:, 0:1], axis=0)), r=[('s_six', fb)], w=[('s_Gf', fb)])
            yield
            P.dma('pool', lambda: nc.gpsimd.indirect_dma_start(
                out=GRf[fb][:, :, :].rearrange("p a c -> p (a c)"), out_offset=None, in_=I['ckr_cache'].rearrange("(r g) c -> r (g c)", g=8),
                in_offset=bass.IndirectOffsetOnAxis(ap=six[fb][:, 0:1], axis=0)), r=[('s_six', fb)], w=[('s_GRf', fb)])
            yield

        def S1(b, g, gb, fb):
            dbg = cfg.get('debug')
            if dbg and b == 0 and g == 0:
                P.dma('sp', lambda: nc.sync.dma_start(out=O['dbg_G'], in_=Gf[fb][:, :, :].rearrange("p a c -> p (a c)")), r=[('s_Gf', fb)], w=['dbg_G'])
                yield
                P.dma('sp', lambda: nc.sync.dma_start(out=O['dbg_GR'], in_=GRf[fb][:, :, :].rearrange("p a c -> p (a c)")), r=[('s_GRf', fb)], w=['dbg_GR'])
                yield
            P.op('dve', lambda: nc.vector.tensor_copy(out=G[gb][:, :, 0:128], in_=Gf[fb][:, :, :]), r=[('s_Gf', fb)], w=[('s_G', gb)])
            yield
            P.op('dve', lambda: nc.vector.tensor_copy(out=GR[gb][:, :, 64:96], in_=GRf[fb][:, :, :]), r=[('s_GRf', fb)], w=[('s_GR', gb)])
            yield
            for i in range(8):
                P.op('pe', lambda: nc.tensor.transpose(out=p_tb[:, i, :], in_=G[gb][:, i, 0:128], identity=identb[:, :]),
                     r=[('s_G', gb), 'identb'], w=[('ptb', 'ops')], inc=(i == 7))
                yield
            P.op('act', lambda: nc.scalar.copy(out=cT[gb][:, :, :], in_=p_tb[:, :, :]), r=[('ptb', 'ops')], w=[('s_cT', gb)])
            yield
            for i in range(8):
                P.op('pe', lambda: nc.tensor.transpose(out=p_tr[0:96, i, :], in_=GR[gb][:, i, :], identity=identb[:, :]),
                     r=[('s_GR', gb), 'identb'], w=['s_ptr2'], inc=(i == 7))
                yield
            P.op('dve', lambda: nc.vector.tensor_copy(out=rT[gb][64:96, :, :], in_=p_tr[64:96, :, :]), r=['s_ptr2'], w=[('s_rT', gb)])
            yield

        def S2(b, g, gb, qlt, qrt):
            dbg = cfg.get('debug')
            for i in range(8):
                P.op('pe', lambda: nc.tensor.matmul(p_s[gb][:, i, :], lhsT=cT[gb][:, i, :], rhs=qlt, start=True, stop=False),
                     r=[('s_cT', gb), 's_QLT'], w=[('s_ps', gb)], inc=False)
                yield
                P.op('pe', lambda: nc.tensor.matmul(p_s[gb][:, i, :], lhsT=rT[gb][64:96, i, :], rhs=qrt, start=False, stop=True),
                     r=[('s_rT', gb), 's_QRT'], w=[('s_ps', gb)], inc=(i == 7))
                yield
            if dbg and b == 0 and g == 0:
                P.op('dve', lambda: nc.vector.tensor_copy(out=dps[:, :], in_=p_s[gb][:, :, :].rearrange("p a c -> p (a c)")), r=[('s_ps', gb)], w=['s_dps'])
                yield
                P.dma('sp', lambda: nc.sync.dma_start(out=O['dbg_ps'], in_=dps[:, :]), r=['s_dps'], w=['dbg_ps'])
                yield
            P.op('act', lambda: nc.scalar.activation(out=pT[gb][:, :, :], in_=p_s[gb][:, :, :], func=AF.Exp, scale=scale),
                 r=[('s_ps', gb)], w=[('s_pT', gb)])
            yield
            for i in range(8):
                P.op('pe', lambda: nc.tensor.matmul(p_ov[0:64, 0:129], lhsT=pT[gb][:, i, :], rhs=G[gb][:, i, :], start=state['first'], stop=False),
                     r=[('s_pT', gb), ('s_G', gb)], w=['s_pov'], inc=(i == 7))
                state['first'] = False
                yield

        def run2(gens):
            alive = list(gens)
            while alive:
                for g_ in list(alive):
                    try:
                        next(g_)
                    except StopIteration:
                        alive.remove(g_)

        groups = [(b, g) for b in range(NB) for g in range(NG)]
        run2([S0(groups[0][0], groups[0][1], 0)])
        run2([S0(groups[1][0], groups[1][1], 1), S1(groups[0][0], groups[0][1], 0, 0)])
        for gi_, (b, g) in enumerate(groups):
            gb = gi_ % 2
            qlt = QLT[:, b, :, :].rearrange("p h t -> p (h t)")
            qrt = QRT[64:96, b, :, :].rearrange("p h t -> p (h t)")
            if g == 0:
                state['first'] = True
            gens = [S2(b, g, gb, qlt, qrt)]
            if gi_ + 1 < len(groups):
                nb_, ng_ = groups[gi_ + 1]
                gens.insert(0, S1(nb_, ng_, (gi_ + 1) % 2, (gi_ + 1) % 3))
            if gi_ + 2 < len(groups):
                nb_, ng_ = groups[gi_ + 2]
                gens.insert(0, S0(nb_, ng_, (gi_ + 2) % 3))
            run2(gens)
            if g != NG - 1:
                continue
            P.op('pe', lambda: nc.tensor.matmul(p_m[0:8, 0:64], lhsT=St['ckvT_s'][:, b * 8:(b + 1) * 8], rhs=qlt, start=True, stop=False),
                 r=['ckvT_s', 's_QLT'], w=['s_pm'], inc=False)
            P.op('pe', lambda: nc.tensor.matmul(p_m[0:8, 0:64], lhsT=St['krT_s'][64:96, b * 8:(b + 1) * 8], rhs=qrt, start=False, stop=True),
                 r=['krT_s', 's_QRT'], w=['s_pm'])
            P.op('act', lambda: nc.scalar.activation(out=pTo[:, :], in_=p_m[0:8, 0:64], func=AF.Exp, scale=scale), r=['s_pm'], w=['s_pTo'])
            P.op('dve', lambda: nc.vector.tensor_tensor(out=pTo[:, :], in0=pTo[:, :], in1=mo[:, :], op=ALU.mult), r=['s_pTo', 's_mo'], w=['s_pTo'])
            P.op('pe', lambda: nc.tensor.matmul(p_ov[0:64, 0:129], lhsT=pTo[:, :], rhs=cks[:, b, :], start=False, stop=True),
                 r=['s_pTo', 's_cks'], w=['s_pov'])
            if dbg and b == 0:
                P.op('dve', lambda: nc.vector.tensor_copy(out=dov[:, :], in_=p_ov[0:64, 0:129]), r=['s_pov'], w=['s_dov'])
                P.dma('sp', lambda: nc.sync.dma_start(out=O['dbg_ov'], in_=dov[:, :]), r=['s_dov'], w=['dbg_ov'])
            P.op('dve', lambda: nc.vector.reciprocal(out=rc[:, :], in_=p_ov[0:64, 128:129]), r=['s_pov'], w=['s_rc'])
            P.op('dve', lambda: nc.vector.tensor_scalar(out=on[:, :], in0=p_ov[0:64, 0:128], scalar1=rc[:, 0:1], scalar2=None, op0=ALU.mult),
                 r=['s_pov', 's_rc'], w=['s_on'])
            P.op('pe', lambda: nc.tensor.transpose(out=p_tr[:, 0, 0:64], in_=on[:, :], identity=identb[0:64, 0:64]), r=['s_on', 'identb'], w=['s_ptr2'])
            P.op('act', lambda: nc.scalar.copy(out=onT[:, :], in_=p_tr[:, 0, 0:64]), r=['s_ptr2'], w=['s_onT'])
            for h in range(8):
                P.op('pe', lambda: nc.tensor.matmul(p_o_s(p_m, h), lhsT=onT[:, h * 8:(h + 1) * 8], rhs=W['wuv'][:, h, :], start=True, stop=True),
                     r=['s_onT', 'Wsmall'], w=['s_pm'], inc=(h == 7))
            P.op('act', lambda: nc.scalar.copy(out=ymb[:, :], in_=p_m[0:8, 0:512]), r=['s_pm'], w=['s_ymb'])
            P.dma('pool', lambda: nc.gpsimd.dma_start(out=S['YM'][b * 8:(b + 1) * 8, :], in_=ymb[:, :]), r=['s_ymb'], w=['YM'])
        P.dma('sp', lambda: nc.sync.dma_start(out=ymg[:, :], in_=S['YM']), r=['YM'], w=['s_ymg'])
        if dbg:
            P.dma('sp', lambda: nc.sync.dma_start(out=O['dbg_YM'], in_=ymg[:, :]), r=['s_ymg'], w=['dbg_YM'])
        P.dma('sp', lambda: nc.sync.dma_start(out=ylg[:, :], in_=S['YL'][SEQ:SEQ + NS, :]), r=[('YL', 's')], w=['s_ylg'])
        P.dma('sp', lambda: nc.sync.dma_start(out=x1g[:, :], in_=S['X1'][SEQ:SEQ + NS, :]), r=[('X1', 's')], w=['s_x1g'])
        NOWN = SEQ // 4
        emit_outproj(P, nc, es, W, 's', ylg[:, :], ymg[:, :], x1g[:, :], S['X2'][NOWN:NOWN + NS, :], [('X2', 's')], ['s_ymg', 's_ylg', 's_x1g'])
    P.barrier()


def p_o_s(p_m, h):
    return p_m[0:8, h * 64:(h + 1) * 64]


def emit_mixer_prompt(P, nc, W, St, I, O, S, cfg):
    SEQ = cfg['SEQ']
    nblk = SEQ // 128
    tg = 'p'
    with ExitStack() as es:
        def two(name, shape, dt):
            return [_tile(nc, es, f"mp_{name}{i}", shape, dt) for i in range(2)]
        x1 = two('x1', [128, D], F32)
        junk = _tile(nc, es, "mp_junk", [128, D], F32)
        xn = two('xn', [128, D], BF16)
        xnT = two('xnT', [128, 8, 128], BF16)
        st = two('st', [128, 4], F32)
        xlx = two('xlx', [128, 4, 131], F32)
        gl = two('gl', [128, 4, 128], F32)
        zc = two('zc', [128, 416], F32)
        xc = two('xc', [128, 4, 128], F32)
        xcb = two('xcb', [128, 4, 128], BF16)
        rr = two('rr', [128, 4, 128], F32)
        ii = two('ii', [128, 4, 128], F32)
        aa = two('aa', [128, 4, 128], F32)
        uu = two('uu', [128, 4, 128], F32)
        hs = two('hs', [128, 4, 128], F32)
        hp = _tile(nc, es, "mp_hp", [128, 4, 1], F32)
        halo = _tile(nc, es, "mp_halo", [128, 4, 3], F32)
        g1 = two('g1', [128, 4, 128], F32)
        yl = two('yl', [128, DL], F32)
        cs = two('cs', [128, 2, 16], F32)
        cqn = two('cqn', [128, QL], BF16)
        cqT = two('cqT', [128, 2, 128], BF16)
        qf = two('qf', [128, 8, 96], F32)
        qb = two('qb', [128, 8, 96], BF16)
        rt = two('rt', [128, 4, 8, 16], F32)
        ckv = two('ckv', [128, KVL], F32)
        ckb = two('ckb', [128, KVL], BF16)
        krf = two('krf', [128, RP], F32)
        krt = two('krt', [128, 4, 16], F32)
        kq = two('kq', [128, 96], BF16)
        p_xl = _psum(nc, es, "mp_pxl", [128, 4, 128], F32)
        p_gl = _psum(nc, es, "mp_pgl", [128, 4, 128], F32)
        p_ga = _psum(nc, es, "mp_pga", [128, 4, 128], F32)
        p_gi = _psum(nc, es, "mp_pgi", [128, 4, 128], F32)
        p_zc = _psum(nc, es, "mp_pzc", [128, 512], F32)
        p_q = _psum(nc, es, "mp_pq", [128, 2, 512], F32)
        p_tb = _psum(nc, es, "mp_ptb", [128, 8, 128], BF16)
        chv, cch = W['chv'], W['cch']
        identb, identf = W['identb'], W['identf']
        for i in range(2):
            P.op('dve', lambda: nc.vector.memset(kq[i][:, :], 0.0), w=[('kq', i)])
            P.op('dve', lambda: nc.vector.memset(xlx[i][:, :, :], 0.0), w=[('xlx', i)])
        P.op('dve', lambda: nc.vector.memset(hp[:, :, :], 0.0), w=['hp'])

        def stageA(blk):
            b = blk % 2
            k_ = lambda n: (n, b)
            r0 = blk * 128
            P.dma('sp', lambda: nc.sync.dma_start(out=x1[b][:, :], in_=S['X1'][r0:r0 + 128, :]), r=[('X1', blk)], w=[k_('x1')])
            yield
            P.dma('sp', lambda: nc.sync.dma_start(out=cs[b][:, 0, :], in_=I['cosp'][r0:r0 + 128, :]), w=[k_('cs')])
            yield
            P.dma('sp', lambda: nc.sync.dma_start(out=cs[b][:, 1, :], in_=I['sinp'][r0:r0 + 128, :]), w=[k_('cs')])
            yield
            P.op('act', lambda: nc.scalar.activation(out=junk[:, :], in_=x1[b][:, :], func=AF.Square, accum_out=st[b][:, 0:1]),
                 r=[k_('x1')], w=['junk', k_('st0')])
            yield
            P.op('act', lambda: nc.scalar.activation(out=st[b][:, 0:1], in_=st[b][:, 0:1], func=AF.Sqrt, bias=EPS, scale=1.0 / D),
                 r=[k_('st0')], w=[k_('st0')])
            yield
            P.op('dve', lambda: nc.vector.reciprocal(out=st[b][:, 0:1], in_=st[b][:, 0:1]), r=[k_('st0')], w=[k_('st0')])
            yield
            P.op('dve', lambda: nc.vector.tensor_scalar(out=xn[b][:, :], in0=x1[b][:, :], scalar1=st[b][:, 0:1], scalar2=None, op0=ALU.mult),
                 r=[k_('x1'), k_('st0')], w=[k_('xn')])
            yield
            for k in range(8):
                P.op('pe', lambda: nc.tensor.transpose(out=p_tb[:, k, :], in_=xn[b][:, k * 128:(k + 1) * 128], identity=identb[:, :]),
                     r=[k_('xn'), 'identb'], w=['ptb'], inc=(k == 7))
            P.op('act', lambda: nc.scalar.copy(out=xnT[b][:, :, :], in_=p_tb[:, :, :]), r=['ptb'], w=[k_('xnT')])
            yield
            for (pp, c0, key) in ((p_xl, 0, 'pxl'), (p_gl, 512, 'pgl')):
                for j in range(4):
                    for k in range(8):
                        P.op('pe', lambda: nc.tensor.matmul(pp[:, j, :], lhsT=W['win'][:, k, c0 + j * 128:c0 + (j + 1) * 128],
                                                            rhs=xnT[b][:, k, :], start=(k == 0), stop=(k == 7)),
                             r=['Wsmall', k_('xnT')], w=[key], inc=(j == 3 and k == 7))
                        yield
            for k in range(8):
                P.op('pe', lambda: nc.tensor.matmul(p_zc[:, 0:416], lhsT=xnT[b][:, k, :], rhs=W['win'][:, k, 1024:1440],
                                                    start=(k == 0), stop=(k == 7)),
                     r=['Wsmall', k_('xnT')], w=['pzc'], inc=(k == 7))
                yield
            P.op('act', lambda: nc.scalar.copy(out=xlx[b][:, :, 3:131], in_=p_xl[:, :, :]), r=['pxl'], w=[k_('xlx')])
            yield
            P.op('dve', lambda: nc.vector.tensor_copy(out=gl[b][:, :, :], in_=p_gl[:, :, :]), r=['pgl'], w=[k_('gl')])
            yield
            P.op('act', lambda: nc.scalar.copy(out=zc[b][:, :], in_=p_zc[:, 0:416]), r=['pzc'], w=[k_('zc')])
            yield

        def stageB1(blk):
            b = blk % 2
            k_ = lambda n: (n, b)
            r0 = blk * 128
            if blk > 0:
                P.op('dve', lambda: nc.vector.tensor_copy(out=xlx[b][:, :, 0:3], in_=halo[:, :, :]),
                     r=['halo', k_('xlx')], w=[k_('xlx')])
                yield
            P.op('dve', lambda: nc.vector.tensor_copy(out=halo[:, :, :], in_=xlx[b][:, :, 128:131]), r=[k_('xlx')], w=['halo'])
            yield
            for j in range(4):
                P.op('dve', lambda: nc.vector.tensor_scalar(out=xc[b][:, j, :], in0=xlx[b][:, j, 0:128], scalar1=chv[:, j * 4:j * 4 + 1],
                                                            scalar2=chv[:, 16 + j:17 + j], op0=ALU.mult, op1=ALU.add),
                     r=[k_('xlx'), 'Wsmall'], w=[k_('xc')])
                yield
                for k in range(1, 4):
                    P.op('dve', lambda: nc.vector.scalar_tensor_tensor(out=xc[b][:, j, :], in0=xlx[b][:, j, k:k + 128], scalar=chv[:, j * 4 + k:j * 4 + k + 1],
                                                                       in1=xc[b][:, j, :], op0=ALU.mult, op1=ALU.add),
                         r=[k_('xlx'), 'Wsmall', k_('xc')], w=[k_('xc')])
                    yield
            P.op('dve', lambda: nc.vector.tensor_copy(out=xcb[b][:, :, :], in_=xc[b][:, :, :]), r=[k_('xc')], w=[k_('xcb')])
            yield
            for (pp, gi, key) in ((p_ga, 0, 'pga'), (p_gi, 1, 'pgi')):
                for j in range(4):
                    P.op('pe', lambda: nc.tensor.matmul(pp[:, j, :], lhsT=W['bd'][:, gi, j, :], rhs=xcb[b][:, j, :], start=True, stop=True),
                         r=['Wsmall', k_('xcb')], w=[key], inc=(j == 3))
                    yield
            for j in range(4):
                P.op('act', lambda: nc.scalar.activation(out=rr[b][:, j, :], in_=p_ga[:, j, :], func=AF.Sigmoid, bias=chv[:, 20 + j:21 + j], scale=1.0),
                     r=['pga', 'Wsmall'], w=[k_('rr')])
                yield
                P.op('act', lambda: nc.scalar.activation(out=ii[b][:, j, :], in_=p_gi[:, j, :], func=AF.Sigmoid, bias=chv[:, 24 + j:25 + j], scale=1.0),
                     r=['pgi', 'Wsmall'], w=[k_('ii')])
                yield
            for j in range(4):
                P.op('act', lambda: nc.scalar.activation(out=aa[b][:, j, :], in_=rr[b][:, j, :], func=AF.Exp, scale=cch[:, j:j + 1]),
                     r=[k_('rr'), 'Wsmall'], w=[k_('aa')])
                yield
            P.op('act', lambda: nc.scalar.activation(out=rr[b][:, :, :], in_=aa[b][:, :, :], func=AF.Square), r=[k_('aa')], w=[k_('rr')])
            yield
            P.op('act', lambda: nc.scalar.activation(out=rr[b][:, :, :], in_=rr[b][:, :, :], func=AF.Sqrt, bias=1.0, scale=-1.0),
                 r=[k_('rr')], w=[k_('rr')])
            yield
            P.op('dve', lambda: nc.vector.tensor_tensor(out=uu[b][:, :, :], in0=ii[b][:, :, :], in1=xc[b][:, :, :], op=ALU.mult),
                 r=[k_('ii'), k_('xc')], w=[k_('uu')])
            yield
            P.op('dve', lambda: nc.vector.tensor_tensor(out=uu[b][:, :, :], in0=uu[b][:, :, :], in1=rr[b][:, :, :], op=ALU.mult),
                 r=[k_('uu'), k_('rr')], w=[k_('uu')])
            yield
            for j in range(4):
                P.op('dve', lambda: nc.vector.tensor_tensor_scan(out=hs[b][:, j, :], data0=aa[b][:, j, :], data1=uu[b][:, j, :],
                                                                initial=hp[:, j, 0:1], op0=ALU.mult, op1=ALU.add),
                     r=[k_('aa'), k_('uu'), 'hp'], w=[k_('hs')])
                yield
            P.op('dve', lambda: nc.vector.tensor_copy(out=hp[:, :, 0:1], in_=hs[b][:, :, 127:128]), r=[k_('hs')], w=['hp'])
            yield
            P.op('act', lambda: nc.scalar.activation(out=g1[b][:, :, :], in_=gl[b][:, :, :], func=AF.Square), r=[k_('gl')], w=[k_('g1')])
            yield
            P.op('dve', lambda: nc.vector.tensor_scalar(out=g1[b][:, :, :], in0=g1[b][:, :, :], scalar1=0.044715, scalar2=1.0, op0=ALU.mult, op1=ALU.add),
                 r=[k_('g1')], w=[k_('g1')])
            yield
            P.op('dve', lambda: nc.vector.tensor_tensor(out=g1[b][:, :, :], in0=g1[b][:, :, :], in1=gl[b][:, :, :], op=ALU.mult),
                 r=[k_('g1'), k_('gl')], w=[k_('g1')])
            yield
            P.op('act', lambda: nc.scalar.activation(out=g1[b][:, :, :], in_=g1[b][:, :, :], func=AF.Sigmoid, scale=1.5957691216057308),
                 r=[k_('g1')], w=[k_('g1')])
            yield
            P.op('dve', lambda: nc.vector.tensor_tensor(out=g1[b][:, :, :], in0=g1[b][:, :, :], in1=gl[b][:, :, :], op=ALU.mult),
                 r=[k_('g1'), k_('gl')], w=[k_('g1')])
            yield
            P.op('dve', lambda: nc.vector.tensor_tensor(out=g1[b][:, :, :], in0=g1[b][:, :, :], in1=hs[b][:, :, :], op=ALU.mult),
                 r=[k_('g1'), k_('hs')], w=[k_('g1')])
            yield
            for j in range(4):
                P.op('pe', lambda: nc.tensor.transpose(out=p_ga[:, j, :], in_=g1[b][:, j, :], identity=identf[:, :]),
                     r=[k_('g1'), 'identf'], w=['pga'], inc=(j == 3))
                yield
            P.op('act', lambda: nc.scalar.copy(out=yl[b][:, :], in_=p_ga[:, :, :].rearrange("p j t -> p (j t)")), r=['pga'], w=[k_('yl')])
            yield
            P.dma('pool', lambda: nc.gpsimd.dma_start(out=S['YL'][r0:r0 + 128, :], in_=yl[b][:, :]), r=[k_('yl')], w=[('YL', blk)])
            yield

        def stageB2(blk):
            b = blk % 2
            k_ = lambda n: (n, b)
            r0 = blk * 128
            P.op('act', lambda: nc.scalar.activation(out=junk[:, 0:QL], in_=zc[b][:, 0:QL], func=AF.Square, accum_out=st[b][:, 1:2]),
                 r=[k_('zc')], w=['junk', k_('st1')])
            yield
            P.op('act', lambda: nc.scalar.activation(out=junk[:, 0:KVL], in_=zc[b][:, QL:QL + KVL], func=AF.Square, accum_out=st[b][:, 2:3]),
                 r=[k_('zc')], w=['junk', k_('st1')])
            yield
            P.op('act', lambda: nc.scalar.activation(out=st[b][:, 1:2], in_=st[b][:, 1:2], func=AF.Sqrt, bias=EPS, scale=1.0 / QL), r=[k_('st1')], w=[k_('st1')])
            yield
            P.op('act', lambda: nc.scalar.activation(out=st[b][:, 2:3], in_=st[b][:, 2:3], func=AF.Sqrt, bias=EPS, scale=1.0 / KVL), r=[k_('st1')], w=[k_('st1')])
            yield
            P.op('dve', lambda: nc.vector.reciprocal(out=st[b][:, 1:3], in_=st[b][:, 1:3]), r=[k_('st1')], w=[k_('st1')])
            yield
            P.op('dve', lambda: nc.vector.tensor_scalar(out=cqn[b][:, :], in0=zc[b][:, 0:QL], scalar1=st[b][:, 1:2], scalar2=None, op0=ALU.mult),
                 r=[k_('zc'), k_('st1')], w=[k_('cqn')])
            yield
            for k in range(2):
                P.op('pe', lambda: nc.tensor.transpose(out=p_tb[:, k, :], in_=cqn[b][:, k * 128:(k + 1) * 128], identity=identb[:, :]),
                     r=[k_('cqn'), 'identb'], w=['ptb'], inc=(k == 1))
            P.op('act', lambda: nc.scalar.copy(out=cqT[b][:, :, :], in_=p_tb[:, 0:2, :]), r=['ptb'], w=[k_('cqT')])
            yield
            for (c0, c1, hb) in ((0, 512, 0), (512, 768, 1)):
                for k in range(2):
                    P.op('pe', lambda: nc.tensor.matmul(p_q[:, hb, 0:c1 - c0], lhsT=cqT[b][:, k, :], rhs=W['wq'][:, k, c0:c1], start=(k == 0), stop=(k == 1)),
                         r=['Wsmall', k_('cqT')], w=['pq'], inc=(k == 1 and hb == 1))
                    yield
            qfl = qf[b][:, :, :].rearrange("p h e -> p (h e)")
            P.op('act', lambda: nc.scalar.copy(out=qfl[:, 0:512], in_=p_q[:, 0, :]), r=['pq'], w=[k_('qf')])
            yield
            P.op('act', lambda: nc.scalar.copy(out=qfl[:, 512:768], in_=p_q[:, 1, 0:256]), r=['pq'], w=[k_('qf')])
            yield
            P.op('dve', lambda: nc.vector.tensor_copy(out=qb[b][:, :, 0:64], in_=qf[b][:, :, 0:64]), r=[k_('qf')], w=[k_('qb')])
            yield
            cosb = cs[b][:, 0, :].unsqueeze(1).to_broadcast([128, 8, 16])
            sinb = cs[b][:, 1, :].unsqueeze(1).to_broadcast([128, 8, 16])
            q1, q2 = qf[b][:, :, 64:80], qf[b][:, :, 80:96]
            for (ti, a_, b_) in ((0, q1, cosb), (1, q2, sinb), (2, q2, cosb), (3, q1, sinb)):
                P.op('dve', lambda: nc.vector.tensor_tensor(out=rt[b][:, ti, :, :], in0=a_, in1=b_, op=ALU.mult),
                     r=[k_('qf'), k_('cs')], w=[k_('rt')])
                yield
            P.op('dve', lambda: nc.vector.tensor_tensor(out=qb[b][:, :, 64:80], in0=rt[b][:, 0, :, :], in1=rt[b][:, 1, :, :], op=ALU.subtract),
                 r=[k_('rt')], w=[k_('qb')])
            yield
            P.op('dve', lambda: nc.vector.tensor_tensor(out=qb[b][:, :, 80:96], in0=rt[b][:, 2, :, :], in1=rt[b][:, 3, :, :], op=ALU.add),
                 r=[k_('rt')], w=[k_('qb')])
            yield
            P.dma('pool', lambda: nc.gpsimd.dma_start(out=S['QS'][r0:r0 + 128, :], in_=qb[b][:, :, :].rearrange("p h e -> p (h e)")),
                  r=[k_('qb')], w=[('QS', blk)])
            yield
            P.op('dve', lambda: nc.vector.scalar_tensor_tensor(out=ckv[b][:, :], in0=zc[b][:, QL:QL + KVL], scalar=st[b][:, 2:3], in1=W['kvg'][:, :],
                                                               op0=ALU.mult, op1=ALU.mult),
                 r=[k_('zc'), k_('st1'), 'Wsmall'], w=[k_('ckv')])
            yield
            P.dma('pool', lambda: nc.gpsimd.dma_start(out=O['kvp'][r0:r0 + 128, :], in_=ckv[b][:, :]), r=[k_('ckv')], w=[('okv', tg, blk)])
            yield
            P.op('dve', lambda: nc.vector.tensor_copy(out=ckb[b][:, :], in_=ckv[b][:, :]), r=[k_('ckv')], w=[k_('ckb')])
            yield
            k1, k2 = zc[b][:, 384:400], zc[b][:, 400:416]
            for (ti, a_, b_) in ((0, k1, cs[b][:, 0, :]), (1, k2, cs[b][:, 1, :]), (2, k2, cs[b][:, 0, :]), (3, k1, cs[b][:, 1, :])):
                P.op('dve', lambda: nc.vector.tensor_tensor(out=krt[b][:, ti, :], in0=a_, in1=b_, op=ALU.mult),
                     r=[k_('zc'), k_('cs')], w=[k_('krt')])
                yield
            P.op('dve', lambda: nc.vector.tensor_tensor(out=krf[b][:, 0:16], in0=krt[b][:, 0, :], in1=krt[b][:, 1, :], op=ALU.subtract), r=[k_('krt')], w=[k_('krf')])
            yield
            P.op('dve', lambda: nc.vector.tensor_tensor(out=krf[b][:, 16:32], in0=krt[b][:, 2, :], in1=krt[b][:, 3, :], op=ALU.add), r=[k_('krt')], w=[k_('krf')])
            yield
            P.dma('pool', lambda: nc.gpsimd.dma_start(out=O['krp'][r0:r0 + 128, :], in_=krf[b][:, :]), r=[k_('krf')], w=[('okr', tg, blk)])
            yield
            P.op('dve', lambda: nc.vector.tensor_copy(out=kq[b][:, 64:96], in_=krf[b][:, :]), r=[k_('krf'), k_('kq')], w=[k_('kq')])
            yield
            P.op('pe', lambda: nc.tensor.transpose(out=p_tb[:, 0, :], in_=ckb[b][:, :], identity=identb[:, :]), r=[k_('ckb'), 'identb'], w=['ptb'], inc=False)
            P.op('pe', lambda: nc.tensor.transpose(out=p_tb[0:96, 1, :], in_=kq[b][:, :], identity=identb[:, :]), r=[k_('kq'), 'identb'], w=['ptb'])
            P.op('act', lambda: nc.scalar.copy(out=St['ckvT'][:, blk * 128:(blk + 1) * 128], in_=p_tb[:, 0, :]), r=['ptb'], w=[('ckvT', blk)])
            for kb in range(2):
                P.op('act', lambda: nc.scalar.copy(out=St['KhT'][kb][64:96, blk * 128:(blk + 1) * 128], in_=p_tb[64:96, 1, :]),
                     r=['ptb'], w=[('KhT', kb, blk)])
                yield


        def run_interleaved(gens):
            alive = list(gens)
            while alive:
                for (g, steps) in list(alive):
                    try:
                        for _ in range(steps):
                            next(g)
                    except StopIteration:
                        alive.remove((g, steps))

        run_interleaved([(stageA(0), 1)])
        for blk in range(nblk):
            gens = [(stageB1(blk), 1), (stageB2(blk), 1)]
            if blk + 1 < nblk:
                gens.insert(0, (stageA(blk + 1), 2))
            run_interleaved(gens)
        lb = (nblk - 1) % 2
        for j in range(4):
            P.op('pe', lambda: nc.tensor.transpose(out=p_ga[0:1, j, :], in_=hs[lb][:, j, 127:128], identity=identf[:, :]),
                 r=[('hs', lb), 'identf'], w=['pga'], inc=(j == 3))
        P.op('act', lambda: nc.scalar.copy(out=yl[0][0:1, :], in_=p_ga[0:1, :, :].rearrange("p j t -> p (j t)")), r=['pga'], w=[('yl', 0)])
        P.dma('pool', lambda: nc.gpsimd.dma_start(out=O['hp'], in_=yl[0][0:1, :]), r=[('yl', 0)], w=['o_hp'])
        for j in range(4):
            P.op('pe', lambda: nc.tensor.transpose(out=p_gi[0:3, j, :], in_=xlx[lb][:, j, 128:131], identity=identf[:, :]),
                 r=[('xlx', lb), 'identf'], w=['pgi'], inc=(j == 3))
        P.op('act', lambda: nc.scalar.copy(out=junk[0:3, 0:DL], in_=p_gi[0:3, :, :].rearrange("p j t -> p (j t)")), r=['pgi'], w=['junk'])
        P.dma('pool', lambda: nc.gpsimd.dma_start(out=O['cvp'], in_=junk[0:3, 0:DL]), r=['junk'], w=['o_cvp'])
    P.barrier()


def _bf16():
    import ml_dtypes
    return ml_dtypes.bfloat16


def make_in_maps(inp, cfg):
    SEQ, NB, NPG, NPHYS = cfg['SEQ'], cfg['NB'], cfg['NPG'], cfg['NPHYS']
    NBLK = SEQ // 128
    NSLOT = NBLK // 4
    NS = NB * 8
    f32 = np.float32
    a = {k: np.asarray(v) for k, v in inp.items()}
    past_len = NPG * 128

    def cols(g, n):
        return np.ascontiguousarray(np.asarray(g, f32).reshape(n, 128).T)
    half = 16
    freqs = (10000.0 ** (-np.arange(half, dtype=f32) / f32(half))).astype(f32)
    pos_p = np.arange(SEQ, dtype=f32)
    pos_s = (past_len + np.arange(8)).astype(f32)
    angp = (pos_p[:, None] * freqs[None, :]).astype(f32)
    angs = np.tile((pos_s[:, None] * freqs[None, :]).astype(f32), (NB, 1))
    common = {
        'final_norm': a['final_norm'].astype(f32),
        'identb': np.eye(128, dtype=f32).astype(_bf16()), 'identf': np.eye(128, dtype=f32),
        'w_in': a['w_in'][0], 'mix_gc': cols(a['mix_norm'][0], 8),
        'w_q_up': a['w_q_up'][0], 'q_gc': cols(a['q_norm'][0], 2),
        'w_out': a['w_out'][0], 'out_gc': cols(np.concatenate([a['out_norm_lru'][0], a['out_norm_mla'][0]]), 8),
        'w_uk': a['w_uk'][0], 'w_uv': a['w_uv'][0], 'lru_w_a': a['lru_w_a'][0], 'lru_w_i': a['lru_w_i'][0],
        'kv_norm': a['kv_norm'][0],
        'cosp': np.cos(angp).astype(f32), 'sinp': np.sin(angp).astype(f32),
        'coss': np.cos(angs).astype(f32), 'sins': np.sin(angs).astype(f32),
        'qcol': (np.arange(128) % 16).astype(f32).reshape(128, 1),
        'ckv_cache': a['cache_kv_latent'][0].reshape(NPHYS * 128, KVL),
        'ckr_cache': a['cache_k_rope'][0].reshape(NPHYS * 128, RP),
    }
    for n in ('ffn1', 'ffn2'):
        common[n + '_wg'] = a[n + '_w_gate'][0]
        common[n + '_wu'] = a[n + '_w_up'][0]
        common[n + '_wd'] = a[n + '_w_down'][0]
        common[n + '_gc'] = cols(a[n + '_norm'][0], 8)
    chv = np.zeros((128, 32), f32)
    cw = a['conv_w'][0]
    for j in range(4):
        for k in range(4):
            chv[:, j * 4 + k] = cw[k, j * 128:(j + 1) * 128]
    chv[:, 16:20] = cols(a['conv_b'][0], 4)
    chv[:, 20:24] = cols(a['lru_b_a'][0], 4)
    chv[:, 24:28] = cols(a['lru_b_i'][0], 4)
    chv[:, 28:32] = cols(a['lru_lambda'][0], 4)
    common['chv'] = chv
    mo = np.zeros((8, 8, 8), f32)
    for kp in range(8):
        mo[kp, :, kp:] = 1.0
    common['mask_own'] = mo.reshape(8, 64)
    tri = (np.arange(128)[:, None] <= np.arange(128)[None, :]).astype(f32)
    maps = []
    for c in range(8):
        seq, j = c // 4, c % 4
        m = dict(common)
        m['xp'] = a['x_prompt'][seq]
        m['xs'] = a['x_sample'][c * NB:(c + 1) * NB].reshape(NS, D)
        m['conv0'] = a['state_conv'][0, c * NB:(c + 1) * NB]
        m['h0'] = a['state_lru_h'][0, c * NB:(c + 1) * NB]
        own = np.zeros((128, NSLOT), np.int32)
        for s in range(NSLOT):
            own[:, s] = (4 * s + j) * 128 + np.arange(128)
        m['own_idx'] = own
        mk = np.zeros((128, 4, 128), f32)
        for i in range(4):
            if i < j:
                mk[:, i, :] = 1.0
            elif i == j:
                mk[:, i, :] = tri
        m['masks'] = mk
        pt = a['page_table'][c * NB:(c + 1) * NB].astype(np.int32)
        m['pt_l'] = np.ascontiguousarray(pt.reshape(NB, NPG // 8, 8).transpose(2, 0, 1))
        maps.append({k: np.ascontiguousarray(v) for k, v in m.items()})
    return maps


def assemble(results, cfg, B=2):
    SEQ, NB = cfg['SEQ'], cfg['NB']
    NBLK = SEQ // 128
    NSLOT = NBLK // 4
    f32 = np.float32
    yp = np.zeros((B, SEQ, D), f32)
    ys = np.zeros((8 * NB, 8, D), f32)
    kvp = np.zeros((1, B, SEQ, KVL), f32); krp = np.zeros((1, B, SEQ, RP), f32)
    hp = np.zeros((1, B, DL), f32); cvp = np.zeros((1, B, 3, DL), f32)
    kvs = np.zeros((1, 8 * NB, 8, KVL), f32); krs = np.zeros((1, 8 * NB, 8, RP), f32)
    hs = np.zeros((1, 8 * NB, DL), f32); cvs = np.zeros((1, 8 * NB, 3, DL), f32)
    for c in range(8):
        r = results[c]
        seq, j = c // 4, c % 4
        yo = np.asarray(r['y_own'])
        for s in range(NSLOT):
            b = 4 * s + j
            yp[seq, b * 128:(b + 1) * 128] = yo[s * 128:(s + 1) * 128]
        if j == 0:
            kvp[0, seq] = r['kvp']; krp[0, seq] = r['krp']; hp[0, seq] = np.asarray(r['hp'])[0]; cvp[0, seq] = r['cvp']
        sl = slice(c * NB, (c + 1) * NB)
        ys[sl] = np.asarray(r['ys']).reshape(NB, 8, D)
        kvs[0, sl] = np.asarray(r['kvs']).reshape(NB, 8, KVL); krs[0, sl] = np.asarray(r['krs']).reshape(NB, 8, RP)
        hs[0, sl] = r['hs_']; cvs[0, sl] = r['cvs']
    return (yp, ys, kvp, krp, hp, cvp, kvs, krs, hs, cvs)


_NC_CACHE = {}


def kernel(**inputs):
    SEQ = int(inputs['x_prompt'].shape[1])
    NB = int(inputs['x_sample'].shape[0]) // 8
    NPG = int(inputs['page_table'].shape[1])
    NPHYS = int(inputs['cache_kv_latent'].shape[1])
    cfg = dict(SEQ=SEQ, NB=NB, NPG=NPG, NPHYS=NPHYS)
    key = tuple(sorted(cfg.items()))
    if key not in _NC_CACHE:
        _NC_CACHE[key] = build_program(cfg)
    nc = _NC_CACHE[key]
    maps = make_in_maps(inputs, cfg)
    res = run_bass_kernel_spmd(nc, maps, core_ids=list(range(8)))
    return assemble(res.results, cfg)
```
